# Optimizing a Trainium2 kernel written in Bass

```python
import jax
import jax.numpy as jnp
from jax import lax
import numpy as np

D_MODEL = 1024
BATCH = 8
SEQ = 2048
DEPTH = 2
DEC_BATCH = 32
DEC_SEQ = 4
PAST_LEN = 16384
PAGE_SIZE = 128

N_A_LAYERS = DEPTH // 2
N_B_LAYERS = DEPTH - N_A_LAYERS
GLA_HEADS = 4
GLA_QK = D_MODEL // 2
GLA_V = D_MODEL
GLA_DK = GLA_QK // GLA_HEADS
GLA_DV = GLA_V // GLA_HEADS
GLA_RANK = 16
GLA_TAU = 16.0
GLA_CHUNK = 64
GLA_IN = 2 * GLA_QK + GLA_V + GLA_RANK + GLA_V
GLA_NORM_EPS = 1e-6
DIL_HEADS = 16
DIL_HD = D_MODEL // DIL_HEADS
DIL_WINDOWS = (128, 512, 2048)
DIL_DILATIONS = (1, 4, 16)
N_GROUPS = len(DIL_WINDOWS)
MAX_WINDOW = max(DIL_WINDOWS)
DIL_SCALE = DIL_HD ** -0.5
ROPE_THETA = 10000.0
D_FF = 2816
FFN_RES = 0.5
ALPHA = (2 * DEPTH) ** 0.25
BETA = (8 * DEPTH) ** -0.25
LN_EPS = 1e-5
N_MOD = 9

kernel_name = 'yoco_gla_dilated_macaron_step'


def layer_norm(x, g, b):
    xf = x.astype(jnp.float32)
    mu = jnp.mean(xf, axis=-1, keepdims=True)
    var = jnp.mean(jnp.square(xf - mu), axis=-1, keepdims=True)
    return ((xf - mu) * lax.rsqrt(var + LN_EPS) * g + b).astype(x.dtype)


def modulate(x, shift, scale):
    return x * (1.0 + scale[:, None, :]) + shift[:, None, :]


def post_norm(x, y, gate, g, b):
    return layer_norm(ALPHA * x + (1.0 + gate[:, None, :]) * y, g, b)


def swiglu(h, w_up, w_down):
    a, u = jnp.split(h @ w_up, 2, axis=-1)
    return (jax.nn.silu(a) * u) @ w_down


def rope(x, positions):
    half = x.shape[-1] // 2
    inv = ROPE_THETA ** (-jnp.arange(half, dtype=jnp.float32) / half)
    ang = positions.astype(jnp.float32)[:, None] * inv[None, :]
    cos, sin = jnp.cos(ang)[None, :, None, :], jnp.sin(ang)[None, :, None, :]
    xf = x.astype(jnp.float32)
    x1, x2 = xf[..., :half], xf[..., half:]
    return jnp.concatenate([x1 * cos - x2 * sin, x2 * cos + x1 * sin], axis=-1).astype(x.dtype)


def masked_softmax(s, mask):
    s = jnp.where(mask, s, -jnp.inf)
    m = jnp.max(s, axis=-1, keepdims=True)
    e = jnp.exp(s - m)
    den = jnp.sum(e, axis=-1, keepdims=True)
    return e / den, (m + jnp.log(den))[..., 0]


def gla_chunk_step(s0, chunk):
    q, k, v, la = chunk
    C = q.shape[1]
    b = jnp.cumsum(la, axis=1)
    o_inter = jnp.einsum('bchk,bhkv->bchv', q * jnp.exp(b), s0)
    causal = jnp.asarray(np.tril(np.ones((C, C), dtype=bool)))
    diff = b[:, :, None] - b[:, None, :]
    decay = jnp.exp(jnp.where(causal[None, :, :, None, None], diff, -jnp.inf))
    scores = jnp.sum(q[:, :, None] * k[:, None, :] * decay, axis=-1)
    o_intra = jnp.einsum('bijh,bjhv->bihv', scores, v)
    b_last = b[:, -1]
    s_new = jnp.exp(b_last)[..., None] * s0 + jnp.einsum(
        'bjhk,bjhv->bhkv', k * jnp.exp(b_last[:, None] - b), v)
    return s_new, o_inter + o_intra


def gla_recurrence(q, k, v, la, s0):
    B, T = q.shape[:2]
    C = GLA_CHUNK if T % GLA_CHUNK == 0 else T
    n = T // C

    def to_chunks(a):
        return jnp.moveaxis(a.reshape(B, n, C, *a.shape[2:]), 1, 0)

    s_fin, o = lax.scan(gla_chunk_step, s0, (to_chunks(q), to_chunks(k), to_chunks(v), to_chunks(la)))
    return jnp.moveaxis(o, 0, 1).reshape(B, T, GLA_HEADS, GLA_DV), s_fin


def gla_mixer(h, s0, w_in, w_gate2, b_gate, g_onorm, w_out):
    B, T, _ = h.shape
    proj = h @ w_in
    q, k, v, g_lr, r = jnp.split(
        proj, [GLA_QK, 2 * GLA_QK, 2 * GLA_QK + GLA_V, 2 * GLA_QK + GLA_V + GLA_RANK], axis=-1)
    f32 = jnp.float32
    q = q.reshape(B, T, GLA_HEADS, GLA_DK).astype(f32) * (GLA_DK ** -0.5)
    k = k.reshape(B, T, GLA_HEADS, GLA_DK).astype(f32)
    v = v.reshape(B, T, GLA_HEADS, GLA_DV).astype(f32)
    la = (jax.nn.log_sigmoid((g_lr @ w_gate2 + b_gate).astype(f32)) / GLA_TAU).reshape(B, T, GLA_HEADS, GLA_DK)
    o, s_fin = gla_recurrence(q, k, v, la, s0.astype(f32))
    o = o * lax.rsqrt(jnp.mean(jnp.square(o), axis=-1, keepdims=True) + GLA_NORM_EPS) * g_onorm
    o = o.astype(h.dtype).reshape(B, T, GLA_V) * jax.nn.silu(r)
    return o @ w_out, s_fin.astype(s0.dtype)


def dilated_group_prompt(q, k, v, dil, steps):
    B, S, H, E = q.shape
    L = S // dil
    nb = -(-L // steps)
    Lp = nb * steps

    def by_residue(a):
        return a.reshape(B, L, dil, H, E).transpose(0, 2, 1, 3, 4)

    qb = jnp.pad(by_residue(q), ((0, 0), (0, 0), (0, Lp - L), (0, 0), (0, 0))).reshape(B, dil, nb, steps, H, E)

    def key_blocks(a):
        ap = jnp.pad(by_residue(a), ((0, 0), (0, 0), (steps, Lp - L), (0, 0), (0, 0)))
        return jnp.concatenate([ap[:, :, :Lp].reshape(B, dil, nb, steps, H, E),
                                ap[:, :, steps:].reshape(B, dil, nb, steps, H, E)], axis=3)

    kb, vb = key_blocks(k), key_blocks(v)
    s = jnp.einsum('bdiqhe,bdikhe->bdihqk', qb, kb, preferred_element_type=jnp.float32) * DIL_SCALE
    qi = np.arange(steps)[:, None]
    ki = np.arange(2 * steps)[None, :]
    dist = qi - ki + steps
    band = (dist >= 0) & (dist <= steps)
    mask = band[None] & ((ki >= steps)[None] | (np.arange(nb)[:, None, None] > 0))
    p, lse = masked_softmax(s, jnp.asarray(mask)[None, None, :, None])
    o = jnp.einsum('bdihqk,bdikhe->bdiqhe', p, vb.astype(jnp.float32))
    o = o.reshape(B, dil, Lp, H, E)[:, :, :L].transpose(0, 2, 1, 3, 4).reshape(B, S, H, E)
    lse = lse.transpose(0, 1, 2, 4, 3).reshape(B, dil, Lp, H)[:, :, :L].transpose(0, 2, 1, 3).reshape(B, S, H)
    return o, lse


def dilated_group_sample(q, k_ctx, v_ctx, dil, steps, wb):
    T = q.shape[1]
    idx = wb + np.arange(T)[:, None] - dil * np.arange(steps + 1)[None, :]
    valid = idx >= 0
    idx = np.maximum(idx, 0)
    kg = k_ctx[:, idx]
    vg = v_ctx[:, idx]
    s = jnp.einsum('bthe,btjhe->bhtj', q, kg, preferred_element_type=jnp.float32) * DIL_SCALE
    p, lse = masked_softmax(s, jnp.asarray(valid)[None, None])
    o = jnp.einsum('bhtj,btjhe->bthe', p, vg.astype(jnp.float32))
    return o, lse.transpose(0, 2, 1)


def dilated_mixer(h, positions, k_ctx, v_ctx, w_q, w_out, is_prompt):
    B, T, _ = h.shape
    q = rope((h @ w_q).reshape(B, T, N_GROUPS * DIL_HEADS, DIL_HD), positions)
    q = q.reshape(B, T, N_GROUPS, DIL_HEADS, DIL_HD)
    outs, lses = [], []
    for g in range(N_GROUPS):
        dil = DIL_DILATIONS[g]
        steps = DIL_WINDOWS[g] // dil
        if is_prompt:
            o, lse = dilated_group_prompt(q[:, :, g], k_ctx, v_ctx, dil, steps)
        else:
            o, lse = dilated_group_sample(q[:, :, g], k_ctx, v_ctx, dil, steps, k_ctx.shape[1] - T)
        outs.append(o)
        lses.append(lse)
    w = jax.nn.softmax(jnp.stack(lses), axis=0)
    o = jnp.einsum('gbth,gbthe->bthe', w, jnp.stack(outs))
    return o.astype(h.dtype).reshape(B, T, DIL_HEADS * DIL_HD) @ w_out


def trunk(x, c, positions, gla_states, k_past, v_past, p):
    B, T, _ = x.shape
    silu_c = jax.nn.silu(c)
    new_states = []
    k_rows = v_rows = k_ctx = v_ctx = None
    for l in range(DEPTH):
        if l == N_A_LAYERS:
            kv_shift, kv_scale = jnp.split(silu_c @ p['w_ada_kv'] + p['b_ada_kv'], 2, axis=-1)
            k_rows, v_rows = jnp.split(modulate(x, kv_shift, kv_scale) @ p['w_kv'], 2, axis=-1)
            k_rows = rope(k_rows.reshape(B, T, DIL_HEADS, DIL_HD), positions)
            v_rows = v_rows.reshape(B, T, DIL_HEADS, DIL_HD)
            if k_past is None:
                k_ctx, v_ctx = k_rows, v_rows
            else:
                k_ctx = jnp.concatenate([k_past.astype(k_rows.dtype), k_rows], axis=1)
                v_ctx = jnp.concatenate([v_past.astype(v_rows.dtype), v_rows], axis=1)
        mods = (silu_c @ p['w_ada'][l] + p['b_ada'][l]).reshape(B, N_MOD, D_MODEL)
        sh1, sc1, g1, sh2, sc2, g2, sh3, sc3, g3 = [mods[:, i] for i in range(N_MOD)]
        y = swiglu(modulate(x, sh1, sc1), p['w_ffn1_up'][l], p['w_ffn1_down'][l])
        x = post_norm(x, FFN_RES * y, g1, p['ln_g'][l, 0], p['ln_b'][l, 0])
        h = modulate(x, sh2, sc2)
        if l < N_A_LAYERS:
            y, s_new = gla_mixer(h, gla_states[l], p['w_in_a'][l], p['w_gate2_a'][l], p['b_gate_a'][l],
                                 p['g_onorm_a'][l], p['w_out_a'][l])
            new_states.append(s_new)
        else:
            j = l - N_A_LAYERS
            y = dilated_mixer(h, positions, k_ctx, v_ctx, p['w_q_b'][j], p['w_out_b'][j], k_past is None)
        x = post_norm(x, y, g2, p['ln_g'][l, 1], p['ln_b'][l, 1])
        y = swiglu(modulate(x, sh3, sc3), p['w_ffn2_up'][l], p['w_ffn2_down'][l])
        x = post_norm(x, FFN_RES * y, g3, p['ln_g'][l, 2], p['ln_b'][l, 2])
    return x, jnp.stack(new_states), k_rows, v_rows


def setup_inputs(seed: int = 0) -> dict:
    key = jax.random.key(seed)
    ks = iter(jax.random.split(key, 32))

    def nrm(shape, s):
        return jax.random.normal(next(ks), shape, jnp.float32) * s

    wb = min(MAX_WINDOW, PAST_LEN)
    return {
        'x_prompt': nrm((BATCH, SEQ, D_MODEL), 1.0),
        'x_sample': nrm((DEC_BATCH, DEC_SEQ, D_MODEL), 1.0),
        'state_gla': nrm((N_A_LAYERS, DEC_BATCH, GLA_HEADS, GLA_DK, GLA_DV), 1.0),
        'cache_k': nrm((DEC_BATCH, wb, DIL_HEADS, DIL_HD), 1.0),
        'cache_v': nrm((DEC_BATCH, wb, DIL_HEADS, DIL_HD), 1.0),
        'c_prompt': nrm((BATCH, D_MODEL), 1.0),
        'c_sample': nrm((DEC_BATCH, D_MODEL), 1.0),
        'w_ada': nrm((DEPTH, D_MODEL, N_MOD * D_MODEL), 0.01),
        'b_ada': nrm((DEPTH, N_MOD * D_MODEL), 0.01),
        'ln_g': 1.0 + nrm((DEPTH, 3, D_MODEL), 0.02),
        'ln_b': nrm((DEPTH, 3, D_MODEL), 0.02),
        'w_ffn1_up': nrm((DEPTH, D_MODEL, 2 * D_FF), D_MODEL ** -0.5),
        'w_ffn1_down': nrm((DEPTH, D_FF, D_MODEL), D_FF ** -0.5 * BETA),
        'w_ffn2_up': nrm((DEPTH, D_MODEL, 2 * D_FF), D_MODEL ** -0.5),
        'w_ffn2_down': nrm((DEPTH, D_FF, D_MODEL), D_FF ** -0.5 * BETA),
        'w_in_a': nrm((N_A_LAYERS, D_MODEL, GLA_IN), D_MODEL ** -0.5),
        'w_gate2_a': nrm((N_A_LAYERS, GLA_RANK, GLA_QK), GLA_RANK ** -0.5),
        'b_gate_a': nrm((N_A_LAYERS, GLA_QK), 0.1),
        'g_onorm_a': 1.0 + nrm((N_A_LAYERS, GLA_DV), 0.02),
        'w_out_a': nrm((N_A_LAYERS, GLA_V, D_MODEL), GLA_V ** -0.5 * BETA),
        'w_ada_kv': nrm((D_MODEL, 2 * D_MODEL), 0.01),
        'b_ada_kv': nrm((2 * D_MODEL,), 0.01),
        'w_kv': nrm((D_MODEL, 2 * DIL_HEADS * DIL_HD), D_MODEL ** -0.5),
        'w_q_b': nrm((N_B_LAYERS, D_MODEL, N_GROUPS * DIL_HEADS * DIL_HD), D_MODEL ** -0.5),
        'w_out_b': nrm((N_B_LAYERS, DIL_HEADS * DIL_HD, D_MODEL), (DIL_HEADS * DIL_HD) ** -0.5 * BETA),
    }


def reference(x_prompt, x_sample, state_gla, cache_k, cache_v, c_prompt, c_sample,
              w_ada, b_ada, ln_g, ln_b, w_ffn1_up, w_ffn1_down, w_ffn2_up, w_ffn2_down,
              w_in_a, w_gate2_a, b_gate_a, g_onorm_a, w_out_a,
              w_ada_kv, b_ada_kv, w_kv, w_q_b, w_out_b):
    p = {
        'w_ada': w_ada, 'b_ada': b_ada, 'ln_g': ln_g, 'ln_b': ln_b,
        'w_ffn1_up': w_ffn1_up, 'w_ffn1_down': w_ffn1_down,
        'w_ffn2_up': w_ffn2_up, 'w_ffn2_down': w_ffn2_down,
        'w_in_a': w_in_a, 'w_gate2_a': w_gate2_a, 'b_gate_a': b_gate_a,
        'g_onorm_a': g_onorm_a, 'w_out_a': w_out_a,
        'w_ada_kv': w_ada_kv, 'b_ada_kv': b_ada_kv, 'w_kv': w_kv,
        'w_q_b': w_q_b, 'w_out_b': w_out_b,
    }
    n_prompt, t_prompt = x_prompt.shape[0], x_prompt.shape[1]
    t_sample = x_sample.shape[1]
    zero_state = jnp.zeros((N_A_LAYERS, n_prompt, GLA_HEADS, GLA_DK, GLA_DV), x_prompt.dtype)
    y_prompt, state_gla_prompt, k_p, v_p = trunk(
        x_prompt, c_prompt, jnp.arange(t_prompt), zero_state, None, None, p)
    keep = min(MAX_WINDOW, t_prompt)
    y_sample, state_gla_sample, k_rows_sample, v_rows_sample = trunk(
        x_sample, c_sample, PAST_LEN + jnp.arange(t_sample), state_gla, cache_k, cache_v, p)
    return (y_prompt, y_sample, state_gla_prompt, state_gla_sample,
            k_p[:, -keep:], v_p[:, -keep:], k_rows_sample, v_rows_sample)
```

```python
import numpy as np
from contextlib import ExitStack
import concourse.bass as bass
import concourse.mybir as mybir
from concourse.bass_utils import run_bass_kernel_spmd

F32 = mybir.dt.float32
BF16 = mybir.dt.bfloat16
AF = mybir.ActivationFunctionType
ALU = mybir.AluOpType
AX = mybir.AxisListType

D = 1024
KC = 8
T = 2048
NSMP = 16
NT = T + NSMP
DFF = 2816
NJ = 22
ALPHA = (2 * 2) ** 0.25
LN_EPS = 1e-5
EPSP = LN_EPS / (ALPHA * ALPHA)
GLA_IN = 3088
NEG = -30000.0
TICKN = 3


class _Op:
    __slots__ = ("eng", "fn", "deps", "dma", "slot", "dval", "ms", "mval")


class Prog:
    ENGS = ("pe", "act", "dve", "pool", "sp")

    def __init__(self, nc):
        self.nc = nc
        self.q = {k: [] for k in self.ENGS}
        self.rows = {}
        self.psum = {}
        self.live = {}
        self.rr = {"sp": 0, "pool": 0, "act": 0}
        self.dcnt = {}
        self.NSLOT = 6

    def reg(self, ap, rowsize):
        self.rows[ap.tensor.name] = rowsize

    def box(self, ap):
        name = ap.tensor.name
        dims = ap.ap
        off = int(ap.offset)
        if name in self.rows:
            R = self.rows[name]
            p0 = off // R
            f0 = off % R
            ps, pc = dims[0]
            assert ps % R == 0, (name, dims, R)
            p1 = p0 + (pc - 1) * (ps // R) + 1
            f1 = f0 + sum((c - 1) * abs(s) for s, c in dims[1:]) + 1
            assert f1 <= R, (name, dims, off, R)
            if name in self.psum:
                bs = self.psum[name]
                return (name, 0, 128, (f0 // bs) * bs, -(-f1 // bs) * bs)
            return (name, p0, p1, f0, f1)
        f1 = off + sum((c - 1) * abs(s) for s, c in dims) + 1
        return (name, 0, 1, off, f1)

    @staticmethod
    def _ov(a, b):
        return a[1] < b[2] and b[1] < a[2] and a[3] < b[4] and b[3] < a[4]

    @staticmethod
    def _inside(a, b):
        return a[1] >= b[1] and a[2] <= b[2] and a[3] >= b[3] and a[4] <= b[4]

    def add(self, eng, fn, reads=(), writes=(), dma=False):
        op = _Op()
        op.eng, op.fn, op.dma, op.ms, op.mval = eng, fn, dma, False, 0
        idx = len(self.q[eng])
        deps = set()
        if dma:
            slot = self.rr[eng] % self.NSLOT
            self.rr[eng] += 1
            c = self.dcnt.get((eng, slot), 0) + 1
            self.dcnt[(eng, slot)] = c
            op.slot, op.dval = slot, 16 * c
            ev = ("D", eng, slot, 16 * c)
        else:
            op.slot, op.dval = None, 0
            ev = ("E", eng, idx)
        rb = [self.box(a) for a in reads]
        wb = [self.box(a) for a in writes]
        for b in rb:
            isps = b[0] in self.psum
            for rec in self.live.get(b[0], ()):
                if (rec[1] == "w" or (isps and rec[2][1] != eng)) and self._ov(b, rec[0]):
                    e = rec[2]
                    if e[0] == "E" and e[1] == eng and not dma and eng == "pe":
                        continue
                    deps.add(e)
        for b in wb:
            for rec in self.live.get(b[0], ()):
                if self._ov(b, rec[0]):
                    e = rec[2]
                    if e[0] == "E" and e[1] == eng and not dma and eng == "pe":
                        continue
                    deps.add(e)
        deps.discard(ev)
        op.deps = deps
        for b in wb:
            lst = self.live.setdefault(b[0], [])
            lst[:] = [r for r in lst if not self._inside(r[0], b)]
            lst.append((b, "w", ev))
        for b in rb:
            lst = self.live.setdefault(b[0], [])
            if not dma:
                lst[:] = [r for r in lst if not (r[1] == "r" and r[2][0] == "E" and r[2][1] == eng
                                                  and self._inside(r[0], b))]
            lst.append((b, "r", ev))
        self.q[eng].append(op)
        return op

    def finalize_and_emit(self, es):
        nc = self.nc
        for eng in self.ENGS:
            for op in self.q[eng]:
                for d in op.deps:
                    if d[0] == "E":
                        self.q[d[1]][d[2]].ms = True
        for eng in self.ENGS:
            c = 0
            for op in self.q[eng]:
                if op.ms:
                    c += 1
                    op.mval = c
        esem = {eng: es.enter_context(nc.semaphore("s_" + eng)) for eng in self.ENGS}
        dsem = {}
        for (eng, slot) in sorted(self.dcnt):
            dsem[(eng, slot)] = es.enter_context(nc.semaphore("d_%s%d" % (eng, slot)))
        engobj = {"pe": "tensor", "act": "scalar", "dve": "vector", "pool": "gpsimd", "sp": "sync"}
        block = es.enter_context(nc.Block())
        prog = self

        def make(eng):
            def body(e):
                known = {}
                for op in prog.q[eng]:
                    waits = {}
                    for d in op.deps:
                        if d[0] == "E":
                            sem, val = esem[d[1]], prog.q[d[1]][d[2]].mval
                        else:
                            sem, val = dsem[(d[1], d[2])], d[3]
                        key = id(sem)
                        if key not in waits or waits[key][1] < val:
                            waits[key] = (sem, val)
                    if op.dma and op.dval > 16:
                        sem = dsem[(eng, op.slot)]
                        key = id(sem)
                        if key not in waits or waits[key][1] < op.dval - 16:
                            waits[key] = (sem, op.dval - 16)
                    for key, (sem, val) in waits.items():
                        if known.get(key, 0) < val:
                            e.wait_ge(sem, val)
                            known[key] = val
                    ins = op.fn(e)
                    if op.dma:
                        ins.then_inc(dsem[(eng, op.slot)], 16)
                    elif op.ms:
                        ins.then_inc(esem[eng], 1)
                if eng == "sp":
                    for (qe, slot), c in sorted(prog.dcnt.items()):
                        e.wait_ge(dsem[(qe, slot)], 16 * c)
                    for oe in prog.ENGS:
                        tot = sum(1 for o in prog.q[oe] if o.ms)
                        if tot and oe != "sp":
                            e.wait_ge(esem[oe], tot)
            return body

        for eng in self.ENGS:
            getattr(block, engobj[eng])(make(eng))


def mk(ap, dims):
    return bass.AP(ap.tensor, ap.offset, [list(d) for d in dims])


def split_last(ap, a, b):
    d = list(ap.ap)
    st, c = d[-1]
    assert c == a * b
    return mk(ap, d[:-1] + [(st * b, a), (st, b)])


def bcast_last(ap, n):
    return mk(ap, list(ap.ap) + [(0, n)])


class Arena:
    def __init__(self, ap, rowsize):
        self.ap = ap
        self.R = rowsize
        self.off = 0

    def reset(self, off=0):
        self.off = off

    def take(self, shape):
        n = 1
        for s in shape[1:]:
            n *= s
        o = self.off
        self.off += n
        assert self.off <= self.R, ("arena overflow", self.off, self.R)
        v = self.ap[0:shape[0], o:o + n]
        if len(shape) == 2:
            return v
        d = list(v.ap)[:1]
        st = n
        for s in shape[1:]:
            st //= s
            d.append((st, s))
        return mk(v, d)


def build_program(debug=False, upto=99, sub=99):
    nc = bass.Bass("TRN2", target_bir_lowering=False)
    es = ExitStack()
    P = Prog(nc)

    def din(name, shape, dt=F32):
        return nc.dram_tensor(name, list(shape), dt, kind="ExternalInput").ap()

    def dout(name, shape, dt=F32):
        return nc.dram_tensor(name, list(shape), dt, kind="ExternalOutput").ap()

    xin = din("xin", [128, KC, NT])
    cin = din("cin", [128, KC, 5])
    w_ada = din("w_ada", [2, 8, 128, KC, 1152])
    b_ada = din("b_ada", [128, 2, 72])
    w_adakv = din("w_adakv", [2, 128, KC, 1024])
    b_adakv = din("b_adakv", [128, 16])
    lnp = din("lnp", [128, 2, 6, KC])
    w_up = din("w_up", [4, NJ, 128, 2, KC, 128])
    w_dn = din("w_dn", [4, KC, 128, NJ, 128])
    cstf_d = din("cstf_d", [128, 720])
    cstb_d = din("cstb_d", [128, 736])
    ropet = din("ropet", [2, 128, NT])
    w_kv_d = din("w_kv_d", [128, KC, 2048])
    w_q_d = din("w_q_d", [KC, 128, KC, 384])
    w_ob_d = din("w_ob_d", [128, KC, 1024])
    cache_k = din("cache_k_d", [4, 2048, 1024])
    cache_v = din("cache_v_d", [4, 2048, 1024])
    w_in_qk = din("w_in_qk", [128, KC, 1024])
    w_in_glr = din("w_in_glr", [128, KC, 16])
    w_in_v = din("w_in_v", [128, KC, 1024])
    w_in_r = din("w_in_r", [128, KC, 1024])
    w_outa = din("w_outa", [128, KC, 1024])
    wg2_d = din("wg2", [16, 512])
    bgate_d = din("bgate", [128, 4])
    gonorm_d = din("gonorm", [128, 256])
    state_in = din("state_in", [4, 128, 4, 256])
    y_fm = dout("y_fm", [128, KC, NT])
    st_p = dout("st_p", [128, 4, 256])
    k_fm = dout("k_fm", [128, KC, NT])
    v_tm = dout("v_tm", [NT, 1024])
    kscr = nc.dram_tensor("kscr", [KC, 128, NT], BF16, kind="Internal").ap()
    vscr = nc.dram_tensor("vscr", [NT, 1040], BF16, kind="Internal").ap()
    st_s = dout("st_s", [4, 128, 4, 256])

    def sb(name, shape, dt):
        t = es.enter_context(nc.sbuf_tensor(name, list(shape), dt))
        a = t[:]
        n = 1
        for s in shape[1:]:
            n *= s
        P.reg(a, n)
        return a

    def ps(name, shape, dt):
        t = es.enter_context(nc.psum_tensor(name, list(shape), dt))
        a = t[:]
        n = 1
        for s in shape[1:]:
            n *= s
        P.reg(a, n)
        P.psum[a.tensor.name] = 512 if dt == F32 else 1024
        return a

    x = sb("x", [128, KC, NT], F32)
    AB_R = 53300
    AF_R = 7600
    arenaB = Arena(sb("arenaB", [128, AB_R], BF16), AB_R)
    arenaF = Arena(sb("arenaF", [128, AF_R], F32), AF_R)
    mods = sb("mods", [128, 2, 72, 5], F32)
    modkv = sb("modkv", [128, 16, 5], F32)
    lnp_sb = sb("lnp_sb", [128, 2, 6, KC], F32)
    cstf = sb("cstf", [128, 720], F32)
    cstb = sb("cstb", [128, 736], BF16)
    sct = sb("sct", [128, KC, 5], BF16)
    cin_sb = sb("cin_sb", [128, KC, 5], F32)
    bada_sb = sb("bada_sb", [128, 2, 72], F32)
    badakv_sb = sb("badakv_sb", [128, 16], F32)
    psF = ps("psF", [128, 7, 512], F32)
    psB = ps("psB", [128, 1024], BF16)

    ones_b = cstb[:, 0:128]
    ident_b = cstb[:, 128:256]
    causal_f = cstf[:, 0:128]
    scanm_p = cstf[:, 128:640]
    scanm_s = cstf[:, 640:656]
    sel_f = cstf[0:65, 656:720]
    maskp_b = cstb[:, 256:512]
    pswap_b = cstb[:, 608:736]
    masks_b = cstb[:, 512:608]

    P.add("sp", lambda e: e.dma_start(out=x, in_=xin), reads=[xin], writes=[x], dma=True)
    P.add("sp", lambda e: e.dma_start(out=cin_sb, in_=cin), reads=[cin], writes=[cin_sb], dma=True)
    P.add("sp", lambda e: e.dma_start(out=lnp_sb, in_=lnp), reads=[lnp], writes=[lnp_sb], dma=True)
    P.add("sp", lambda e: e.dma_start(out=cstf, in_=cstf_d), reads=[cstf_d], writes=[cstf], dma=True)
    P.add("sp", lambda e: e.dma_start(out=bada_sb, in_=b_ada), reads=[b_ada], writes=[bada_sb], dma=True)
    P.add("sp", lambda e: e.dma_start(out=badakv_sb, in_=b_adakv), reads=[b_adakv], writes=[badakv_sb], dma=True)
    P.add("pool", lambda e: e.dma_start(out=cstb, in_=cstb_d), reads=[cstb_d], writes=[cstb], dma=True)
    P.add("act", lambda e: e.activation(out=sct, in_=cin_sb, func=AF.Silu), reads=[cin_sb], writes=[sct])

    DERIVE_ALL = ((1, None), (4, None), (7, None), (2, 0.5 / ALPHA), (5, 1.0 / ALPHA), (8, 0.5 / ALPHA))

    def derive(mt, entries=DERIVE_ALL):
        for i, coef in entries:
            v = mt[:, i * 8:(i + 1) * 8, :]
            if coef is None:
                P.add("dve", lambda e, v=v: e.tensor_scalar_add(out=v, in0=v, scalar1=1.0), reads=[v], writes=[v])
            else:
                P.add("dve", lambda e, v=v, coef=coef: e.tensor_scalar(out=v, in0=v, scalar1=1.0, scalar2=coef,
                                                                      op0=ALU.add, op1=ALU.mult), reads=[v], writes=[v])

    def ada_gen(l, bufs, noc, bank, pcs=range(8), entries=DERIVE_ALL):
        nb_ = 0
        for pc in pcs:
            for sp_ in range(9 // noc):
                wb_ = bufs[nb_ % 2]
                nb_ += 1
                src = w_ada[l, pc][:, :, sp_ * noc * 128:(sp_ + 1) * noc * 128]
                P.add("pool", lambda e, wb_=wb_, src=src: e.dma_start(out=wb_, in_=src),
                      reads=[src], writes=[wb_], dma=True)
                for ol in range(noc):
                    oc = pc * 9 + sp_ * noc + ol
                    for k in range(KC):
                        o = psF[:, bank, oc * 5:oc * 5 + 5]
                        lt = wb_[:, k, ol * 128:(ol + 1) * 128]
                        rh = sct[:, k, :]
                        P.add("pe", lambda e, o=o, lt=lt, rh=rh, k=k: e.matmul(o, lhsT=lt, rhs=rh, start=(k == 0), stop=(k == KC - 1)),
                              reads=[lt, rh], writes=[o])
                yield 1
        o0, o1 = min(pcs) * 9, (max(pcs) + 1) * 9
        pv = mk(psF[:, bank, o0 * 5:o1 * 5], [psF[:, bank, o0 * 5:o1 * 5].ap[0], (5, o1 - o0), (1, 5)])
        bb = bcast_last(bada_sb[:, l, o0:o1], 5)
        mo = mods[:, l, o0:o1, :]
        P.add("dve", lambda e, mo=mo, pv=pv, bb=bb: e.tensor_tensor(out=mo, in0=pv, in1=bb, op=ALU.add),
              reads=[pv, bada_sb[:, l, o0:o1]], writes=[mo])
        derive(mods[:, l], entries)
        yield 0

    arenaB.reset()
    wada_buf = [arenaB.take([128, KC, 1152]) for _ in range(2)]
    for _ in ada_gen(0, wada_buf, 9, 0, pcs=range(0, 3), entries=DERIVE_ALL[0:1] + DERIVE_ALL[3:4]):
        pass
    nb = 0
    for pc in range(2):
        wb_ = wada_buf[nb % 2][:, :, 0:1024]
        nb += 1
        src = w_adakv[pc]
        P.add("pool", lambda e, wb_=wb_, src=src: e.dma_start(out=wb_, in_=src), reads=[src], writes=[wb_], dma=True)
        for ol in range(8):
            oc = pc * 8 + ol
            for k in range(KC):
                o = psF[:, 2, oc * 5:oc * 5 + 5]
                lt = wb_[:, k, ol * 128:(ol + 1) * 128]
                rh = sct[:, k, :]
                P.add("pe", lambda e, o=o, lt=lt, rh=rh, k=k: e.matmul(o, lhsT=lt, rhs=rh, start=(k == 0), stop=(k == KC - 1)),
                      reads=[lt, rh], writes=[o])
    pv = mk(psF[:, 2, 0:80], [psF[:, 2, 0:80].ap[0], (5, 16), (1, 5)])
    bb = bcast_last(badakv_sb, 5)
    P.add("dve", lambda e, pv=pv, bb=bb: e.tensor_tensor(out=modkv, in0=pv, in1=bb, op=ALU.add),
          reads=[pv, badakv_sb], writes=[modkv])
    v = modkv[:, 8:16, :]
    P.add("dve", lambda e, v=v: e.tensor_scalar_add(out=v, in0=v, scalar1=1.0), reads=[v], writes=[v])

    halves = [[(0, 512), (512, 512)], [(1024, 512), (1536, 512), (2048, 16)]]

    def modulate(dst, h0, cols, mod_t, sh_oc, sc_oc):
        for (lo, n) in cols:
            if lo < T:
                for k in range(KC):
                    o = dst[:, k, lo - h0:lo - h0 + n]
                    i_ = x[:, k, lo:lo + n]
                    b_ = mod_t[:, sh_oc + k, 0:1]
                    s_ = mod_t[:, sc_oc + k, 0:1]
                    P.add("act", lambda e, o=o, i_=i_, b_=b_, s_=s_: e.activation(out=o, in_=i_, func=AF.Identity, bias=b_, scale=s_),
                          reads=[i_, b_, s_], writes=[o])
            else:
                o = dst[:, :, lo - h0:lo - h0 + n]
                i_ = x[:, :, lo:lo + n]
                tmp = arenaF_tmp16
                s_ = mod_t[:, sc_oc:sc_oc + 8, 1:5]
                b_ = mod_t[:, sh_oc:sh_oc + 8, 1:5]
                P.add("dve", lambda e, i_=i_, s_=s_, tmp=tmp: e.tensor_tensor(out=split_last(tmp, 4, 4), in0=split_last(i_, 4, 4),
                                                                          in1=bcast_last(s_, 4), op=ALU.mult),
                      reads=[i_, s_], writes=[tmp])
                P.add("dve", lambda e, o=o, b_=b_, tmp=tmp: e.tensor_tensor(out=split_last(o, 4, 4), in0=split_last(tmp, 4, 4),
                                                                        in1=bcast_last(b_, 4), op=ALU.add),
                      reads=[tmp, b_], writes=[o])

    def residual_add(Y, c, lo, n, mod_t, g_oc):
        xs = x[:, c, lo:lo + n]
        if lo < T:
            g_ = mod_t[:, g_oc + c, 0:1]
            P.add("dve", lambda e, xs=xs, Y=Y, g_=g_: e.scalar_tensor_tensor(out=xs, in0=Y, scalar=g_, in1=xs, op0=ALU.mult, op1=ALU.add),
                  reads=[Y, g_, xs], writes=[xs])
        else:
            g_ = mod_t[:, g_oc + c, 1:5]
            tmp = arenaF_tmp16[:, 0, :]
            P.add("dve", lambda e, Y=Y, g_=g_, tmp=tmp: e.tensor_tensor(out=split_last(tmp, 4, 4), in0=split_last(Y, 4, 4),
                                                                    in1=bcast_last(g_, 4), op=ALU.mult),
                  reads=[Y, g_], writes=[tmp])
            P.add("dve", lambda e, xs=xs, tmp=tmp: e.tensor_tensor(out=xs, in0=xs, in1=tmp, op=ALU.add),
                  reads=[xs, tmp], writes=[xs])

    class LNState:
        pass

    def ln_begin(tiles):
        st = LNState()
        st.tiles = tiles
        st.mu = {}
        st.e2 = {}
        pi = 0
        for (lo, n) in tiles:
            if lo < T:
                st.mu[lo] = psF[:, 2 * pi, 0:n]
                st.e2[lo] = psF[:, 2 * pi + 1, 0:n]
                pi += 1
            else:
                st.mu[lo] = psF[:, 6, 0:n]
                st.e2[lo] = psF[:, 6, n:2 * n]
        st.pending = []
        return st

    def ln_accum(st, c, lo, n, zb, zq):
        xs = x[:, c, lo:lo + n]
        zb_ = zb[:, 0:n]
        zq_ = zq[:, 0:n] if lo < T else zb[:, n:2 * n]
        P.add("act", lambda e, zb_=zb_, xs=xs: e.activation(out=zb_, in_=xs, func=AF.Identity), reads=[xs], writes=[zb_])
        P.add("act", lambda e, zq_=zq_, xs=xs: e.activation(out=zq_, in_=xs, func=AF.Square), reads=[xs], writes=[zq_])
        mu, e2 = st.mu[lo], st.e2[lo]

        def later():
            if lo >= T:
                both = zb[:, 0:2 * n]
                o2 = psF[:, 6, 0:2 * n]
                P.add("pe", lambda e: e.matmul(o2, lhsT=ones_b, rhs=both, start=(c == 0), stop=(c == KC - 1)), reads=[ones_b, both], writes=[o2])
                return
            P.add("pe", lambda e: e.matmul(mu, lhsT=ones_b, rhs=zb_, start=(c == 0), stop=(c == KC - 1)), reads=[ones_b, zb_], writes=[mu])
            P.add("pe", lambda e: e.matmul(e2, lhsT=ones_b, rhs=zq_, start=(c == 0), stop=(c == KC - 1)), reads=[ones_b, zq_], writes=[e2])
        st.pending.append(later)

    def ln_flush(st, keep=0):
        while len(st.pending) > keep:
            st.pending.pop(0)()

    def ln_finish(st, li, defer=False):
        ln_flush(st)
        per = []
        for j, (lo, n) in enumerate(st.tiles):
            mu_ps, e2_ps = st.mu[lo], st.e2[lo]
            mu_sb = lnF["mu"][j][:, 0:n]
            m2 = lnF["va"][j][:, 0:n]
            rstd = lnF["rs"][j][:, 0:n]
            P.add("act", lambda e, mu_sb=mu_sb, mu_ps=mu_ps: e.activation(out=mu_sb, in_=mu_ps, func=AF.Identity), reads=[mu_ps], writes=[mu_sb])
            P.add("act", lambda e, m2=m2, mu_ps=mu_ps: e.activation(out=m2, in_=mu_ps, func=AF.Square), reads=[mu_ps], writes=[m2])
            P.add("dve", lambda e, m2=m2, e2_ps=e2_ps: e.tensor_tensor(out=m2, in0=e2_ps, in1=m2, op=ALU.subtract), reads=[e2_ps, m2], writes=[m2])
            P.add("act", lambda e, m2=m2, rstd=rstd: e.activation(out=rstd, in_=m2, func=AF.Sqrt, bias=EPSP, scale=1.0),
                  reads=[m2], writes=[rstd])
            P.add("dve", lambda e, rstd=rstd: e.reciprocal(out=rstd, in_=rstd), reads=[rstd], writes=[rstd])
            per.append((lo, n, mu_sb, rstd))
        tbufs = lnF["t"]

        def pass2():
            k2 = 0
            for (lo, n, mu_sb, rstd) in per:
                for c in range(KC):
                    xs = x[:, c, lo:lo + n]
                    t1 = tbufs[k2 % 2][:, 0:n]
                    k2 += 1
                    g_ = lnp_sb[:, 0, li, c:c + 1]
                    b_ = lnp_sb[:, 1, li, c:c + 1]
                    P.add("dve", lambda e, t1=t1, xs=xs, mu_sb=mu_sb: e.tensor_tensor(out=t1, in0=xs, in1=mu_sb, op=ALU.subtract), reads=[xs, mu_sb], writes=[t1])
                    P.add("dve", lambda e, t1=t1, rstd=rstd: e.tensor_tensor(out=t1, in0=t1, in1=rstd, op=ALU.mult), reads=[t1, rstd], writes=[t1])
                    P.add("act", lambda e, xs=xs, t1=t1, g_=g_, b_=b_: e.activation(out=xs, in_=t1, func=AF.Identity, bias=b_, scale=g_),
                          reads=[t1, g_, b_], writes=[xs])
                    yield 1
        gen2 = pass2()
        if defer:
            return gen2
        for _ in gen2:
            pass
        return None

    def ln_bufs(ntile_p, with_sample):
        d = {"mu": [], "va": [], "rs": [], "t": []}
        for _ in range(ntile_p):
            for kname in ("mu", "va", "rs"):
                d[kname].append(arenaF.take([128, 512]))
        if with_sample:
            for kname in ("mu", "va", "rs"):
                d[kname].append(arenaF.take([128, 16]))
        d["t"] = [arenaF.take([128, 512]) for _ in range(2)]
        return d

    def ffn(l, which, ada_bg=False):
        fi = l * 2 + which
        li = l * 3 + (0 if which == 0 else 2)
        mod_t = mods[:, l]
        sh_oc, sc_oc, g_oc = ((0, 8, 16) if which == 0 else (48, 56, 64))
        arenaB.reset()
        hbuf = arenaB.take([128, KC, 1040])
        gbuf = arenaB.take([128, NJ, 1040])
        wup = [arenaB.take([128, 2, KC, 128]) for _ in range(3)]
        wdn = [arenaB.take([128, NJ, 128]) for _ in range(2)]
        zbq = [arenaB.take([128, 512]) for _ in range(6)]
        bg = None
        if ada_bg:
            bg = ada_gen(0, [arenaB.take([128, KC, 384]) for _ in range(2)], 3, 6, pcs=range(3, 8),
                         entries=DERIVE_ALL[1:3] + DERIVE_ALL[4:6])
        arenaF.reset()
        nonlocal arenaF_tmp16, lnF
        arenaF_tmp16 = arenaF.take([128, KC, 16])
        sa = [arenaF.take([128, 512]) for _ in range(2)]
        lnF = ln_bufs(2, True)
        cnt = 0
        wcnt = 0
        dcnt_ = 0
        pend_ln = None
        for hi, tiles in enumerate(halves):
            h0 = tiles[0][0]
            if hi == 0:
                modulate(hbuf, h0, tiles, mod_t, sh_oc, sc_oc)
            for j in range(NJ):
                wb_ = wup[wcnt % 3]
                wcnt += 1
                src = w_up[fi, j]
                P.add("pool", lambda e, wb_=wb_, src=src: e.dma_start(out=wb_, in_=src), reads=[src], writes=[wb_], dma=True)
                for (lo, n) in tiles:
                    pa = psF[:, cnt % 2, 0:n]
                    pu = psF[:, 2 + cnt % 2, 0:n]
                    sa_ = sa[cnt % 2][:, 0:n]
                    cnt += 1
                    for (o, a_or_u) in ((pa, 0), (pu, 1)):
                        for k in range(KC):
                            lt = wb_[:, a_or_u, k, :]
                            rh = hbuf[:, k, lo - h0:lo - h0 + n]
                            P.add("pe", lambda e, o=o, lt=lt, rh=rh, k=k: e.matmul(o, lhsT=lt, rhs=rh, start=(k == 0), stop=(k == KC - 1)),
                                  reads=[lt, rh], writes=[o])
                    P.add("act", lambda e, sa_=sa_, pa=pa: e.activation(out=sa_, in_=pa, func=AF.Silu), reads=[pa], writes=[sa_])
                    go = gbuf[:, j, lo - h0:lo - h0 + n]
                    P.add("dve", lambda e, go=go, sa_=sa_, pu=pu: e.tensor_tensor(out=go, in0=pu, in1=sa_, op=ALU.mult), reads=[pu, sa_], writes=[go])
                    if pend_ln is not None:
                        next(pend_ln, None)
                    if bg is not None:
                        next(bg, None)
            if hi == 0:
                modulate(hbuf, halves[1][0][0], halves[1], mod_t, sh_oc, sc_oc)
            st = ln_begin(tiles)
            zc = 0
            for c in range(KC):
                wb_ = wdn[dcnt_ % 2]
                dcnt_ += 1
                src = w_dn[fi, c]
                P.add("pool", lambda e, wb_=wb_, src=src: e.dma_start(out=wb_, in_=src), reads=[src], writes=[wb_], dma=True)
                for (lo, n) in tiles:
                    Y = psF[:, 4 + zc % 2, 0:n]
                    for kk in range(NJ):
                        lt = wb_[:, kk, :]
                        rh = gbuf[:, kk, lo - h0:lo - h0 + n]
                        P.add("pe", lambda e, Y=Y, lt=lt, rh=rh, kk=kk: e.matmul(Y, lhsT=lt, rhs=rh, start=(kk == 0), stop=(kk == NJ - 1)),
                              reads=[lt, rh], writes=[Y])
                    ln_flush(st, keep=0)
                    residual_add(Y, c, lo, n, mod_t, g_oc)
                    ln_accum(st, c, lo, n, zbq[(zc % 3) * 2], zbq[(zc % 3) * 2 + 1])
                    zc += 1
            if pend_ln is not None:
                for _ in pend_ln:
                    pass
            pend_ln = ln_finish(st, li, defer=(hi == 0))
        if bg is not None:
            for _ in bg:
                pass


    def gla_phase():
        nonlocal arenaF_tmp16, lnF
        mod_t = mods[:, 0]
        sh_oc, sc_oc, g_oc, li = 24, 32, 40, 1
        arenaB.reset()
        arenaF.reset()
        wqk = arenaB.take([128, KC, 1024])
        wglr = arenaB.take([128, KC, 16])
        wbuf = [arenaB.take([128, KC, 1024]) for _ in range(2)]
        ht = arenaB.take([128, KC, 512])
        qk = arenaB.take([128, 8, 512])
        vtm = arenaB.take([128, 4, 1024])
        ATs = [arenaB.take([128, 4, 128]) for _ in range(2)]
        ktm = [arenaB.take([128, 4, 128]) for _ in range(2)]
        Sbf = [arenaB.take([128, 4, 256]) for _ in range(2)]
        on = [arenaB.take([128, 1024]) for _ in range(2)]
        sR = arenaB.take([128, KC, 512])
        zbq = [arenaB.take([128, 512]) for _ in range(6)]
        arenaF_tmp16 = arenaF.take([128, KC, 16])
        tf = [arenaF.take([128, 512]) for _ in range(8)]
        lnF = {"mu": [tf[3], tf[3]], "va": [tf[4], tf[4]], "rs": [tf[5], tf[5]], "t": [tf[6], tf[7]]}
        S = arenaF.take([128, 4, 256])
        S1 = arenaF.take([128, 4, 256])
        eLs = arenaF.take([128, 4, 4])
        rstd4 = arenaF.take([128, 4])
        ssq = arenaF.take([128, 4])
        nbg = arenaF.take([128, 4])
        gB = arenaF.take([128, 256])
        wg2 = arenaF.take([16, 512])
        glrT = arenaF.take([16, 512])
        DKS = 128 ** -0.5

        P.add("pool", lambda e: e.dma_start(out=wqk, in_=w_in_qk), reads=[w_in_qk], writes=[wqk], dma=True)
        P.add("pool", lambda e: e.dma_start(out=wglr, in_=w_in_glr), reads=[w_in_glr], writes=[wglr], dma=True)
        P.add("sp", lambda e: e.dma_start(out=wg2, in_=wg2_d), reads=[wg2_d], writes=[wg2], dma=True)
        P.add("sp", lambda e: e.dma_start(out=nbg, in_=bgate_d), reads=[bgate_d], writes=[nbg], dma=True)
        P.add("sp", lambda e: e.dma_start(out=gB, in_=gonorm_d), reads=[gonorm_d], writes=[gB], dma=True)
        P.add("dve", lambda e: e.tensor_scalar_mul(out=nbg, in0=nbg, scalar1=-1.0), reads=[nbg], writes=[nbg])
        P.add("dve", lambda e: e.memset(S, 0.0), writes=[S])
        P.add("dve", lambda e: e.memset(Sbf[0], 0.0), writes=[Sbf[0]])
        state = {"wb": 0, "sb": 0, "ab": 0, "y": 0}

        def load_w(src):
            wb_ = wbuf[state["wb"] % 2]
            state["wb"] += 1
            P.add("pool", lambda e: e.dma_start(out=wb_, in_=src), reads=[src], writes=[wb_], dma=True)
            return wb_

        def chunk(cs, nt, vt, ck):
            i2 = state["ab"] % 2
            state["ab"] += 1
            A_, K_, on_ = ATs[i2], ktm[i2], on[i2]
            Scur = Sbf[state["sb"] % 2]
            Snext = Sbf[(state["sb"] + 1) % 2]
            state["sb"] += 1
            atp = psF[0:nt, 6, :]
            for h in range(4):
                o_ = atp[:, h * 128:h * 128 + nt]
                lt = qk[:, 4 + h, cs]
                rh = qk[:, h, cs]
                P.add("pe", lambda e, o_=o_, lt=lt, rh=rh: e.matmul(o_, lhsT=lt, rhs=rh, start=True, stop=True), reads=[lt, rh], writes=[o_])
            atv = mk(atp, [atp.ap[0], (128, 4), (1, nt)])
            av = A_[0:nt, :, 0:nt]
            cm = mk(causal_f[0:nt, 0:nt], [causal_f[0:nt, 0:nt].ap[0], (0, 4), (1, nt)])
            P.add("dve", lambda e: e.tensor_tensor(out=av, in0=atv, in1=cm, op=ALU.mult), reads=[atv, causal_f[0:nt, 0:nt]], writes=[av])
            for h in range(4):
                o_ = psB[0:nt, h * 128:(h + 1) * 128]
                i_ = qk[:, 4 + h, cs]
                P.add("pe", lambda e, o_=o_, i_=i_: e.transpose(o_, i_, ident_b), reads=[i_, ident_b], writes=[o_])
            kv_ = K_[0:nt]
            pb = mk(psB[0:nt, 0:512], [psB[0:nt, 0:512].ap[0], (128, 4), (1, 128)])
            P.add("act", lambda e: e.activation(out=kv_, in_=pb, func=AF.Identity), reads=[pb], writes=[kv_])
            ob_ = 4 if (state["ab"] % 2 == 1) else 0
            o_ps = mk(psF[0:nt, ob_, :], [psF[0:nt, ob_, :].ap[0], (256, 4), (1, 256)])
            for h in range(4):
                oh = o_ps[:, h, :]
                l1 = A_[0:nt, h, 0:nt]
                r1 = vt[0:nt, h * 256:(h + 1) * 256]
                l2 = qk[:, h, cs]
                r2 = Scur[:, h, :]
                P.add("pe", lambda e, oh=oh, l1=l1, r1=r1: e.matmul(oh, lhsT=l1, rhs=r1, start=True, stop=False), reads=[l1, r1], writes=[oh])
                P.add("pe", lambda e, oh=oh, l2=l2, r2=r2: e.matmul(oh, lhsT=l2, rhs=r2, start=False, stop=True), reads=[l2, r2], writes=[oh])
            u_ps = mk(psF[:, 2, :], [psF[:, 2, :].ap[0], (256, 4), (1, 256)])
            for h in range(4):
                uh = u_ps[:, h, :]
                l1 = K_[0:nt, h, :]
                r1 = vt[0:nt, h * 256:(h + 1) * 256]
                P.add("pe", lambda e, uh=uh, l1=l1, r1=r1: e.matmul(uh, lhsT=l1, rhs=r1, start=True, stop=True), reads=[l1, r1], writes=[uh])
            for h in range(4):
                el = eLs[:, h, ck:ck + 1]
                s_h, s1_h, uh = S[:, h, :], S1[:, h, :], u_ps[:, h, :]
                P.add("act", lambda e, s_h=s_h, s1_h=s1_h, el=el: e.activation(out=s1_h, in_=s_h, func=AF.Identity, scale=el), reads=[s_h, el], writes=[s1_h])
                P.add("dve", lambda e, s_h=s_h, s1_h=s1_h, uh=uh, el=el: e.scalar_tensor_tensor(out=s_h, in0=uh, scalar=el, in1=s1_h, op0=ALU.mult, op1=ALU.add),
                      reads=[uh, el, s1_h], writes=[s_h])
            P.add("act", lambda e: e.activation(out=Snext, in_=S, func=AF.Identity), reads=[S], writes=[Snext])
            def part_b():
                sq = mk(tf[0][0:nt, :], [tf[0][0:nt, :].ap[0], (1, 512)])
                sqv = mk(tf[0][0:nt, 0:1], [tf[0][0:nt, 0:1].ap[0], (256, 4), (1, 256)])
                sq_box = [tf[0][0:nt, :], tf[1][0:nt, :]]
                P.add("act", lambda e: e.activation(out=sqv, in_=o_ps, func=AF.Square), reads=[o_ps], writes=sq_box)
                ss = ssq[0:nt, :]
                rs = rstd4[0:nt, :]
                P.add("dve", lambda e: e.tensor_reduce(out=ss, in_=sqv, axis=AX.X, op=ALU.add), reads=sq_box, writes=[ss])
                P.add("act", lambda e: e.activation(out=rs, in_=ss, func=AF.Sqrt, bias=1e-6, scale=1.0 / 256.0), reads=[ss], writes=[rs])
                P.add("dve", lambda e: e.reciprocal(out=rs, in_=rs), reads=[rs], writes=[rs])
                for h in range(4):
                    oh = o_ps[:, h, :]
                    onh = on_[0:nt, h * 256:(h + 1) * 256]
                    r_ = rstd4[0:nt, h:h + 1]
                    g_ = gB[0:nt, :]
                    P.add("dve", lambda e, oh=oh, onh=onh, r_=r_, g_=g_: e.scalar_tensor_tensor(out=onh, in0=oh, scalar=r_, in1=g_, op0=ALU.mult, op1=ALU.mult),
                          reads=[oh, r_, g_], writes=[onh])
                for c in range(KC):
                    o_ = psB[:, c * 128:c * 128 + nt]
                    i_ = on_[0:nt, c * 128:(c + 1) * 128]
                    idn = ident_b[0:nt, 0:nt]
                    P.add("pe", lambda e, o_=o_, i_=i_, idn=idn: e.transpose(o_, i_, idn), reads=[i_, idn], writes=[o_])
                pbv = mk(psB[:, 0:1], [psB[:, 0:1].ap[0], (128, KC), (1, nt)])
                srv = sR[:, :, cs]
                P.add("dve", lambda e: e.tensor_tensor(out=srv, in0=pbv, in1=srv, op=ALU.mult), reads=[psB[:, :], srv], writes=[srv])
            return part_b

        all_tiles = [(0, 512), (512, 512), (1024, 512), (1536, 512), (2048, 16)]
        def st_laqk(ti):
            lo, n = all_tiles[ti]
            smp = lo >= T
            nch = 4
            ctok = 4 if smp else 128
            if ti == 0:
                modulate(ht, lo, [(lo, n)], mod_t, sh_oc, sc_oc)
            gp = psF[0:16, 6, 0:n]
            for k in range(KC):
                lt = wglr[:, k, :]
                rh = ht[:, k, 0:n]
                P.add("pe", lambda e, gp=gp, lt=lt, rh=rh, k=k: e.matmul(gp, lhsT=lt, rhs=rh, start=(k == 0), stop=(k == KC - 1)), reads=[lt, rh], writes=[gp])
            gl = glrT[:, 0:n]
            P.add("act", lambda e, gl=gl, gp=gp: e.activation(out=gl, in_=gp, func=AF.Identity), reads=[gp], writes=[gl])
            for h in range(4):
                ta, tb, tq, tk = [tf[(h % 2) * 4 + i][:, 0:n] for i in range(4)]
                lp = psF[:, h % 2, 0:n]
                lt = wg2[:, h * 128:(h + 1) * 128]
                P.add("pe", lambda e, lp=lp, lt=lt, gl=gl: e.matmul(lp, lhsT=lt, rhs=gl, start=True, stop=True), reads=[lt, gl], writes=[lp])
                nb_ = nbg[:, h:h + 1]
                P.add("act", lambda e, ta=ta, lp=lp, nb_=nb_: e.activation(out=ta, in_=lp, func=AF.Exp, bias=nb_, scale=-1.0), reads=[lp, nb_], writes=[ta])
                P.add("act", lambda e, ta=ta: e.activation(out=ta, in_=ta, func=AF.Ln, bias=1.0, scale=1.0), reads=[ta], writes=[ta])
                sm = (scanm_s if smp else scanm_p)[:, 0:n]
                P.add("dve", lambda e, tb=tb, ta=ta, sm=sm: e.tensor_tensor_scan(out=tb, data0=sm, data1=ta, initial=0.0, op0=ALU.mult, op1=ALU.add),
                      reads=[ta, sm], writes=[tb])
                P.add("act", lambda e, tq=tq, tb=tb: e.activation(out=tq, in_=tb, func=AF.Exp, scale=-1.0 / 16.0), reads=[tb], writes=[tq])
                P.add("act", lambda e, tk=tk, tb=tb: e.activation(out=tk, in_=tb, func=AF.Exp, scale=1.0 / 16.0), reads=[tb], writes=[tk])
                cl = 4 if smp else 128
                src_ = mk(tq[:, cl - 1:cl], [tq[:, cl - 1:cl].ap[0], (cl, 4)])
                dst_ = eLs[:, h, :]
                P.add("act", lambda e, src_=src_, dst_=dst_: e.activation(out=dst_, in_=src_, func=AF.Identity), reads=[tq], writes=[dst_])
                qp = psF[:, 2 + 2 * (h % 2), 0:n]
                kp = psF[:, 3 + 2 * (h % 2), 0:n]
                for (o_, cb) in ((qp, h * 128), (kp, 512 + h * 128)):
                    for k in range(KC):
                        lt = wqk[:, k, cb:cb + 128]
                        rh = ht[:, k, 0:n]
                        P.add("pe", lambda e, o_=o_, lt=lt, rh=rh, k=k: e.matmul(o_, lhsT=lt, rhs=rh, start=(k == 0), stop=(k == KC - 1)), reads=[lt, rh], writes=[o_])
                qo = qk[:, h, 0:n]
                ko = qk[:, 4 + h, 0:n]
                P.add("dve", lambda e, qo=qo, qp=qp, tq=tq: e.scalar_tensor_tensor(out=qo, in0=qp, scalar=DKS, in1=tq, op0=ALU.mult, op1=ALU.mult), reads=[qp, tq], writes=[qo])
                P.add("dve", lambda e, ko=ko, kp=kp, tk=tk: e.tensor_tensor(out=ko, in0=kp, in1=tk, op=ALU.mult), reads=[kp, tk], writes=[ko])

        def st_vr(ti):
            lo, n = all_tiles[ti]
            smp = lo >= T
            nch = 4
            ctok = 4 if smp else 128
            wv_ = load_w(w_in_v)
            nch = 4
            ctok = 4 if smp else 128
            for ck in range(nch):
                vp = mk(psF[0:ctok, 0, :], [psF[0:ctok, 0, :].ap[0], (1, 1024)])
                for hf in range(2):
                    o_ = psF[0:ctok, hf, :]
                    for k in range(KC):
                        lt = ht[:, k, ck * ctok:(ck + 1) * ctok]
                        rh = wv_[:, k, hf * 512:(hf + 1) * 512]
                        P.add("pe", lambda e, o_=o_, lt=lt, rh=rh, k=k: e.matmul(o_, lhsT=lt, rhs=rh, start=(k == 0), stop=(k == KC - 1)), reads=[lt, rh], writes=[o_])
                vo = vtm[0:ctok, ck, :]
                P.add("act", lambda e, vo=vo, vp=vp: e.activation(out=vo, in_=vp, func=AF.Identity), reads=[psF[0:ctok, 0:2, :]], writes=[vo])
            wr_ = load_w(w_in_r)
            for c in range(KC):
                rp = psF[:, 2 + c % 2, 0:n]
                for k in range(KC):
                    lt = wr_[:, k, c * 128:(c + 1) * 128]
                    rh = ht[:, k, 0:n]
                    P.add("pe", lambda e, rp=rp, lt=lt, rh=rh, k=k: e.matmul(rp, lhsT=lt, rhs=rh, start=(k == 0), stop=(k == KC - 1)), reads=[lt, rh], writes=[rp])
                so = sR[:, c, 0:n]
                P.add("act", lambda e, so=so, rp=rp: e.activation(out=so, in_=rp, func=AF.Silu), reads=[rp], writes=[so])
            if ti + 1 < len(all_tiles):
                nlo, nn = all_tiles[ti + 1]
                modulate(ht, nlo, [(nlo, nn)], mod_t, sh_oc, sc_oc)

        def st_rec(ti):
            lo, n = all_tiles[ti]
            smp = lo >= T
            nch = 4
            ctok = 4 if smp else 128
            pend_b = None
            for ck in range(nch):
                if smp:
                    src = state_in[ck]
                    P.add("sp", lambda e, src=src: e.dma_start(out=S, in_=src), reads=[src], writes=[S], dma=True)
                    Snx = Sbf[state["sb"] % 2]
                    P.add("act", lambda e, Snx=Snx: e.activation(out=Snx, in_=S, func=AF.Identity), reads=[S], writes=[Snx])
                nb_ = chunk(slice(ck * ctok, (ck + 1) * ctok), ctok, vtm[:, ck, :], ck)
                if smp:
                    dst = st_s[ck]
                    P.add("sp", lambda e, dst=dst: e.dma_start(out=dst, in_=S), reads=[S], writes=[dst], dma=True)
                if pend_b is not None:
                    pend_b()
                pend_b = nb_
            pend_b()
            if ti == 3:
                P.add("sp", lambda e: e.dma_start(out=st_p, in_=S), reads=[S], writes=[st_p], dma=True)

        def st_out(ti):
            lo, n = all_tiles[ti]
            smp = lo >= T
            nch = 4
            ctok = 4 if smp else 128
            wo_ = load_w(w_outa)
            st = ln_begin([(lo, n)])
            zc = 0
            for c in range(KC):
                Y = psF[:, 2 + c % 2, 0:n]
                for k in range(KC):
                    lt = wo_[:, k, c * 128:(c + 1) * 128]
                    rh = sR[:, k, 0:n]
                    P.add("pe", lambda e, Y=Y, lt=lt, rh=rh, k=k: e.matmul(Y, lhsT=lt, rhs=rh, start=(k == 0), stop=(k == KC - 1)), reads=[lt, rh], writes=[Y])
                ln_flush(st, keep=0)
                residual_add(Y, c, lo, n, mod_t, g_oc)
                ln_accum(st, c, lo, n, zbq[(zc % 3) * 2], zbq[(zc % 3) * 2 + 1])
                zc += 1
            ln_finish(st, li)

        nT = len(all_tiles)
        st_laqk(0)
        st_vr(0)
        for ti in range(nT):
            st_rec(ti)
            if ti + 1 < nT:
                st_laqk(ti + 1)
            st_out(ti)
            if ti + 1 < nT:
                st_vr(ti + 1)


    KS_OFF, QS_OFF = 52700, 52830
    ks_all = mk(arenaB.ap[:, KS_OFF:KS_OFF + 128], [arenaB.ap[:, KS_OFF:KS_OFF + 128].ap[0], (16, KC), (1, 16)])
    q3s_all = mk(arenaB.ap[:, QS_OFF:QS_OFF + 384], [arenaB.ap[:, QS_OFF:QS_OFF + 384].ap[0], (48, KC), (16, 3), (1, 16)])
    all_tiles5 = [(0, 512), (512, 512), (1024, 512), (1536, 512), (2048, 16)]

    def rope_tiles():
        cosT = arenaF.take([128, NT])
        sinT = arenaF.take([128, NT])
        return cosT, sinT

    def rope_a(xp, n, xb):
        P.add("act", lambda e: e.activation(out=xb, in_=xp, func=AF.Identity), reads=[xp], writes=[xb])

    def rope_b(xp, lo, n, cosT, sinT, t1, t2, outs, xb, xs_ps):
        P.add("pe", lambda e: e.matmul(xs_ps, lhsT=pswap_b, rhs=xb, start=True, stop=True), reads=[pswap_b, xb], writes=[xs_ps])
        c_ = cosT[:, lo:lo + n]
        s_ = sinT[:, lo:lo + n]
        P.add("dve", lambda e: e.tensor_tensor(out=t1, in0=xp, in1=c_, op=ALU.mult), reads=[xp, c_], writes=[t1])
        P.add("dve", lambda e: e.tensor_tensor(out=t2, in0=xs_ps, in1=s_, op=ALU.mult), reads=[xs_ps, s_], writes=[t2])
        for o_ in outs:
            if isinstance(o_, tuple):
                o_, d_ = o_
                a1 = mk(t1, [t1.ap[0], (1, d_), (d_, n // d_)])
                a2 = mk(t2, [t2.ap[0], (1, d_), (d_, n // d_)])
                P.add("dve", lambda e, o_=o_, a1=a1, a2=a2: e.tensor_tensor(out=o_, in0=a1, in1=a2, op=ALU.add), reads=[t1, t2], writes=[o_])
            else:
                P.add("dve", lambda e, o_=o_: e.tensor_tensor(out=o_, in0=t1, in1=t2, op=ALU.add), reads=[t1, t2], writes=[o_])

    def kv_phase():
        nonlocal arenaF_tmp16
        arenaB.reset()
        arenaF.reset()
        hkv = arenaB.take([128, KC, NT])
        wkv = arenaB.take([128, KC, 2048])
        kst = [arenaB.take([128, 512]) for _ in range(2)]
        vst = [arenaB.take([128, 16, 65]) for _ in range(2)]
        ada1 = ada_gen(1, [arenaB.take([128, KC, 384]) for _ in range(2)], 3, 6)
        cosT, sinT = rope_tiles()
        arenaF_tmp16 = arenaF.take([128, KC, 16])
        scr = arenaF.take([128, 3200])
        t1 = scr[:, 0:512]
        t2 = scr[:, 512:1024]
        kout = [scr[:, 1024:1536], scr[:, 1536:2048]]
        vout = [scr[:, 1024:2048], scr[:, 2048:3072]]
        P.add("sp", lambda e: e.dma_start(out=cosT, in_=ropet[0]), reads=[ropet[0]], writes=[cosT], dma=True)
        P.add("sp", lambda e: e.dma_start(out=sinT, in_=ropet[1]), reads=[ropet[1]], writes=[sinT], dma=True)
        for q_ in range(4):
            wd_, ws_ = wkv[:, :, q_ * 512:(q_ + 1) * 512], w_kv_d[:, :, q_ * 512:(q_ + 1) * 512]
            P.add("pool", lambda e, wd_=wd_, ws_=ws_: e.dma_start(out=wd_, in_=ws_), reads=[ws_], writes=[wd_], dma=True)
        for v_ in vst:
            P.add("dve", lambda e, v_=v_: e.memset(v_, 1.0), writes=[v_])
        vt_tiles = [(i * 128, 128) for i in range(16)] + [(2048, 16)]

        def v_tile(vi):
            lo, nt = vt_tiles[vi]
            vp = mk(psF[0:nt, 2, :], [psF[0:nt, 2, :].ap[0], (1, 1024)])
            for hf in range(2):
                o_ = psF[0:nt, 2 + hf, :]
                for k in range(KC):
                    lt = hkv[:, k, lo:lo + nt]
                    rh = wkv[:, k, 1024 + hf * 512:1024 + (hf + 1) * 512]
                    P.add("pe", lambda e, o_=o_, lt=lt, rh=rh, k=k: e.matmul(o_, lhsT=lt, rhs=rh, start=(k == 0), stop=(k == KC - 1)), reads=[lt, rh], writes=[o_])
            vo = vout1[0:nt, :]
            vb = vst[vi % 2][0:nt, :, 0:64]
            vpv = mk(vp, [vp.ap[0], (64, 16), (1, 64)])
            pbx = [psF[0:nt, 2:4, :]]
            P.add("act", lambda e, vo=vo, vp=vp: e.activation(out=vo, in_=vp, func=AF.Identity), reads=pbx, writes=[vo])
            P.add("dve", lambda e, vb=vb, vpv=vpv: e.tensor_copy(out=vb, in_=vpv), reads=pbx, writes=[vb])
            dst = v_tm[lo:lo + nt, :]
            P.add("sp", lambda e, dst=dst, vo=vo: e.dma_start(out=dst, in_=vo), reads=[vo], writes=[dst], dma=True)
            vfull = mk(vst[vi % 2][0:nt], [vst[vi % 2][0:nt].ap[0], (1, 1040)])
            dst2 = vscr[lo:lo + nt, :]
            P.add("sp", lambda e, dst2=dst2, vfull=vfull: e.dma_start(out=dst2, in_=vfull), reads=[vfull], writes=[dst2], dma=True)


        vout1 = scr[:, 2048:3072]
        vdone = [0]
        cnt = 0
        xbs = [arenaB.take([128, 512]) for _ in range(2)]
        prev = None

        def finish(u):
            (kp, lo, n, c, cnt_) = u
            ko = kout[cnt_ % 2][:, 0:n]
            kb = kst[cnt_ % 2][:, 0:n] if lo < T else ks_all[:, c, :]
            rope_b(kp, lo, n, cosT, sinT, t1[:, 0:n], t2[:, 0:n], [ko], xbs[cnt_ % 2][:, 0:n], psF[:, 4 + cnt_ % 2, 0:n])
            P.add("act", lambda e: e.activation(out=kb, in_=ko, func=AF.Identity), reads=[ko], writes=[kb])
            dst = k_fm[:, c, lo:lo + n]
            P.add("sp", lambda e: e.dma_start(out=dst, in_=ko), reads=[ko], writes=[dst], dma=True)
            dst2 = kscr[c, :, lo:lo + n]
            P.add("sp", lambda e: e.dma_start(out=dst2, in_=kb), reads=[kb], writes=[dst2], dma=True)

        modulate(hkv, 0, all_tiles5[0:1], modkv, 0, 8)
        for t5, (lo, n) in enumerate(all_tiles5):
            if t5 + 1 < len(all_tiles5):
                modulate(hkv, 0, all_tiles5[t5 + 1:t5 + 2], modkv, 0, 8)
            for c in range(KC):
                kp = psF[:, cnt % 2, 0:n]
                for k in range(KC):
                    lt = wkv[:, k, c * 128:(c + 1) * 128]
                    rh = hkv[:, k, lo:lo + n]
                    P.add("pe", lambda e, kp=kp, lt=lt, rh=rh, k=k: e.matmul(kp, lhsT=lt, rhs=rh, start=(k == 0), stop=(k == KC - 1)), reads=[lt, rh], writes=[kp])
                rope_a(kp, n, xbs[cnt % 2][:, 0:n])
                if prev is not None:
                    finish(prev)
                prev = (kp, lo, n, c, cnt)
                cnt += 1
                next(ada1, None)
                if cnt % 2 == 0 and vdone[0] < len(vt_tiles) and vt_tiles[vdone[0]][0] < lo + n:
                    v_tile(vdone[0])
                    vdone[0] += 1
        finish(prev)
        for _ in ada1:
            pass
        while vdone[0] < len(vt_tiles):
            v_tile(vdone[0])
            vdone[0] += 1
    def attn_phase():
        nonlocal arenaF_tmp16, lnF
        l = 1
        mod_t = mods[:, 1]
        sh_oc, sc_oc, g_oc, li = 24, 32, 40, 4
        arenaB.reset()
        arenaF.reset()
        oT = arenaB.take([128, KC, NT])
        B0 = arenaB.off
        wq_ = arenaB.take([128, KC, 384])
        ht = arenaB.take([128, KC, 512])
        q3b = [arenaB.take([128, 3, T]) for _ in range(2)]
        kt_ = arenaB.take([128, T])
        vpm = [arenaB.take([128, 3, 16, 130]) for _ in range(2)]
        Pt = [arenaB.take([128, 256]) for _ in range(4)]
        xbq = [arenaB.take([128, 512]) for _ in range(2)]
        assert arenaB.off <= KS_OFF, arenaB.off
        cosT, sinT = rope_tiles()
        P.add("sp", lambda e: e.dma_start(out=cosT, in_=ropet[0]), reads=[ropet[0]], writes=[cosT], dma=True)
        P.add("sp", lambda e: e.dma_start(out=sinT, in_=ropet[1]), reads=[ropet[1]], writes=[sinT], dma=True)
        arenaF_tmp16 = arenaF.take([128, KC, 16])
        F0 = arenaF.off
        t1 = arenaF.take([128, 512])
        t2 = arenaF.take([128, 512])
        oacc = arenaF.take([65, 2048])
        rdn = arenaF.take([64, 256])
        DILS = (1, 4, 16)

        def gather_v(c, vb):
            base = vscr[0:T, c * 130:(c + 1) * 130]
            off0 = int(base.offset)
            s0 = bass.AP(vscr.tensor, off0, [[1040, 128], [128 * 1040, 16], [1, 130]])
            d0 = vb[:, 0, :, :]
            P.add("sp", lambda e: e.dma_start(out=d0, in_=s0), reads=[vscr[0:T, :]], writes=[d0], dma=True)
            for r in range(4):
                s1 = bass.AP(vscr.tensor, off0 + r * 1040, [[4 * 1040, 128], [512 * 1040, 4], [1, 130]])
                d1 = vb[:, 1, r * 4:(r + 1) * 4, :]
                P.add("sp", lambda e, s1=s1, d1=d1: e.dma_start(out=d1, in_=s1), reads=[vscr[0:T, :]], writes=[d1], dma=True)
            s2 = bass.AP(vscr.tensor, off0, [[16 * 1040, 128], [1040, 16], [1, 130]])
            d2 = vb[:, 2, :, :]
            P.add("sp", lambda e: e.dma_start(out=d2, in_=s2), reads=[vscr[0:T, :]], writes=[d2], dma=True)

        def blocks_of(g):
            d = DILS[g]
            nbk = 16 // d
            return [(r, i, nbk) for r in range(d) for i in range(nbk)]

        def cols(ap2d, g, r, i, nblk):
            d = DILS[g]
            st = i * 128 * d + r
            return mk(ap2d[:, st:st + 1], [ap2d[:, st:st + 1].ap[0], (d, 128 * nblk)])

        cnts = {"s": 0, "p": 0, "q": 0, "tick": 0, "inhead": 0, "mid": 0}
        pend = {"norm": None}

        def qproj_gen(c, q3):
            src = w_q_d[c]
            P.add("pool", lambda e, src=src: e.dma_start(out=wq_, in_=src), reads=[src], writes=[wq_], dma=True)
            for (lo, n) in all_tiles5:
                modulate(ht, lo, [(lo, n)], mod_t, sh_oc, sc_oc)
                for g in range(3):
                    qp = psF[:, 5, 0:n]
                    cnts["q"] += 1
                    for k in range(KC):
                        lt = wq_[:, k, g * 128:(g + 1) * 128]
                        rh = ht[:, k, 0:n]
                        P.add("pe", lambda e, qp=qp, lt=lt, rh=rh, k=k: e.matmul(qp, lhsT=lt, rhs=rh, start=(k == 0), stop=(k == KC - 1)), reads=[lt, rh], writes=[qp])
                    if lo >= T:
                        qo = q3s_all[:, c, g, :]
                    elif g == 0:
                        qo = q3[:, g, lo:lo + n]
                    else:
                        d_ = DILS[g]
                        b_ = q3[:, g, lo // d_:lo // d_ + 1]
                        qo = (mk(b_, [b_.ap[0], (T // d_, d_), (1, n // d_)]), d_)
                    xb = xbq[cnts["q"] % 2][:, 0:n]
                    rope_a(qp, n, xb)
                    cnts["mid"] = 1
                    yield 1
                    rope_b(qp, lo, n, cosT, sinT, t1[:, 0:n], t2[:, 0:n], [qo], xb, psF[:, 6, 0:n])
                    cnts["mid"] = 0
                    yield 1

        def attn_head(c, hh, vb, q3, tick):
            pb0 = hh * 64
            for g in range(3):
                d = DILS[g]
                nbk = 16 // d
                kq = q3[pb0:pb0 + 64, g, :]
                kk = kt_[pb0:pb0 + 64, :]
                blocks = [(r, i) for r in range(d) for i in range(nbk)]
                Pof = {}

                def emit_S(b):
                    r, i = blocks[b]
                    nq = 2 if i < nbk - 1 else 1
                    sp_ = psF[:, cnts["s"] % 3, 0:128 * nq]
                    Pb = Pt[cnts["s"] % 4][:, 0:128 * nq]
                    cnts["s"] += 1
                    ma = maskp_b[:, 0:128 * nq]
                    P.add("pe", lambda e: e.matmul(sp_, lhsT=ident_b, rhs=ma, start=True, stop=False), reads=[ident_b, ma], writes=[sp_])
                    lt = cols(kk, g, r, i, 1)
                    if g == 0:
                        rh = cols(kq, g, r, i, nq)
                    else:
                        st_ = r * (T // d) + i * 128
                        rh = kq[:, st_:st_ + 128 * nq]
                    P.add("pe", lambda e: e.matmul(sp_, lhsT=lt, rhs=rh, start=False, stop=True), reads=[kk, kq], writes=[sp_])
                    P.add("act", lambda e: e.activation(out=Pb, in_=sp_, func=AF.Exp, scale=0.125), reads=[sp_], writes=[Pb])
                    Pof[b] = Pb

                def emit_V(b, po):
                    r, i = blocks[b]
                    tix = (i if g == 0 else (r * 4 + i if g == 1 else r))
                    first = (i == 0)
                    if not first:
                        l0 = vb[:, g, tix - 1, hh * 65:(hh + 1) * 65]
                        r0 = Pof[b - 1][:, 128:256]
                        P.add("pe", lambda e: e.matmul(po, lhsT=l0, rhs=r0, start=True, stop=False), reads=[l0, r0], writes=[po])
                    l1 = vb[:, g, tix, hh * 65:(hh + 1) * 65]
                    r1 = Pof[b][:, 0:128]
                    P.add("pe", lambda e: e.matmul(po, lhsT=l1, rhs=r1, start=first, stop=True), reads=[l1, r1], writes=[po])

                emit_S(0)
                emit_S(1)
                for b0 in range(0, 16, 4):
                    po_bank = psF[0:65, 3 + cnts["p"] % 2, :]
                    cnts["p"] += 1
                    for j in range(4):
                        b = b0 + j
                        if b + 2 < 16:
                            emit_S(b + 2)
                        emit_V(b, po_bank[:, j * 128:(j + 1) * 128])
                        tick()
                    rb, ib = blocks[b0]
                    if g == 0:
                        dst = oacc[:, ib * 128:(ib + 4) * 128]
                        P.add("dve", lambda e, dst=dst, pbk=po_bank: e.tensor_copy(out=dst, in_=pbk), reads=[po_bank], writes=[dst])
                    elif g == 1:
                        dst = mk(oacc[:, rb:rb + 1], [oacc[:, rb:rb + 1].ap[0], (4, 512)])
                        P.add("dve", lambda e, dst=dst, pbk=po_bank: e.tensor_tensor(out=dst, in0=dst, in1=pbk, op=ALU.add), reads=[po_bank, oacc[:, :]], writes=[oacc[:, :]])
                    else:
                        dst = mk(oacc[:, rb:rb + 1], [oacc[:, rb:rb + 1].ap[0], (1, 4), (16, 128)])
                        src3 = mk(po_bank, [po_bank.ap[0], (128, 4), (1, 128)])
                        P.add("dve", lambda e, dst=dst, src3=src3: e.tensor_tensor(out=dst, in0=dst, in1=src3, op=ALU.add), reads=[po_bank, oacc[:, :]], writes=[oacc[:, :]])
            def norm():
                drow = oacc[64:65, :]
                P.add("act", lambda e: e.activation(out=drow, in_=drow, func=AF.Ln), reads=[drow], writes=[drow])
                P.add("act", lambda e: e.activation(out=drow, in_=drow, func=AF.Exp, scale=-1.0), reads=[drow], writes=[drow])
                for tix in range(4):
                    cs_ = slice(tix * 512, (tix + 1) * 512)
                    dps = psF[0:64, 5 + cnts["q"] % 2, :]
                    cnts["q"] += 1
                    rh = oacc[64:65, cs_]
                    lt = sel_f[64:65, :]
                    P.add("pe", lambda e, dps=dps, rh=rh, lt=lt: e.matmul(dps, lhsT=lt, rhs=rh, start=True, stop=True), reads=[lt, rh], writes=[dps])
                    num = oacc[0:64, cs_]
                    dst = oT[pb0:pb0 + 64, c, cs_]
                    P.add("dve", lambda e, dst=dst, num=num, dps=dps: e.tensor_tensor(out=dst, in0=num, in1=dps, op=ALU.mult), reads=[num, dps], writes=[dst])
            return norm

        for _ in qproj_gen(0, q3b[0]):
            pass
        for c in range(KC):
            srck = kscr[c][:, 0:T]
            P.add("sp", lambda e, srck=srck: e.dma_start(out=kt_, in_=srck), reads=[kscr[c]], writes=[kt_], dma=True)
            vb = vpm[c % 2]
            if c == 0:
                gather_v(0, vb)
            if c + 1 < KC:
                gather_v(c + 1, vpm[(c + 1) % 2])
            gen = qproj_gen(c + 1, q3b[(c + 1) % 2]) if c + 1 < KC else iter(())

            def tick():
                cnts["tick"] += 1
                cnts["inhead"] += 1
                if cnts["inhead"] == 3 and pend["norm"] is not None:
                    if cnts["mid"]:
                        next(gen, None)
                    pend["norm"]()
                    pend["norm"] = None
                if cnts["tick"] % TICKN == 0:
                    next(gen, None)
            for hh in range(2):
                cnts["inhead"] = 0
                nf = attn_head(c, hh, vb, q3b[c % 2], tick)
                assert pend["norm"] is None
                pend["norm"] = nf
            for _ in gen:
                pass
        assert not cnts["mid"]
        pend["norm"]()
        arenaB.reset(B0)
        kctx = arenaB.take([128, KC, 1024])
        vctx = arenaB.take([128, 8, 1040])
        stb = [arenaB.take([128, 1024]) for _ in range(4)]
        Pgs = [arenaB.take([128, 8, 4]) for _ in range(3)]
        wob = arenaB.take([128, KC, 1024])
        zbq = [arenaB.take([128, 512]) for _ in range(6)]
        assert arenaB.off <= KS_OFF
        arenaF.reset(0)
        stf = [arenaF.take([128, 1024]) for _ in range(4)]
        arenaF.reset(F0)
        Pfs = [arenaF.take([128, 96]) for _ in range(2)]
        oas = arenaF.take([65, 16, 16])
        rds = arenaF.take([64, 256])
        lnF = ln_bufs(1, False)
        lnF = {k_: v_ + v_ for k_, v_ in lnF.items()}
        P.add("pool", lambda e: e.dma_start(out=wob, in_=w_ob_d), reads=[w_ob_d], writes=[wob], dma=True)
        P.add("dve", lambda e: e.memset(kctx[:, :, 896:1024], 0.0), writes=[kctx[:, :, 896:1024]])
        P.add("dve", lambda e: e.memset(vctx[:, 7, :], 0.0), writes=[vctx[:, 7, :]])
        P.add("dve", lambda e: e.memset(vctx[:, 0:7, :], 1.0), writes=[vctx[:, 0:7, :]])
        zst = {"zc": 0}

        def out_proj(lo, n):
            st = ln_begin([(lo, n)])
            for c in range(KC):
                Y = psF[:, 2 + c % 2, 0:n]
                for k in range(KC):
                    lt = wob[:, k, c * 128:(c + 1) * 128]
                    rh = oT[:, k, lo:lo + n]
                    P.add("pe", lambda e, Y=Y, lt=lt, rh=rh, k=k: e.matmul(Y, lhsT=lt, rhs=rh, start=(k == 0), stop=(k == KC - 1)), reads=[lt, rh], writes=[Y])
                ln_flush(st, keep=0)
                residual_add(Y, c, lo, n, mod_t, g_oc)
                zc = zst["zc"]
                ln_accum(st, c, lo, n, zbq[(zc % 3) * 2], zbq[(zc % 3) * 2 + 1])
                zst["zc"] += 1
            ln_finish(st, li)

        sc = 0
        hcnt = 0
        for s_ in range(4):
            for tile in range(7):
                for (cache, isk) in ((cache_k, True), (cache_v, False)):
                    sf = stf[sc % 4]
                    sbb = stb[sc % 4]
                    sc += 1
                    if tile < 3:
                        for t_ in range(4):
                            src = bass.AP(cache.tensor, int(cache[s_].offset) + (16 * 32 * tile + t_) * 1024, [[16 * 1024, 32], [1, 1024]])
                            dstp = sf[t_ * 32:(t_ + 1) * 32, :]
                            P.add("sp", lambda e, src=src, dstp=dstp: e.dma_start(out=dstp, in_=src), reads=[cache[s_]], writes=[dstp], dma=True)
                    else:
                        r0_ = 1536 + (tile - 3) * 128
                        src = cache[s_, r0_:r0_ + 128, :]
                        P.add("sp", lambda e, src=src, sf=sf: e.dma_start(out=sf, in_=src), reads=[src], writes=[sf], dma=True)
                    if isk:
                        P.add("act", lambda e, sf=sf, sbb=sbb: e.activation(out=sbb, in_=sf, func=AF.Identity), reads=[sf], writes=[sbb])
                        for c in range(KC):
                            o_ = psB[:, c * 128:(c + 1) * 128]
                            i_ = sbb[:, c * 128:(c + 1) * 128]
                            P.add("pe", lambda e, o_=o_, i_=i_: e.transpose(o_, i_, ident_b), reads=[i_, ident_b], writes=[o_])
                        pbv = mk(psB[:, :], [psB[:, :].ap[0], (128, KC), (1, 128)])
                        dst = kctx[:, :, tile * 128:(tile + 1) * 128]
                        P.add("dve", lambda e, dst=dst, pbv=pbv: e.tensor_copy(out=dst, in_=pbv), reads=[psB[:, :]], writes=[dst])
                    else:
                        dst = mk(vctx[:, tile, 0:1], [vctx[:, tile, 0:1].ap[0], (65, 16), (1, 64)])
                        sfv = mk(sf, [sf.ap[0], (64, 16), (1, 64)])
                        P.add("dve", lambda e, dst=dst, sfv=sfv: e.tensor_copy(out=dst, in_=sfv), reads=[sf], writes=[vctx[:, tile, :]])
            ksn = ks_all[:, :, 4 * s_:4 * s_ + 4]
            kd = kctx[:, :, 896:900]
            P.add("act", lambda e, kd=kd, ksn=ksn: e.activation(out=kd, in_=ksn, func=AF.Identity), reads=[ksn], writes=[kd])
            vsrc = vscr[T + 4 * s_:T + 4 * s_ + 4, :]
            vd = vctx[0:4, 7, :]
            P.add("sp", lambda e, vd=vd, vsrc=vsrc: e.dma_start(out=vd, in_=vsrc), reads=[vsrc], writes=[vd], dma=True)

            def head_S(h):
                c, pb0 = h // 2, (h % 2) * 64
                sps = psF[:, 4 + hcnt_of[h] % 2, 0:96]
                Pf = Pfs[hcnt_of[h] % 2]
                pg = Pgs[hcnt_of[h] % 3]
                P.add("pe", lambda e: e.matmul(sps, lhsT=ident_b, rhs=masks_b, start=True, stop=False), reads=[ident_b, masks_b], writes=[sps])
                for tile in range(8):
                    for g in range(3):
                        o_ = sps[:, tile * 12 + g * 4:tile * 12 + g * 4 + 4]
                        lt = kctx[pb0:pb0 + 64, c, tile * 128:(tile + 1) * 128]
                        rh = q3s_all[pb0:pb0 + 64, c, g, 4 * s_:4 * s_ + 4]
                        last = (tile == 7 and g == 2)
                        P.add("pe", lambda e, o_=o_, lt=lt, rh=rh, last=last: e.matmul(o_, lhsT=lt, rhs=rh, start=False, stop=last), reads=[lt, rh], writes=[o_])
                P.add("act", lambda e: e.activation(out=Pf, in_=sps, func=AF.Exp, scale=0.125), reads=[sps], writes=[Pf])
                pfv = mk(Pf, [Pf.ap[0], (12, 8), (1, 4), (4, 3)])

                def _red(e):
                    with nc.allow_low_precision(reason="3-term fp32 sum rounded once to the bf16 matmul operand"):
                        return e.tensor_reduce(out=pg, in_=pfv, axis=AX.X, op=ALU.add)
                P.add("dve", _red, reads=[Pf], writes=[pg])
                return pg

            def head_V(h, pg):
                po = psF[0:65, 6, (h % 8) * 4:(h % 8) * 4 + 4]
                for tile in range(8):
                    lt = vctx[:, tile, h * 65:(h + 1) * 65]
                    rh = pg[:, tile, :]
                    P.add("pe", lambda e, lt=lt, rh=rh, tile=tile: e.matmul(po, lhsT=lt, rhs=rh, start=(tile == 0), stop=(tile == 7)), reads=[lt, rh], writes=[po])
                if h % 8 == 7:
                    srcv = mk(psF[0:65, 6, 0:32], [psF[0:65, 6, 0:32].ap[0], (4, 8), (1, 4)])
                    dst = oas[:, (h // 8) * 8:(h // 8) * 8 + 8, 4 * s_:4 * s_ + 4]
                    P.add("dve", lambda e: e.tensor_copy(out=dst, in_=srcv), reads=[psF[0:65, 6, 0:32]], writes=[dst])

            hcnt_of = {h: hcnt + h for h in range(16)}
            hcnt += 16
            pgs_ = {0: head_S(0)}
            for h in range(16):
                if h + 1 < 16:
                    pgs_[h + 1] = head_S(h + 1)
                head_V(h, pgs_[h])
            out_proj(*all_tiles5[s_])
        dps = psF[0:64, 6, 0:256]
        oasf = mk(oas, [oas.ap[0], (1, 256)])
        P.add("pe", lambda e: e.matmul(dps, lhsT=sel_f, rhs=oasf, start=True, stop=True), reads=[sel_f, oas], writes=[dps])
        P.add("dve", lambda e: e.reciprocal(out=rds, in_=dps), reads=[dps], writes=[rds])
        for hh in range(2):
            num = mk(oas[0:64, hh:hh + 1, :], [oas[0:64, hh:hh + 1, :].ap[0], (32, 8), (1, 16)])
            rdv = mk(rds[:, hh * 16:hh * 16 + 1], [rds[:, hh * 16:hh * 16 + 1].ap[0], (32, 8), (1, 16)])
            dst = oT[hh * 64:(hh + 1) * 64, :, T:NT]
            P.add("dve", lambda e, dst=dst, num=num, rdv=rdv: e.tensor_tensor(out=dst, in0=num, in1=rdv, op=ALU.mult), reads=[oas, rds], writes=[dst])
        out_proj(*all_tiles5[4])

    arenaF_tmp16 = None
    lnF = None

    stages = [lambda: ffn(0, 0, ada_bg=True), gla_phase, lambda: ffn(0, 1), kv_phase, lambda: ffn(1, 0), attn_phase, lambda: ffn(1, 1)]
    for si, fn_ in enumerate(stages):
        if si < upto:
            fn_()

    P.add("sp", lambda e: e.dma_start(out=y_fm, in_=x), reads=[x], writes=[y_fm], dma=True)

    P.finalize_and_emit(es)
    es.close()
    return nc


def _prep_inputs(inp):
    f = np.float32
    shared = {}
    wa = np.asarray(inp["w_ada"], f)
    shared["w_ada"] = np.ascontiguousarray(wa.reshape(2, KC, 128, 8, 1152).transpose(0, 3, 2, 1, 4))
    shared["b_ada"] = np.ascontiguousarray(np.asarray(inp["b_ada"], f).reshape(2, 72, 128).transpose(2, 0, 1))
    wk = np.asarray(inp["w_ada_kv"], f)
    shared["w_adakv"] = np.ascontiguousarray(wk.reshape(KC, 128, 2, 1024).transpose(2, 1, 0, 3))
    shared["b_adakv"] = np.ascontiguousarray(np.asarray(inp["b_ada_kv"], f).reshape(16, 128).T)
    g = np.asarray(inp["ln_g"], f).reshape(6, KC, 128)
    b = np.asarray(inp["ln_b"], f).reshape(6, KC, 128)
    shared["lnp"] = np.ascontiguousarray(np.stack([g, b], 0).transpose(3, 0, 1, 2))
    ups, dns = [], []
    for l in range(2):
        for nm in ("w_ffn1", "w_ffn2"):
            wu = np.asarray(inp[nm + "_up"][l], f)
            wu = wu.reshape(KC, 128, 2, NJ, 128).transpose(3, 1, 2, 0, 4)
            ups.append(wu)
            wd = np.asarray(inp[nm + "_down"][l], f)
            wd = wd.reshape(NJ, 128, KC, 128).transpose(2, 1, 0, 3)
            dns.append(wd)
    shared["w_up"] = np.ascontiguousarray(np.stack(ups, 0))
    shared["w_dn"] = np.ascontiguousarray(np.stack(dns, 0))
    wi = np.asarray(inp["w_in_a"][0], f).reshape(KC, 128, GLA_IN).transpose(1, 0, 2)
    shared["w_in_qk"] = np.ascontiguousarray(wi[:, :, 0:1024])
    shared["w_in_v"] = np.ascontiguousarray(wi[:, :, 1024:2048])
    shared["w_in_glr"] = np.ascontiguousarray(wi[:, :, 2048:2064])
    shared["w_in_r"] = np.ascontiguousarray(wi[:, :, 2064:3088])
    shared["w_outa"] = np.ascontiguousarray(np.asarray(inp["w_out_a"][0], f).reshape(KC, 128, D).transpose(1, 0, 2))
    shared["wg2"] = np.ascontiguousarray(np.asarray(inp["w_gate2_a"][0], f))
    shared["bgate"] = np.ascontiguousarray(np.asarray(inp["b_gate_a"][0], f).reshape(4, 128).T)
    shared["gonorm"] = np.ascontiguousarray(np.broadcast_to(np.asarray(inp["g_onorm_a"][0], f)[None, :], (128, 256)))
    shared["w_kv_d"] = np.ascontiguousarray(np.asarray(inp["w_kv"], f).reshape(KC, 128, 2048).transpose(1, 0, 2))
    wq = np.asarray(inp["w_q_b"][0], f).reshape(KC, 128, 3, KC, 128)
    shared["w_q_d"] = np.ascontiguousarray(wq.transpose(3, 1, 0, 2, 4).reshape(KC, 128, KC, 384))
    shared["w_ob_d"] = np.ascontiguousarray(np.asarray(inp["w_out_b"][0], f).reshape(KC, 128, D).transpose(1, 0, 2))
    cf = np.zeros((128, 720), f)
    cf[:, 0:128] = np.triu(np.ones((128, 128), f))
    cf[:, 128:640] = 1.0
    cf[:, 128:640:128] = 0.0
    cf[:, 640:656] = 1.0
    cf[:, 640:656:4] = 0.0
    cf[64, 656:720] = 1.0
    shared["cstf_d"] = cf
    cb = np.zeros((128, 736), f)
    cb[:, 0:128] = 1.0 / 1024.0
    cb[:, 128:256] = np.eye(128, dtype=f)
    ki = np.arange(128)[:, None]
    qi = np.arange(128)[None, :]
    cb[:, 256:384] = np.where(ki <= qi, 0.0, NEG)
    cb[:, 384:512] = np.where(ki >= qi, 0.0, NEG)
    for p_ in range(128):
        cb[p_, 608 + (p_ ^ 32)] = 1.0
    pidx = np.arange(128)
    for tile in range(8):
        if tile < 3:
            rho = 16 * (32 * tile + (pidx % 32)) + pidx // 32
        elif tile < 7:
            rho = 1536 + (tile - 3) * 128 + pidx
        else:
            rho = np.where(pidx < 4, 2048 + pidx, -10 ** 6)
        for g, dil in enumerate((1, 4, 16)):
            for t in range(4):
                dist = 2048 + t - rho
                ok = (dist >= 0) & (dist % dil == 0) & (dist // dil <= 128)
                cb[:, 512 + tile * 12 + g * 4 + t] = np.where(ok, 0.0, NEG)
    shared["cstb_d"] = cb
    pos = np.concatenate([np.arange(T), np.tile(16384 + np.arange(4), 4)]).astype(f)
    inv = np.power(np.float32(10000.0), -np.arange(32, dtype=f) / np.float32(32.0)).astype(f)
    ang = (pos[None, :] * inv[:, None]).astype(f).astype(np.float64)
    cosv = np.cos(ang).astype(f)
    sinv = np.sin(ang).astype(f)
    rt = np.zeros((2, 128, NT), f)
    for p in range(128):
        rt[0, p] = cosv[p % 32]
        rt[1, p] = -sinv[p % 32] if (p % 64) < 32 else sinv[p % 32]
    shared["ropet"] = rt
    per_core = []
    xp = np.asarray(inp["x_prompt"], f)
    xs = np.asarray(inp["x_sample"], f)
    cp = np.asarray(inp["c_prompt"], f)
    cs = np.asarray(inp["c_sample"], f)
    for c in range(8):
        xa = np.concatenate([xp[c], xs[4 * c:4 * c + 4].reshape(16, D)], 0)
        xin = np.ascontiguousarray(xa.T.reshape(KC, 128, NT).transpose(1, 0, 2))
        ca = np.concatenate([cp[c:c + 1], cs[4 * c:4 * c + 4]], 0)
        cin = np.ascontiguousarray(ca.T.reshape(KC, 128, 5).transpose(1, 0, 2))
        m = dict(shared)
        m["xin"] = xin
        m["cin"] = cin
        sg = np.asarray(inp["state_gla"], f)[0, 4 * c:4 * c + 4]
        m["state_in"] = np.ascontiguousarray(sg.transpose(0, 2, 1, 3))
        m["cache_k_d"] = np.asarray(inp["cache_k"], f)[4 * c:4 * c + 4].reshape(4, 2048, 1024)
        m["cache_v_d"] = np.asarray(inp["cache_v"], f)[4 * c:4 * c + 4].reshape(4, 2048, 1024)
        per_core.append(m)
    return per_core


_NC_CACHE = {}


def kernel(**inputs):
    if "nc" not in _NC_CACHE:
        _NC_CACHE["nc"] = build_program()
    nc = _NC_CACHE["nc"]
    in_maps = _prep_inputs(inputs)
    res = run_bass_kernel_spmd(nc, in_maps, core_ids=list(range(8)))
    outs = res.results
    y = np.stack([o["y_fm"] for o in outs], 0)
    y = y.transpose(0, 3, 2, 1).reshape(8, NT, D)
    y_prompt = np.ascontiguousarray(y[:, :T])
    y_sample = np.ascontiguousarray(y[:, T:].reshape(32, 4, D))
    stp = np.stack([o["st_p"] for o in outs], 0)
    state_p = np.ascontiguousarray(stp.transpose(0, 2, 1, 3))[None]
    sts = np.stack([o["st_s"] for o in outs], 0).reshape(32, 128, 4, 256)
    state_s = np.ascontiguousarray(sts.transpose(0, 2, 1, 3))[None]
    kf = np.stack([o["k_fm"] for o in outs], 0)
    kf = kf.transpose(0, 3, 2, 1).reshape(8, NT, 16, 64)
    k_p = np.ascontiguousarray(kf[:, :T])
    k_s = np.ascontiguousarray(kf[:, T:].reshape(32, 4, 16, 64))
    vt = np.stack([o["v_tm"] for o in outs], 0).reshape(8, NT, 16, 64)
    v_p = np.ascontiguousarray(vt[:, :T])
    v_s = np.ascontiguousarray(vt[:, T:].reshape(32, 4, 16, 64))
    return y_prompt, y_sample, state_p, state_s, k_p, v_p, k_s, v_s
```

```python
import numpy as np
from contextlib import ExitStack
import concourse.bass as bass
import concourse.mybir as mybir
from concourse.bass_utils import run_bass_kernel_spmd

F32 = mybir.dt.float32
BF16 = mybir.dt.bfloat16
AF = mybir.ActivationFunctionType
ALU = mybir.AluOpType
AX = mybir.AxisListType

D = 1024
KC = 8
T = 2048
NSMP = 16
NT = T + NSMP
DFF = 2816
NJ = 22
ALPHA = (2 * 2) ** 0.25
LN_EPS = 1e-5
EPSP = LN_EPS / (ALPHA * ALPHA)
GLA_IN = 3088
NEG = -30000.0
TICKN = 3


class _Op:
    __slots__ = ("eng", "fn", "deps", "dma", "slot", "dval", "ms", "mval")


class Prog:
    ENGS = ("pe", "act", "dve", "pool", "sp")

    def __init__(self, nc):
        self.nc = nc
        self.q = {k: [] for k in self.ENGS}
        self.rows = {}
        self.psum = {}
        self.live = {}
        self.rr = {"sp": 0, "pool": 0, "act": 0}
        self.dcnt = {}
        self.NSLOT = 6

    def reg(self, ap, rowsize):
        self.rows[ap.tensor.name] = rowsize

    def box(self, ap):
        name = ap.tensor.name
        dims = ap.ap
        off = int(ap.offset)
        if name in self.rows:
            R = self.rows[name]
            p0 = off // R
            f0 = off % R
            ps, pc = dims[0]
            assert ps % R == 0, (name, dims, R)
            p1 = p0 + (pc - 1) * (ps // R) + 1
            f1 = f0 + sum((c - 1) * abs(s) for s, c in dims[1:]) + 1
            assert f1 <= R, (name, dims, off, R)
            if name in self.psum:
                bs = self.psum[name]
                return (name, 0, 128, (f0 // bs) * bs, -(-f1 // bs) * bs)
            return (name, p0, p1, f0, f1)
        f1 = off + sum((c - 1) * abs(s) for s, c in dims) + 1
        return (name, 0, 1, off, f1)

    @staticmethod
    def _ov(a, b):
        return a[1] < b[2] and b[1] < a[2] and a[3] < b[4] and b[3] < a[4]

    @staticmethod
    def _inside(a, b):
        return a[1] >= b[1] and a[2] <= b[2] and a[3] >= b[3] and a[4] <= b[4]

    def add(self, eng, fn, reads=(), writes=(), dma=False):
        op = _Op()
        op.eng, op.fn, op.dma, op.ms, op.mval = eng, fn, dma, False, 0
        idx = len(self.q[eng])
        deps = set()
        if dma:
            slot = self.rr[eng] % self.NSLOT
            self.rr[eng] += 1
            c = self.dcnt.get((eng, slot), 0) + 1
            self.dcnt[(eng, slot)] = c
            op.slot, op.dval = slot, 16 * c
            ev = ("D", eng, slot, 16 * c)
        else:
            op.slot, op.dval = None, 0
            ev = ("E", eng, idx)
        rb = [self.box(a) for a in reads]
        wb = [self.box(a) for a in writes]
        for b in rb:
            isps = b[0] in self.psum
            for rec in self.live.get(b[0], ()):
                if (rec[1] == "w" or (isps and rec[2][1] != eng)) and self._ov(b, rec[0]):
                    e = rec[2]
                    if e[0] == "E" and e[1] == eng and not dma and eng == "pe":
                        continue
                    deps.add(e)
        for b in wb:
            for rec in self.live.get(b[0], ()):
                if self._ov(b, rec[0]):
                    e = rec[2]
                    if e[0] == "E" and e[1] == eng and not dma and eng == "pe":
                        continue
                    deps.add(e)
        deps.discard(ev)
        op.deps = deps
        for b in wb:
            lst = self.live.setdefault(b[0], [])
            lst[:] = [r for r in lst if not self._inside(r[0], b)]
            lst.append((b, "w", ev))
        for b in rb:
            lst = self.live.setdefault(b[0], [])
            if not dma:
                lst[:] = [r for r in lst if not (r[1] == "r" and r[2][0] == "E" and r[2][1] == eng
                                                  and self._inside(r[0], b))]
            lst.append((b, "r", ev))
        self.q[eng].append(op)
        return op

    def finalize_and_emit(self, es):
        nc = self.nc
        for eng in self.ENGS:
            for op in self.q[eng]:
                for d in op.deps:
                    if d[0] == "E":
                        self.q[d[1]][d[2]].ms = True
        for eng in self.ENGS:
            c = 0
            for op in self.q[eng]:
                if op.ms:
                    c += 1
                    op.mval = c
        esem = {eng: es.enter_context(nc.semaphore("s_" + eng)) for eng in self.ENGS}
        dsem = {}
        for (eng, slot) in sorted(self.dcnt):
            dsem[(eng, slot)] = es.enter_context(nc.semaphore("d_%s%d" % (eng, slot)))
        engobj = {"pe": "tensor", "act": "scalar", "dve": "vector", "pool": "gpsimd", "sp": "sync"}
        block = es.enter_context(nc.Block())
        prog = self

        def make(eng):
            def body(e):
                known = {}
                for op in prog.q[eng]:
                    waits = {}
                    for d in op.deps:
                        if d[0] == "E":
                            sem, val = esem[d[1]], prog.q[d[1]][d[2]].mval
                        else:
                            sem, val = dsem[(d[1], d[2])], d[3]
                        key = id(sem)
                        if key not in waits or waits[key][1] < val:
                            waits[key] = (sem, val)
                    if op.dma and op.dval > 16:
                        sem = dsem[(eng, op.slot)]
                        key = id(sem)
                        if key not in waits or waits[key][1] < op.dval - 16:
                            waits[key] = (sem, op.dval - 16)
                    for key, (sem, val) in waits.items():
                        if known.get(key, 0) < val:
                            e.wait_ge(sem, val)
                            known[key] = val
                    ins = op.fn(e)
                    if op.dma:
                        ins.then_inc(dsem[(eng, op.slot)], 16)
                    elif op.ms:
                        ins.then_inc(esem[eng], 1)
                if eng == "sp":
                    for (qe, slot), c in sorted(prog.dcnt.items()):
                        e.wait_ge(dsem[(qe, slot)], 16 * c)
                    for oe in prog.ENGS:
                        tot = sum(1 for o in prog.q[oe] if o.ms)
                        if tot and oe != "sp":
                            e.wait_ge(esem[oe], tot)
            return body

        for eng in self.ENGS:
            getattr(block, engobj[eng])(make(eng))


def mk(ap, dims):
    return bass.AP(ap.tensor, ap.offset, [list(d) for d in dims])


def split_last(ap, a, b):
    d = list(ap.ap)
    st, c = d[-1]
    assert c == a * b
    return mk(ap, d[:-1] + [(st * b, a), (st, b)])


def bcast_last(ap, n):
    return mk(ap, list(ap.ap) + [(0, n)])


class Arena:
    def __init__(self, ap, rowsize):
        self.ap = ap
        self.R = rowsize
        self.off = 0

    def reset(self, off=0):
        self.off = off

    def take(self, shape):
        n = 1
        for s in shape[1:]:
            n *= s
        o = self.off
        self.off += n
        assert self.off <= self.R, ("arena overflow", self.off, self.R)
        v = self.ap[0:shape[0], o:o + n]
        if len(shape) == 2:
            return v
        d = list(v.ap)[:1]
        st = n
        for s in shape[1:]:
            st //= s
            d.append((st, s))
        return mk(v, d)


def build_program(debug=False, upto=99, sub=99):
    nc = bass.Bass("TRN2", target_bir_lowering=False)
    es = ExitStack()
    P = Prog(nc)

    def din(name, shape, dt=F32):
        return nc.dram_tensor(name, list(shape), dt, kind="ExternalInput").ap()

    def dout(name, shape, dt=F32):
        return nc.dram_tensor(name, list(shape), dt, kind="ExternalOutput").ap()

    xin = din("xin", [128, KC, NT])
    cin = din("cin", [128, KC, 5])
    w_ada = din("w_ada", [2, 8, 128, KC, 1152])
    b_ada = din("b_ada", [128, 2, 72])
    w_adakv = din("w_adakv", [2, 128, KC, 1024])
    b_adakv = din("b_adakv", [128, 16])
    lnp = din("lnp", [128, 2, 6, KC])
    w_up = din("w_up", [4, NJ, 128, 2, KC, 128])
    w_dn = din("w_dn", [4, KC, 128, NJ, 128])
    cstf_d = din("cstf_d", [128, 720])
    cstb_d = din("cstb_d", [128, 736])
    ropet = din("ropet", [2, 128, NT])
    w_kv_d = din("w_kv_d", [128, KC, 2048])
    w_q_d = din("w_q_d", [KC, 128, KC, 384])
    w_ob_d = din("w_ob_d", [128, KC, 1024])
    cache_k = din("cache_k_d", [4, 2048, 1024])
    cache_v = din("cache_v_d", [4, 2048, 1024])
    w_in_qk = din("w_in_qk", [128, KC, 1024])
    w_in_glr = din("w_in_glr", [128, KC, 16])
    w_in_v = din("w_in_v", [128, KC, 1024])
    w_in_r = din("w_in_r", [128, KC, 1024])
    w_outa = din("w_outa", [128, KC, 1024])
    wg2_d = din("wg2", [16, 512])
    bgate_d = din("bgate", [128, 4])
    gonorm_d = din("gonorm", [128, 256])
    state_in = din("state_in", [4, 128, 4, 256])
    y_fm = dout("y_fm", [128, KC, NT])
    st_p = dout("st_p", [128, 4, 256])
    k_fm = dout("k_fm", [128, KC, NT])
    v_tm = dout("v_tm", [NT, 1024])
    kscr = nc.dram_tensor("kscr", [KC, 128, NT], BF16, kind="Internal").ap()
    vscr = nc.dram_tensor("vscr", [NT, 1040], BF16, kind="Internal").ap()
    st_s = dout("st_s", [4, 128, 4, 256])

    def sb(name, shape, dt):
        t = es.enter_context(nc.sbuf_tensor(name, list(shape), dt))
        a = t[:]
        n = 1
        for s in shape[1:]:
            n *= s
        P.reg(a, n)
        return a

    def ps(name, shape, dt):
        t = es.enter_context(nc.psum_tensor(name, list(shape), dt))
        a = t[:]
        n = 1
        for s in shape[1:]:
            n *= s
        P.reg(a, n)
        P.psum[a.tensor.name] = 512 if dt == F32 else 1024
        return a

    x = sb("x", [128, KC, NT], F32)
    AB_R = 53300
    AF_R = 7600
    arenaB = Arena(sb("arenaB", [128, AB_R], BF16), AB_R)
    arenaF = Arena(sb("arenaF", [128, AF_R], F32), AF_R)
    mods = sb("mods", [128, 2, 72, 5], F32)
    modkv = sb("modkv", [128, 16, 5], F32)
    lnp_sb = sb("lnp_sb", [128, 2, 6, KC], F32)
    cstf = sb("cstf", [128, 720], F32)
    cstb = sb("cstb", [128, 736], BF16)
    sct = sb("sct", [128, KC, 5], BF16)
    cin_sb = sb("cin_sb", [128, KC, 5], F32)
    bada_sb = sb("bada_sb", [128, 2, 72], F32)
    badakv_sb = sb("badakv_sb", [128, 16], F32)
    psF = ps("psF", [128, 7, 512], F32)
    psB = ps("psB", [128, 1024], BF16)

    ones_b = cstb[:, 0:128]
    ident_b = cstb[:, 128:256]
    causal_f = cstf[:, 0:128]
    scanm_p = cstf[:, 128:640]
    scanm_s = cstf[:, 640:656]
    sel_f = cstf[0:65, 656:720]
    maskp_b = cstb[:, 256:512]
    pswap_b = cstb[:, 608:736]
    masks_b = cstb[:, 512:608]

    P.add("sp", lambda e: e.dma_start(out=x, in_=xin), reads=[xin], writes=[x], dma=True)
    P.add("sp", lambda e: e.dma_start(out=cin_sb, in_=cin), reads=[cin], writes=[cin_sb], dma=True)
    P.add("sp", lambda e: e.dma_start(out=lnp_sb, in_=lnp), reads=[lnp], writes=[lnp_sb], dma=True)
    P.add("sp", lambda e: e.dma_start(out=cstf, in_=cstf_d), reads=[cstf_d], writes=[cstf], dma=True)
    P.add("sp", lambda e: e.dma_start(out=bada_sb, in_=b_ada), reads=[b_ada], writes=[bada_sb], dma=True)
    P.add("sp", lambda e: e.dma_start(out=badakv_sb, in_=b_adakv), reads=[b_adakv], writes=[badakv_sb], dma=True)
    P.add("pool", lambda e: e.dma_start(out=cstb, in_=cstb_d), reads=[cstb_d], writes=[cstb], dma=True)
    P.add("act", lambda e: e.activation(out=sct, in_=cin_sb, func=AF.Silu), reads=[cin_sb], writes=[sct])

    DERIVE_ALL = ((1, None), (4, None), (7, None), (2, 0.5 / ALPHA), (5, 1.0 / ALPHA), (8, 0.5 / ALPHA))

    def derive(mt, entries=DERIVE_ALL):
        for i, coef in entries:
            v = mt[:, i * 8:(i + 1) * 8, :]
            if coef is None:
                P.add("dve", lambda e, v=v: e.tensor_scalar_add(out=v, in0=v, scalar1=1.0), reads=[v], writes=[v])
            else:
                P.add("dve", lambda e, v=v, coef=coef: e.tensor_scalar(out=v, in0=v, scalar1=1.0, scalar2=coef,
                                                                      op0=ALU.add, op1=ALU.mult), reads=[v], writes=[v])

    def ada_gen(l, bufs, noc, bank, pcs=range(8), entries=DERIVE_ALL):
        nb_ = 0
        for pc in pcs:
            for sp_ in range(9 // noc):
                wb_ = bufs[nb_ % 2]
                nb_ += 1
                src = w_ada[l, pc][:, :, sp_ * noc * 128:(sp_ + 1) * noc * 128]
                P.add("pool", lambda e, wb_=wb_, src=src: e.dma_start(out=wb_, in_=src),
                      reads=[src], writes=[wb_], dma=True)
                for ol in range(noc):
                    oc = pc * 9 + sp_ * noc + ol
                    for k in range(KC):
                        o = psF[:, bank, oc * 5:oc * 5 + 5]
                        lt = wb_[:, k, ol * 128:(ol + 1) * 128]
                        rh = sct[:, k, :]
                        P.add("pe", lambda e, o=o, lt=lt, rh=rh, k=k: e.matmul(o, lhsT=lt, rhs=rh, start=(k == 0), stop=(k == KC - 1)),
                              reads=[lt, rh], writes=[o])
                yield 1
        o0, o1 = min(pcs) * 9, (max(pcs) + 1) * 9
        pv = mk(psF[:, bank, o0 * 5:o1 * 5], [psF[:, bank, o0 * 5:o1 * 5].ap[0], (5, o1 - o0), (1, 5)])
        bb = bcast_last(bada_sb[:, l, o0:o1], 5)
        mo = mods[:, l, o0:o1, :]
        P.add("dve", lambda e, mo=mo, pv=pv, bb=bb: e.tensor_tensor(out=mo, in0=pv, in1=bb, op=ALU.add),
              reads=[pv, bada_sb[:, l, o0:o1]], writes=[mo])
        derive(mods[:, l], entries)
        yield 0

    arenaB.reset()
    wada_buf = [arenaB.take([128, KC, 1152]) for _ in range(2)]
    for _ in ada_gen(0, wada_buf, 9, 0, pcs=range(0, 3), entries=DERIVE_ALL[0:1] + DERIVE_ALL[3:4]):
        pass
    nb = 0
    for pc in range(2):
        wb_ = wada_buf[nb % 2][:, :, 0:1024]
        nb += 1
        src = w_adakv[pc]
        P.add("pool", lambda e, wb_=wb_, src=src: e.dma_start(out=wb_, in_=src), reads=[src], writes=[wb_], dma=True)
        for ol in range(8):
            oc = pc * 8 + ol
            for k in range(KC):
                o = psF[:, 2, oc * 5:oc * 5 + 5]
                lt = wb_[:, k, ol * 128:(ol + 1) * 128]
                rh = sct[:, k, :]
                P.add("pe", lambda e, o=o, lt=lt, rh=rh, k=k: e.matmul(o, lhsT=lt, rhs=rh, start=(k == 0), stop=(k == KC - 1)),
                      reads=[lt, rh], writes=[o])
    pv = mk(psF[:, 2, 0:80], [psF[:, 2, 0:80].ap[0], (5, 16), (1, 5)])
    bb = bcast_last(badakv_sb, 5)
    P.add("dve", lambda e, pv=pv, bb=bb: e.tensor_tensor(out=modkv, in0=pv, in1=bb, op=ALU.add),
          reads=[pv, badakv_sb], writes=[modkv])
    v = modkv[:, 8:16, :]
    P.add("dve", lambda e, v=v: e.tensor_scalar_add(out=v, in0=v, scalar1=1.0), reads=[v], writes=[v])

    halves = [[(0, 512), (512, 512)], [(1024, 512), (1536, 512), (2048, 16)]]

    def modulate(dst, h0, cols, mod_t, sh_oc, sc_oc):
        for (lo, n) in cols:
            if lo < T:
                for k in range(KC):
                    o = dst[:, k, lo - h0:lo - h0 + n]
                    i_ = x[:, k, lo:lo + n]
                    b_ = mod_t[:, sh_oc + k, 0:1]
                    s_ = mod_t[:, sc_oc + k, 0:1]
                    P.add("act", lambda e, o=o, i_=i_, b_=b_, s_=s_: e.activation(out=o, in_=i_, func=AF.Identity, bias=b_, scale=s_),
                          reads=[i_, b_, s_], writes=[o])
            else:
                o = dst[:, :, lo - h0:lo - h0 + n]
                i_ = x[:, :, lo:lo + n]
                tmp = arenaF_tmp16
                s_ = mod_t[:, sc_oc:sc_oc + 8, 1:5]
                b_ = mod_t[:, sh_oc:sh_oc + 8, 1:5]
                P.add("dve", lambda e, i_=i_, s_=s_, tmp=tmp: e.tensor_tensor(out=split_last(tmp, 4, 4), in0=split_last(i_, 4, 4),
                                                                          in1=bcast_last(s_, 4), op=ALU.mult),
                      reads=[i_, s_], writes=[tmp])
                P.add("dve", lambda e, o=o, b_=b_, tmp=tmp: e.tensor_tensor(out=split_last(o, 4, 4), in0=split_last(tmp, 4, 4),
                                                                        in1=bcast_last(b_, 4), op=ALU.add),
                      reads=[tmp, b_], writes=[o])

    def residual_add(Y, c, lo, n, mod_t, g_oc):
        xs = x[:, c, lo:lo + n]
        if lo < T:
            g_ = mod_t[:, g_oc + c, 0:1]
            P.add("dve", lambda e, xs=xs, Y=Y, g_=g_: e.scalar_tensor_tensor(out=xs, in0=Y, scalar=g_, in1=xs, op0=ALU.mult, op1=ALU.add),
                  reads=[Y, g_, xs], writes=[xs])
        else:
            g_ = mod_t[:, g_oc + c, 1:5]
            tmp = arenaF_tmp16[:, 0, :]
            P.add("dve", lambda e, Y=Y, g_=g_, tmp=tmp: e.tensor_tensor(out=split_last(tmp, 4, 4), in0=split_last(Y, 4, 4),
                                                                    in1=bcast_last(g_, 4), op=ALU.mult),
                  reads=[Y, g_], writes=[tmp])
            P.add("dve", lambda e, xs=xs, tmp=tmp: e.tensor_tensor(out=xs, in0=xs, in1=tmp, op=ALU.add),
                  reads=[xs, tmp], writes=[xs])

    class LNState:
        pass

    def ln_begin(tiles):
        st = LNState()
        st.tiles = tiles
        st.mu = {}
        st.e2 = {}
        pi = 0
        for (lo, n) in tiles:
            if lo < T:
                st.mu[lo] = psF[:, 2 * pi, 0:n]
                st.e2[lo] = psF[:, 2 * pi + 1, 0:n]
                pi += 1
            else:
                st.mu[lo] = psF[:, 6, 0:n]
                st.e2[lo] = psF[:, 6, n:2 * n]
        st.pending = []
        return st

    def ln_accum(st, c, lo, n, zb, zq):
        xs = x[:, c, lo:lo + n]
        zb_ = zb[:, 0:n]
        zq_ = zq[:, 0:n] if lo < T else zb[:, n:2 * n]
        P.add("act", lambda e, zb_=zb_, xs=xs: e.activation(out=zb_, in_=xs, func=AF.Identity), reads=[xs], writes=[zb_])
        P.add("act", lambda e, zq_=zq_, xs=xs: e.activation(out=zq_, in_=xs, func=AF.Square), reads=[xs], writes=[zq_])
        mu, e2 = st.mu[lo], st.e2[lo]

        def later():
            if lo >= T:
                both = zb[:, 0:2 * n]
                o2 = psF[:, 6, 0:2 * n]
                P.add("pe", lambda e: e.matmul(o2, lhsT=ones_b, rhs=both, start=(c == 0), stop=(c == KC - 1)), reads=[ones_b, both], writes=[o2])
                return
            P.add("pe", lambda e: e.matmul(mu, lhsT=ones_b, rhs=zb_, start=(c == 0), stop=(c == KC - 1)), reads=[ones_b, zb_], writes=[mu])
            P.add("pe", lambda e: e.matmul(e2, lhsT=ones_b, rhs=zq_, start=(c == 0), stop=(c == KC - 1)), reads=[ones_b, zq_], writes=[e2])
        st.pending.append(later)

    def ln_flush(st, keep=0):
        while len(st.pending) > keep:
            st.pending.pop(0)()

    def ln_finish(st, li, defer=False, aff="act"):
        ln_flush(st)
        per = []
        for j, (lo, n) in enumerate(st.tiles):
            mu_ps, e2_ps = st.mu[lo], st.e2[lo]
            mu_sb = lnF["mu"][j][:, 0:n]
            m2 = lnF["va"][j][:, 0:n]
            rstd = lnF["rs"][j][:, 0:n]
            P.add("act", lambda e, mu_sb=mu_sb, mu_ps=mu_ps: e.activation(out=mu_sb, in_=mu_ps, func=AF.Identity), reads=[mu_ps], writes=[mu_sb])
            P.add("act", lambda e, m2=m2, mu_ps=mu_ps: e.activation(out=m2, in_=mu_ps, func=AF.Square), reads=[mu_ps], writes=[m2])
            P.add("dve", lambda e, m2=m2, e2_ps=e2_ps: e.tensor_tensor(out=m2, in0=e2_ps, in1=m2, op=ALU.subtract), reads=[e2_ps, m2], writes=[m2])
            P.add("act", lambda e, m2=m2, rstd=rstd: e.activation(out=rstd, in_=m2, func=AF.Sqrt, bias=EPSP, scale=1.0),
                  reads=[m2], writes=[rstd])
            P.add("dve", lambda e, rstd=rstd: e.reciprocal(out=rstd, in_=rstd), reads=[rstd], writes=[rstd])
            per.append((lo, n, mu_sb, rstd))
        tbufs = lnF["t"]

        def pass2():
            k2 = 0
            for (lo, n, mu_sb, rstd) in per:
                for c in range(KC):
                    xs = x[:, c, lo:lo + n]
                    t1 = tbufs[k2 % 2][:, 0:n]
                    k2 += 1
                    g_ = lnp_sb[:, 0, li, c:c + 1]
                    b_ = lnp_sb[:, 1, li, c:c + 1]
                    P.add("dve", lambda e, t1=t1, xs=xs, mu_sb=mu_sb: e.tensor_tensor(out=t1, in0=xs, in1=mu_sb, op=ALU.subtract), reads=[xs, mu_sb], writes=[t1])
                    P.add("dve", lambda e, t1=t1, rstd=rstd: e.tensor_tensor(out=t1, in0=t1, in1=rstd, op=ALU.mult), reads=[t1, rstd], writes=[t1])
                    if aff == "act":
                        P.add("act", lambda e, xs=xs, t1=t1, g_=g_, b_=b_: e.activation(out=xs, in_=t1, func=AF.Identity, bias=b_, scale=g_),
                              reads=[t1, g_, b_], writes=[xs])
                    else:
                        P.add("dve", lambda e, xs=xs, t1=t1, g_=g_, b_=b_: e.tensor_scalar(out=xs, in0=t1, scalar1=g_, scalar2=b_, op0=ALU.mult, op1=ALU.add),
                              reads=[t1, g_, b_], writes=[xs])
                    yield 1
        gen2 = pass2()
        if defer:
            return gen2
        for _ in gen2:
            pass
        return None

    def ln_bufs(ntile_p, with_sample):
        d = {"mu": [], "va": [], "rs": [], "t": []}
        for _ in range(ntile_p):
            for kname in ("mu", "va", "rs"):
                d[kname].append(arenaF.take([128, 512]))
        if with_sample:
            for kname in ("mu", "va", "rs"):
                d[kname].append(arenaF.take([128, 16]))
        d["t"] = [arenaF.take([128, 512]) for _ in range(2)]
        return d

    def ffn(l, which, ada_bg=False):
        fi = l * 2 + which
        li = l * 3 + (0 if which == 0 else 2)
        mod_t = mods[:, l]
        sh_oc, sc_oc, g_oc = ((0, 8, 16) if which == 0 else (48, 56, 64))
        arenaB.reset()
        hbuf = arenaB.take([128, KC, 1040])
        gbuf = arenaB.take([128, NJ, 1040])
        wup = [arenaB.take([128, 2, KC, 128]) for _ in range(3)]
        wdn = [arenaB.take([128, NJ, 128]) for _ in range(2)]
        zbq = [arenaB.take([128, 512]) for _ in range(6)]
        bg = None
        if ada_bg:
            bg = ada_gen(0, [arenaB.take([128, KC, 384]) for _ in range(2)], 3, 6, pcs=range(3, 8),
                         entries=DERIVE_ALL[1:3] + DERIVE_ALL[4:6])
        arenaF.reset()
        nonlocal arenaF_tmp16, lnF
        arenaF_tmp16 = arenaF.take([128, KC, 16])
        sa = [arenaF.take([128, 512]) for _ in range(2)]
        lnF = ln_bufs(2, True)
        cnt = 0
        wcnt = 0
        dcnt_ = 0
        pend_ln = None
        for hi, tiles in enumerate(halves):
            h0 = tiles[0][0]
            if hi == 0:
                modulate(hbuf, h0, tiles, mod_t, sh_oc, sc_oc)
            for j in range(NJ):
                wb_ = wup[wcnt % 3]
                wcnt += 1
                src = w_up[fi, j]
                P.add("pool", lambda e, wb_=wb_, src=src: e.dma_start(out=wb_, in_=src), reads=[src], writes=[wb_], dma=True)
                for (lo, n) in tiles:
                    pa = psF[:, cnt % 2, 0:n]
                    pu = psF[:, 2 + cnt % 2, 0:n]
                    sa_ = sa[cnt % 2][:, 0:n]
                    cnt += 1
                    for (o, a_or_u) in ((pa, 0), (pu, 1)):
                        for k in range(KC):
                            lt = wb_[:, a_or_u, k, :]
                            rh = hbuf[:, k, lo - h0:lo - h0 + n]
                            P.add("pe", lambda e, o=o, lt=lt, rh=rh, k=k: e.matmul(o, lhsT=lt, rhs=rh, start=(k == 0), stop=(k == KC - 1)),
                                  reads=[lt, rh], writes=[o])
                    P.add("act", lambda e, sa_=sa_, pa=pa: e.activation(out=sa_, in_=pa, func=AF.Silu), reads=[pa], writes=[sa_])
                    go = gbuf[:, j, lo - h0:lo - h0 + n]
                    P.add("dve", lambda e, go=go, sa_=sa_, pu=pu: e.tensor_tensor(out=go, in0=pu, in1=sa_, op=ALU.mult), reads=[pu, sa_], writes=[go])
                    if pend_ln is not None:
                        next(pend_ln, None)
                    if bg is not None:
                        next(bg, None)
            if hi == 0:
                modulate(hbuf, halves[1][0][0], halves[1], mod_t, sh_oc, sc_oc)
            st = ln_begin(tiles)
            zc = 0
            for c in range(KC):
                wb_ = wdn[dcnt_ % 2]
                dcnt_ += 1
                src = w_dn[fi, c]
                P.add("pool", lambda e, wb_=wb_, src=src: e.dma_start(out=wb_, in_=src), reads=[src], writes=[wb_], dma=True)
                for (lo, n) in tiles:
                    Y = psF[:, 4 + zc % 2, 0:n]
                    for kk in range(NJ):
                        lt = wb_[:, kk, :]
                        rh = gbuf[:, kk, lo - h0:lo - h0 + n]
                        P.add("pe", lambda e, Y=Y, lt=lt, rh=rh, kk=kk: e.matmul(Y, lhsT=lt, rhs=rh, start=(kk == 0), stop=(kk == NJ - 1)),
                              reads=[lt, rh], writes=[Y])
                    ln_flush(st, keep=0)
                    residual_add(Y, c, lo, n, mod_t, g_oc)
                    ln_accum(st, c, lo, n, zbq[(zc % 3) * 2], zbq[(zc % 3) * 2 + 1])
                    zc += 1
            if pend_ln is not None:
                for _ in pend_ln:
                    pass
            pend_ln = ln_finish(st, li, defer=(hi == 0))
        if bg is not None:
            for _ in bg:
                pass


    def gla_phase():
        nonlocal arenaF_tmp16, lnF
        mod_t = mods[:, 0]
        sh_oc, sc_oc, g_oc, li = 24, 32, 40, 1
        arenaB.reset()
        arenaF.reset()
        wqk = arenaB.take([128, KC, 1024])
        wglr = arenaB.take([128, KC, 16])
        wbuf = [arenaB.take([128, KC, 1024]) for _ in range(2)]
        ht = arenaB.take([128, KC, 512])
        qk = arenaB.take([128, 8, 512])
        vtm = arenaB.take([128, 4, 1024])
        ATs = [arenaB.take([128, 4, 128]) for _ in range(2)]
        ktm = [arenaB.take([128, 4, 128]) for _ in range(2)]
        Sbf = [arenaB.take([128, 4, 256]) for _ in range(2)]
        on = [arenaB.take([128, 1024]) for _ in range(2)]
        sR = arenaB.take([128, KC, 512])
        zbq = [arenaB.take([128, 512]) for _ in range(6)]
        arenaF_tmp16 = arenaF.take([128, KC, 16])
        tf = [arenaF.take([128, 512]) for _ in range(8)]
        lnF = {"mu": [tf[3], tf[3]], "va": [tf[4], tf[4]], "rs": [tf[5], tf[5]], "t": [tf[6], tf[7]]}
        S = arenaF.take([128, 4, 256])
        S1 = arenaF.take([128, 4, 256])
        eLs = arenaF.take([128, 4, 4])
        rstd4 = arenaF.take([128, 4])
        ssq = arenaF.take([128, 4])
        nbg = arenaF.take([128, 4])
        gB = arenaF.take([128, 256])
        wg2 = arenaF.take([16, 512])
        glrT = arenaF.take([16, 512])
        DKS = 128 ** -0.5

        P.add("pool", lambda e: e.dma_start(out=wqk, in_=w_in_qk), reads=[w_in_qk], writes=[wqk], dma=True)
        P.add("pool", lambda e: e.dma_start(out=wglr, in_=w_in_glr), reads=[w_in_glr], writes=[wglr], dma=True)
        P.add("sp", lambda e: e.dma_start(out=wg2, in_=wg2_d), reads=[wg2_d], writes=[wg2], dma=True)
        P.add("sp", lambda e: e.dma_start(out=nbg, in_=bgate_d), reads=[bgate_d], writes=[nbg], dma=True)
        P.add("sp", lambda e: e.dma_start(out=gB, in_=gonorm_d), reads=[gonorm_d], writes=[gB], dma=True)
        P.add("dve", lambda e: e.tensor_scalar_mul(out=nbg, in0=nbg, scalar1=-1.0), reads=[nbg], writes=[nbg])
        P.add("dve", lambda e: e.memset(S, 0.0), writes=[S])
        P.add("dve", lambda e: e.memset(Sbf[0], 0.0), writes=[Sbf[0]])
        state = {"wb": 0, "sb": 0, "ab": 0, "y": 0}

        def load_w(src):
            wb_ = wbuf[state["wb"] % 2]
            state["wb"] += 1
            P.add("pool", lambda e: e.dma_start(out=wb_, in_=src), reads=[src], writes=[wb_], dma=True)
            return wb_

        def chunk(cs, nt, vt, ck):
            i2 = state["ab"] % 2
            state["ab"] += 1
            A_, K_, on_ = ATs[i2], ktm[i2], on[i2]
            Scur = Sbf[state["sb"] % 2]
            Snext = Sbf[(state["sb"] + 1) % 2]
            state["sb"] += 1
            atp = psF[0:nt, 6, :]
            for h in range(4):
                o_ = atp[:, h * 128:h * 128 + nt]
                lt = qk[:, 4 + h, cs]
                rh = qk[:, h, cs]
                P.add("pe", lambda e, o_=o_, lt=lt, rh=rh: e.matmul(o_, lhsT=lt, rhs=rh, start=True, stop=True), reads=[lt, rh], writes=[o_])
            atv = mk(atp, [atp.ap[0], (128, 4), (1, nt)])
            av = A_[0:nt, :, 0:nt]
            cm = mk(causal_f[0:nt, 0:nt], [causal_f[0:nt, 0:nt].ap[0], (0, 4), (1, nt)])
            P.add("dve", lambda e: e.tensor_tensor(out=av, in0=atv, in1=cm, op=ALU.mult), reads=[atv, causal_f[0:nt, 0:nt]], writes=[av])
            for h in range(4):
                o_ = psB[0:nt, h * 128:(h + 1) * 128]
                i_ = qk[:, 4 + h, cs]
                P.add("pe", lambda e, o_=o_, i_=i_: e.transpose(o_, i_, ident_b), reads=[i_, ident_b], writes=[o_])
            kv_ = K_[0:nt]
            pb = mk(psB[0:nt, 0:512], [psB[0:nt, 0:512].ap[0], (128, 4), (1, 128)])
            P.add("act", lambda e: e.activation(out=kv_, in_=pb, func=AF.Identity), reads=[pb], writes=[kv_])
            ob_ = 4 if (state["ab"] % 2 == 1) else 0
            o_ps = mk(psF[0:nt, ob_, :], [psF[0:nt, ob_, :].ap[0], (256, 4), (1, 256)])
            for h in range(4):
                oh = o_ps[:, h, :]
                l1 = A_[0:nt, h, 0:nt]
                r1 = vt[0:nt, h * 256:(h + 1) * 256]
                l2 = qk[:, h, cs]
                r2 = Scur[:, h, :]
                P.add("pe", lambda e, oh=oh, l1=l1, r1=r1: e.matmul(oh, lhsT=l1, rhs=r1, start=True, stop=False), reads=[l1, r1], writes=[oh])
                P.add("pe", lambda e, oh=oh, l2=l2, r2=r2: e.matmul(oh, lhsT=l2, rhs=r2, start=False, stop=True), reads=[l2, r2], writes=[oh])
            u_ps = mk(psF[:, 2, :], [psF[:, 2, :].ap[0], (256, 4), (1, 256)])
            for h in range(4):
                uh = u_ps[:, h, :]
                l1 = K_[0:nt, h, :]
                r1 = vt[0:nt, h * 256:(h + 1) * 256]
                P.add("pe", lambda e, uh=uh, l1=l1, r1=r1: e.matmul(uh, lhsT=l1, rhs=r1, start=True, stop=True), reads=[l1, r1], writes=[uh])
            for h in range(4):
                el = eLs[:, h, ck:ck + 1]
                s_h, s1_h, uh = S[:, h, :], S1[:, h, :], u_ps[:, h, :]
                P.add("act", lambda e, s_h=s_h, s1_h=s1_h, el=el: e.activation(out=s1_h, in_=s_h, func=AF.Identity, scale=el), reads=[s_h, el], writes=[s1_h])
                P.add("dve", lambda e, s_h=s_h, s1_h=s1_h, uh=uh, el=el: e.scalar_tensor_tensor(out=s_h, in0=uh, scalar=el, in1=s1_h, op0=ALU.mult, op1=ALU.add),
                      reads=[uh, el, s1_h], writes=[s_h])
            P.add("act", lambda e: e.activation(out=Snext, in_=S, func=AF.Identity), reads=[S], writes=[Snext])
            def part_b():
                sq = mk(tf[0][0:nt, :], [tf[0][0:nt, :].ap[0], (1, 512)])
                sqv = mk(tf[0][0:nt, 0:1], [tf[0][0:nt, 0:1].ap[0], (256, 4), (1, 256)])
                sq_box = [tf[0][0:nt, :], tf[1][0:nt, :]]
                P.add("act", lambda e: e.activation(out=sqv, in_=o_ps, func=AF.Square), reads=[o_ps], writes=sq_box)
                ss = ssq[0:nt, :]
                rs = rstd4[0:nt, :]
                P.add("dve", lambda e: e.tensor_reduce(out=ss, in_=sqv, axis=AX.X, op=ALU.add), reads=sq_box, writes=[ss])
                P.add("act", lambda e: e.activation(out=rs, in_=ss, func=AF.Sqrt, bias=1e-6, scale=1.0 / 256.0), reads=[ss], writes=[rs])
                P.add("dve", lambda e: e.reciprocal(out=rs, in_=rs), reads=[rs], writes=[rs])
                for h in range(4):
                    oh = o_ps[:, h, :]
                    onh = on_[0:nt, h * 256:(h + 1) * 256]
                    r_ = rstd4[0:nt, h:h + 1]
                    g_ = gB[0:nt, :]
                    P.add("dve", lambda e, oh=oh, onh=onh, r_=r_, g_=g_: e.scalar_tensor_tensor(out=onh, in0=oh, scalar=r_, in1=g_, op0=ALU.mult, op1=ALU.mult),
                          reads=[oh, r_, g_], writes=[onh])
                for c in range(KC):
                    o_ = psB[:, c * 128:c * 128 + nt]
                    i_ = on_[0:nt, c * 128:(c + 1) * 128]
                    idn = ident_b[0:nt, 0:nt]
                    P.add("pe", lambda e, o_=o_, i_=i_, idn=idn: e.transpose(o_, i_, idn), reads=[i_, idn], writes=[o_])
                pbv = mk(psB[:, 0:1], [psB[:, 0:1].ap[0], (128, KC), (1, nt)])
                srv = sR[:, :, cs]
                P.add("dve", lambda e: e.tensor_tensor(out=srv, in0=pbv, in1=srv, op=ALU.mult), reads=[psB[:, :], srv], writes=[srv])
            return part_b

        all_tiles = [(0, 512), (512, 512), (1024, 512), (1536, 512), (2048, 16)]
        def st_laqk(ti):
            lo, n = all_tiles[ti]
            smp = lo >= T
            nch = 4
            ctok = 4 if smp else 128
            if ti == 0:
                modulate(ht, lo, [(lo, n)], mod_t, sh_oc, sc_oc)
            gp = psF[0:16, 6, 0:n]
            for k in range(KC):
                lt = wglr[:, k, :]
                rh = ht[:, k, 0:n]
                P.add("pe", lambda e, gp=gp, lt=lt, rh=rh, k=k: e.matmul(gp, lhsT=lt, rhs=rh, start=(k == 0), stop=(k == KC - 1)), reads=[lt, rh], writes=[gp])
            gl = glrT[:, 0:n]
            P.add("act", lambda e, gl=gl, gp=gp: e.activation(out=gl, in_=gp, func=AF.Identity), reads=[gp], writes=[gl])
            for h in range(4):
                ta, tb, tq, tk = [tf[(h % 2) * 4 + i][:, 0:n] for i in range(4)]
                lp = psF[:, h % 2, 0:n]
                lt = wg2[:, h * 128:(h + 1) * 128]
                P.add("pe", lambda e, lp=lp, lt=lt, gl=gl: e.matmul(lp, lhsT=lt, rhs=gl, start=True, stop=True), reads=[lt, gl], writes=[lp])
                nb_ = nbg[:, h:h + 1]
                P.add("act", lambda e, ta=ta, lp=lp, nb_=nb_: e.activation(out=ta, in_=lp, func=AF.Exp, bias=nb_, scale=-1.0), reads=[lp, nb_], writes=[ta])
                P.add("act", lambda e, ta=ta: e.activation(out=ta, in_=ta, func=AF.Ln, bias=1.0, scale=1.0), reads=[ta], writes=[ta])
                sm = (scanm_s if smp else scanm_p)[:, 0:n]
                P.add("dve", lambda e, tb=tb, ta=ta, sm=sm: e.tensor_tensor_scan(out=tb, data0=sm, data1=ta, initial=0.0, op0=ALU.mult, op1=ALU.add),
                      reads=[ta, sm], writes=[tb])
                P.add("act", lambda e, tq=tq, tb=tb: e.activation(out=tq, in_=tb, func=AF.Exp, scale=-1.0 / 16.0), reads=[tb], writes=[tq])
                P.add("act", lambda e, tk=tk, tb=tb: e.activation(out=tk, in_=tb, func=AF.Exp, scale=1.0 / 16.0), reads=[tb], writes=[tk])
                cl = 4 if smp else 128
                src_ = mk(tq[:, cl - 1:cl], [tq[:, cl - 1:cl].ap[0], (cl, 4)])
                dst_ = eLs[:, h, :]
                P.add("act", lambda e, src_=src_, dst_=dst_: e.activation(out=dst_, in_=src_, func=AF.Identity), reads=[tq], writes=[dst_])
                qp = psF[:, 2 + 2 * (h % 2), 0:n]
                kp = psF[:, 3 + 2 * (h % 2), 0:n]
                for (o_, cb) in ((qp, h * 128), (kp, 512 + h * 128)):
                    for k in range(KC):
                        lt = wqk[:, k, cb:cb + 128]
                        rh = ht[:, k, 0:n]
                        P.add("pe", lambda e, o_=o_, lt=lt, rh=rh, k=k: e.matmul(o_, lhsT=lt, rhs=rh, start=(k == 0), stop=(k == KC - 1)), reads=[lt, rh], writes=[o_])
                qo = qk[:, h, 0:n]
                ko = qk[:, 4 + h, 0:n]
                P.add("dve", lambda e, qo=qo, qp=qp, tq=tq: e.scalar_tensor_tensor(out=qo, in0=qp, scalar=DKS, in1=tq, op0=ALU.mult, op1=ALU.mult), reads=[qp, tq], writes=[qo])
                P.add("dve", lambda e, ko=ko, kp=kp, tk=tk: e.tensor_tensor(out=ko, in0=kp, in1=tk, op=ALU.mult), reads=[kp, tk], writes=[ko])

        def st_vr(ti):
            lo, n = all_tiles[ti]
            smp = lo >= T
            nch = 4
            ctok = 4 if smp else 128
            wv_ = load_w(w_in_v)
            nch = 4
            ctok = 4 if smp else 128
            for ck in range(nch):
                vp = mk(psF[0:ctok, 4, :], [psF[0:ctok, 4, :].ap[0], (1, 1024)])
                for hf in range(2):
                    o_ = psF[0:ctok, 4 + hf, :]
                    for k in range(KC):
                        lt = ht[:, k, ck * ctok:(ck + 1) * ctok]
                        rh = wv_[:, k, hf * 512:(hf + 1) * 512]
                        P.add("pe", lambda e, o_=o_, lt=lt, rh=rh, k=k: e.matmul(o_, lhsT=lt, rhs=rh, start=(k == 0), stop=(k == KC - 1)), reads=[lt, rh], writes=[o_])
                vo = vtm[0:ctok, ck, :]
                P.add("act", lambda e, vo=vo, vp=vp: e.activation(out=vo, in_=vp, func=AF.Identity), reads=[psF[0:ctok, 4:6, :]], writes=[vo])
            wr_ = load_w(w_in_r)
            for c in range(KC):
                rp = psF[:, 2 + c % 2, 0:n]
                for k in range(KC):
                    lt = wr_[:, k, c * 128:(c + 1) * 128]
                    rh = ht[:, k, 0:n]
                    P.add("pe", lambda e, rp=rp, lt=lt, rh=rh, k=k: e.matmul(rp, lhsT=lt, rhs=rh, start=(k == 0), stop=(k == KC - 1)), reads=[lt, rh], writes=[rp])
                so = sR[:, c, 0:n]
                P.add("act", lambda e, so=so, rp=rp: e.activation(out=so, in_=rp, func=AF.Silu), reads=[rp], writes=[so])
            if ti + 1 < len(all_tiles):
                nlo, nn = all_tiles[ti + 1]
                modulate(ht, nlo, [(nlo, nn)], mod_t, sh_oc, sc_oc)

        def st_rec(ti):
            lo, n = all_tiles[ti]
            smp = lo >= T
            nch = 4
            ctok = 4 if smp else 128
            pend_b = None
            for ck in range(nch):
                if smp:
                    src = state_in[ck]
                    P.add("sp", lambda e, src=src: e.dma_start(out=S, in_=src), reads=[src], writes=[S], dma=True)
                    Snx = Sbf[state["sb"] % 2]
                    P.add("act", lambda e, Snx=Snx: e.activation(out=Snx, in_=S, func=AF.Identity), reads=[S], writes=[Snx])
                nb_ = chunk(slice(ck * ctok, (ck + 1) * ctok), ctok, vtm[:, ck, :], ck)
                if smp:
                    dst = st_s[ck]
                    P.add("sp", lambda e, dst=dst: e.dma_start(out=dst, in_=S), reads=[S], writes=[dst], dma=True)
                if pend_b is not None:
                    pend_b()
                pend_b = nb_
            pend_b()
            if ti == 3:
                P.add("sp", lambda e: e.dma_start(out=st_p, in_=S), reads=[S], writes=[st_p], dma=True)

        def st_out(ti):
            lo, n = all_tiles[ti]
            smp = lo >= T
            nch = 4
            ctok = 4 if smp else 128
            wo_ = load_w(w_outa)
            st = ln_begin([(lo, n)])
            zc = 0
            for c in range(KC):
                Y = psF[:, 2 + c % 2, 0:n]
                for k in range(KC):
                    lt = wo_[:, k, c * 128:(c + 1) * 128]
                    rh = sR[:, k, 0:n]
                    P.add("pe", lambda e, Y=Y, lt=lt, rh=rh, k=k: e.matmul(Y, lhsT=lt, rhs=rh, start=(k == 0), stop=(k == KC - 1)), reads=[lt, rh], writes=[Y])
                ln_flush(st, keep=0)
                residual_add(Y, c, lo, n, mod_t, g_oc)
                ln_accum(st, c, lo, n, zbq[(zc % 3) * 2], zbq[(zc % 3) * 2 + 1])
                zc += 1
            ln_finish(st, li, aff="dve")

        nT = len(all_tiles)
        st_laqk(0)
        st_vr(0)
        for ti in range(nT):
            st_rec(ti)
            if ti + 1 < nT:
                st_laqk(ti + 1)
            st_out(ti)
            if ti + 1 < nT:
                st_vr(ti + 1)


    KS_OFF, QS_OFF = 52700, 52830
    ks_all = mk(arenaB.ap[:, KS_OFF:KS_OFF + 128], [arenaB.ap[:, KS_OFF:KS_OFF + 128].ap[0], (16, KC), (1, 16)])
    q3s_all = mk(arenaB.ap[:, QS_OFF:QS_OFF + 384], [arenaB.ap[:, QS_OFF:QS_OFF + 384].ap[0], (48, KC), (16, 3), (1, 16)])
    all_tiles5 = [(0, 512), (512, 512), (1024, 512), (1536, 512), (2048, 16)]

    def rope_tiles():
        cosT = arenaF.take([128, NT])
        sinT = arenaF.take([128, NT])
        return cosT, sinT

    def rope_a(xp, n, xb):
        P.add("act", lambda e: e.activation(out=xb, in_=xp, func=AF.Identity), reads=[xp], writes=[xb])

    def rope_b(xp, lo, n, cosT, sinT, t1, t2, outs, xb, xs_ps):
        P.add("pe", lambda e: e.matmul(xs_ps, lhsT=pswap_b, rhs=xb, start=True, stop=True), reads=[pswap_b, xb], writes=[xs_ps])
        c_ = cosT[:, lo:lo + n]
        s_ = sinT[:, lo:lo + n]
        P.add("dve", lambda e: e.tensor_tensor(out=t1, in0=xp, in1=c_, op=ALU.mult), reads=[xp, c_], writes=[t1])
        P.add("dve", lambda e: e.tensor_tensor(out=t2, in0=xs_ps, in1=s_, op=ALU.mult), reads=[xs_ps, s_], writes=[t2])
        for o_ in outs:
            if isinstance(o_, tuple):
                o_, d_ = o_
                a1 = mk(t1, [t1.ap[0], (1, d_), (d_, n // d_)])
                a2 = mk(t2, [t2.ap[0], (1, d_), (d_, n // d_)])
                P.add("dve", lambda e, o_=o_, a1=a1, a2=a2: e.tensor_tensor(out=o_, in0=a1, in1=a2, op=ALU.add), reads=[t1, t2], writes=[o_])
            else:
                P.add("dve", lambda e, o_=o_: e.tensor_tensor(out=o_, in0=t1, in1=t2, op=ALU.add), reads=[t1, t2], writes=[o_])

    def kv_phase():
        nonlocal arenaF_tmp16
        arenaB.reset()
        arenaF.reset()
        hkv = arenaB.take([128, KC, NT])
        wkv = arenaB.take([128, KC, 2048])
        kst = [arenaB.take([128, 512]) for _ in range(2)]
        vst = [arenaB.take([128, 16, 65]) for _ in range(2)]
        ada1 = ada_gen(1, [arenaB.take([128, KC, 384]) for _ in range(2)], 3, 6)
        cosT, sinT = rope_tiles()
        arenaF_tmp16 = arenaF.take([128, KC, 16])
        scr = arenaF.take([128, 3200])
        t1 = scr[:, 0:512]
        t2 = scr[:, 512:1024]
        kout = [scr[:, 1024:1536], scr[:, 1536:2048]]
        vout = [scr[:, 1024:2048], scr[:, 2048:3072]]
        P.add("sp", lambda e: e.dma_start(out=cosT, in_=ropet[0]), reads=[ropet[0]], writes=[cosT], dma=True)
        P.add("sp", lambda e: e.dma_start(out=sinT, in_=ropet[1]), reads=[ropet[1]], writes=[sinT], dma=True)
        for q_ in range(4):
            wd_, ws_ = wkv[:, :, q_ * 512:(q_ + 1) * 512], w_kv_d[:, :, q_ * 512:(q_ + 1) * 512]
            P.add("pool", lambda e, wd_=wd_, ws_=ws_: e.dma_start(out=wd_, in_=ws_), reads=[ws_], writes=[wd_], dma=True)
        for v_ in vst:
            P.add("dve", lambda e, v_=v_: e.memset(v_, 1.0), writes=[v_])
        vt_tiles = [(i * 128, 128) for i in range(16)] + [(2048, 16)]

        def v_tile(vi):
            lo, nt = vt_tiles[vi]
            vp = mk(psF[0:nt, 2, :], [psF[0:nt, 2, :].ap[0], (1, 1024)])
            for hf in range(2):
                o_ = psF[0:nt, 2 + hf, :]
                for k in range(KC):
                    lt = hkv[:, k, lo:lo + nt]
                    rh = wkv[:, k, 1024 + hf * 512:1024 + (hf + 1) * 512]
                    P.add("pe", lambda e, o_=o_, lt=lt, rh=rh, k=k: e.matmul(o_, lhsT=lt, rhs=rh, start=(k == 0), stop=(k == KC - 1)), reads=[lt, rh], writes=[o_])
            vo = vout1[0:nt, :]
            vb = vst[vi % 2][0:nt, :, 0:64]
            vpv = mk(vp, [vp.ap[0], (64, 16), (1, 64)])
            pbx = [psF[0:nt, 2:4, :]]
            P.add("act", lambda e, vo=vo, vp=vp: e.activation(out=vo, in_=vp, func=AF.Identity), reads=pbx, writes=[vo])
            P.add("dve", lambda e, vb=vb, vpv=vpv: e.tensor_copy(out=vb, in_=vpv), reads=pbx, writes=[vb])
            dst = v_tm[lo:lo + nt, :]
            P.add("sp", lambda e, dst=dst, vo=vo: e.dma_start(out=dst, in_=vo), reads=[vo], writes=[dst], dma=True)
            vfull = mk(vst[vi % 2][0:nt], [vst[vi % 2][0:nt].ap[0], (1, 1040)])
            dst2 = vscr[lo:lo + nt, :]
            P.add("sp", lambda e, dst2=dst2, vfull=vfull: e.dma_start(out=dst2, in_=vfull), reads=[vfull], writes=[dst2], dma=True)


        vout1 = scr[:, 2048:3072]
        vdone = [0]
        cnt = 0
        xbs = [arenaB.take([128, 512]) for _ in range(2)]
        prev = None

        def finish(u):
            (kp, lo, n, c, cnt_) = u
            ko = kout[cnt_ % 2][:, 0:n]
            kb = kst[cnt_ % 2][:, 0:n] if lo < T else ks_all[:, c, :]
            rope_b(kp, lo, n, cosT, sinT, t1[:, 0:n], t2[:, 0:n], [ko], xbs[cnt_ % 2][:, 0:n], psF[:, 4 + cnt_ % 2, 0:n])
            P.add("act", lambda e: e.activation(out=kb, in_=ko, func=AF.Identity), reads=[ko], writes=[kb])
            dst = k_fm[:, c, lo:lo + n]
            P.add("sp", lambda e: e.dma_start(out=dst, in_=ko), reads=[ko], writes=[dst], dma=True)
            dst2 = kscr[c, :, lo:lo + n]
            P.add("sp", lambda e: e.dma_start(out=dst2, in_=kb), reads=[kb], writes=[dst2], dma=True)

        modulate(hkv, 0, all_tiles5[0:1], modkv, 0, 8)
        for t5, (lo, n) in enumerate(all_tiles5):
            if t5 + 1 < len(all_tiles5):
                modulate(hkv, 0, all_tiles5[t5 + 1:t5 + 2], modkv, 0, 8)
            for c in range(KC):
                kp = psF[:, cnt % 2, 0:n]
                for k in range(KC):
                    lt = wkv[:, k, c * 128:(c + 1) * 128]
                    rh = hkv[:, k, lo:lo + n]
                    P.add("pe", lambda e, kp=kp, lt=lt, rh=rh, k=k: e.matmul(kp, lhsT=lt, rhs=rh, start=(k == 0), stop=(k == KC - 1)), reads=[lt, rh], writes=[kp])
                rope_a(kp, n, xbs[cnt % 2][:, 0:n])
                if prev is not None:
                    finish(prev)
                prev = (kp, lo, n, c, cnt)
                cnt += 1
                next(ada1, None)
                if cnt % 2 == 0 and vdone[0] < len(vt_tiles) and vt_tiles[vdone[0]][0] < lo + n:
                    v_tile(vdone[0])
                    vdone[0] += 1
        finish(prev)
        for _ in ada1:
            pass
        while vdone[0] < len(vt_tiles):
            v_tile(vdone[0])
            vdone[0] += 1
    def attn_phase():
        nonlocal arenaF_tmp16, lnF
        l = 1
        mod_t = mods[:, 1]
        sh_oc, sc_oc, g_oc, li = 24, 32, 40, 4
        arenaB.reset()
        arenaF.reset()
        oT = arenaB.take([128, KC, NT])
        B0 = arenaB.off
        wq_ = arenaB.take([128, KC, 384])
        ht = arenaB.take([128, KC, 512])
        q3b = [arenaB.take([128, 3, T]) for _ in range(2)]
        kt_ = arenaB.take([128, T])
        vpm = [arenaB.take([128, 3, 16, 130]) for _ in range(2)]
        Pt = [arenaB.take([128, 256]) for _ in range(4)]
        xbq = [arenaB.take([128, 512]) for _ in range(2)]
        assert arenaB.off <= KS_OFF, arenaB.off
        cosT, sinT = rope_tiles()
        P.add("sp", lambda e: e.dma_start(out=cosT, in_=ropet[0]), reads=[ropet[0]], writes=[cosT], dma=True)
        P.add("sp", lambda e: e.dma_start(out=sinT, in_=ropet[1]), reads=[ropet[1]], writes=[sinT], dma=True)
        arenaF_tmp16 = arenaF.take([128, KC, 16])
        F0 = arenaF.off
        t1 = arenaF.take([128, 512])
        t2 = arenaF.take([128, 512])
        oacc = arenaF.take([65, 2048])
        rdn = arenaF.take([64, 256])
        DILS = (1, 4, 16)

        def gather_v(c, vb):
            base = vscr[0:T, c * 130:(c + 1) * 130]
            off0 = int(base.offset)
            s0 = bass.AP(vscr.tensor, off0, [[1040, 128], [128 * 1040, 16], [1, 130]])
            d0 = vb[:, 0, :, :]
            P.add("sp", lambda e: e.dma_start(out=d0, in_=s0), reads=[vscr[0:T, :]], writes=[d0], dma=True)
            for r in range(4):
                s1 = bass.AP(vscr.tensor, off0 + r * 1040, [[4 * 1040, 128], [512 * 1040, 4], [1, 130]])
                d1 = vb[:, 1, r * 4:(r + 1) * 4, :]
                P.add("sp", lambda e, s1=s1, d1=d1: e.dma_start(out=d1, in_=s1), reads=[vscr[0:T, :]], writes=[d1], dma=True)
            s2 = bass.AP(vscr.tensor, off0, [[16 * 1040, 128], [1040, 16], [1, 130]])
            d2 = vb[:, 2, :, :]
            P.add("sp", lambda e: e.dma_start(out=d2, in_=s2), reads=[vscr[0:T, :]], writes=[d2], dma=True)

        def blocks_of(g):
            d = DILS[g]
            nbk = 16 // d
            return [(r, i, nbk) for r in range(d) for i in range(nbk)]

        def cols(ap2d, g, r, i, nblk):
            d = DILS[g]
            st = i * 128 * d + r
            return mk(ap2d[:, st:st + 1], [ap2d[:, st:st + 1].ap[0], (d, 128 * nblk)])

        cnts = {"s": 0, "p": 0, "q": 0, "tick": 0, "inhead": 0, "mid": 0}
        pend = {"norm": None}

        def qproj_gen(c, q3):
            src = w_q_d[c]
            P.add("pool", lambda e, src=src: e.dma_start(out=wq_, in_=src), reads=[src], writes=[wq_], dma=True)
            for (lo, n) in all_tiles5:
                modulate(ht, lo, [(lo, n)], mod_t, sh_oc, sc_oc)
                for g in range(3):
                    qp = psF[:, 5, 0:n]
                    cnts["q"] += 1
                    for k in range(KC):
                        lt = wq_[:, k, g * 128:(g + 1) * 128]
                        rh = ht[:, k, 0:n]
                        P.add("pe", lambda e, qp=qp, lt=lt, rh=rh, k=k: e.matmul(qp, lhsT=lt, rhs=rh, start=(k == 0), stop=(k == KC - 1)), reads=[lt, rh], writes=[qp])
                    if lo >= T:
                        qo = q3s_all[:, c, g, :]
                    elif g == 0:
                        qo = q3[:, g, lo:lo + n]
                    else:
                        d_ = DILS[g]
                        b_ = q3[:, g, lo // d_:lo // d_ + 1]
                        qo = (mk(b_, [b_.ap[0], (T // d_, d_), (1, n // d_)]), d_)
                    xb = xbq[cnts["q"] % 2][:, 0:n]
                    rope_a(qp, n, xb)
                    cnts["mid"] = 1
                    yield 1
                    rope_b(qp, lo, n, cosT, sinT, t1[:, 0:n], t2[:, 0:n], [qo], xb, psF[:, 6, 0:n])
                    cnts["mid"] = 0
                    yield 1

        def attn_head(c, hh, vb, q3, tick):
            pb0 = hh * 64
            for g in range(3):
                d = DILS[g]
                nbk = 16 // d
                kq = q3[pb0:pb0 + 64, g, :]
                kk = kt_[pb0:pb0 + 64, :]
                blocks = [(r, i) for r in range(d) for i in range(nbk)]
                Pof = {}

                def emit_S(b):
                    r, i = blocks[b]
                    nq = 2 if i < nbk - 1 else 1
                    sp_ = psF[:, cnts["s"] % 3, 0:128 * nq]
                    Pb = Pt[cnts["s"] % 4][:, 0:128 * nq]
                    cnts["s"] += 1
                    ma = maskp_b[:, 0:128 * nq]
                    P.add("pe", lambda e: e.matmul(sp_, lhsT=ident_b, rhs=ma, start=True, stop=False), reads=[ident_b, ma], writes=[sp_])
                    lt = cols(kk, g, r, i, 1)
                    if g == 0:
                        rh = cols(kq, g, r, i, nq)
                    else:
                        st_ = r * (T // d) + i * 128
                        rh = kq[:, st_:st_ + 128 * nq]
                    P.add("pe", lambda e: e.matmul(sp_, lhsT=lt, rhs=rh, start=False, stop=True), reads=[kk, kq], writes=[sp_])
                    P.add("act", lambda e: e.activation(out=Pb, in_=sp_, func=AF.Exp, scale=0.125), reads=[sp_], writes=[Pb])
                    Pof[b] = Pb

                def emit_V(b, po):
                    r, i = blocks[b]
                    tix = (i if g == 0 else (r * 4 + i if g == 1 else r))
                    first = (i == 0)
                    if not first:
                        l0 = vb[:, g, tix - 1, hh * 65:(hh + 1) * 65]
                        r0 = Pof[b - 1][:, 128:256]
                        P.add("pe", lambda e: e.matmul(po, lhsT=l0, rhs=r0, start=True, stop=False), reads=[l0, r0], writes=[po])
                    l1 = vb[:, g, tix, hh * 65:(hh + 1) * 65]
                    r1 = Pof[b][:, 0:128]
                    P.add("pe", lambda e: e.matmul(po, lhsT=l1, rhs=r1, start=first, stop=True), reads=[l1, r1], writes=[po])

                emit_S(0)
                emit_S(1)
                for b0 in range(0, 16, 4):
                    po_bank = psF[0:65, 3 + cnts["p"] % 2, :]
                    cnts["p"] += 1
                    for j in range(4):
                        b = b0 + j
                        if b + 2 < 16:
                            emit_S(b + 2)
                        emit_V(b, po_bank[:, j * 128:(j + 1) * 128])
                        tick()
                    rb, ib = blocks[b0]
                    if g == 0:
                        dst = oacc[:, ib * 128:(ib + 4) * 128]
                        P.add("dve", lambda e, dst=dst, pbk=po_bank: e.tensor_copy(out=dst, in_=pbk), reads=[po_bank], writes=[dst])
                    elif g == 1:
                        dst = mk(oacc[:, rb:rb + 1], [oacc[:, rb:rb + 1].ap[0], (4, 512)])
                        P.add("dve", lambda e, dst=dst, pbk=po_bank: e.tensor_tensor(out=dst, in0=dst, in1=pbk, op=ALU.add), reads=[po_bank, oacc[:, :]], writes=[oacc[:, :]])
                    else:
                        dst = mk(oacc[:, rb:rb + 1], [oacc[:, rb:rb + 1].ap[0], (1, 4), (16, 128)])
                        src3 = mk(po_bank, [po_bank.ap[0], (128, 4), (1, 128)])
                        P.add("dve", lambda e, dst=dst, src3=src3: e.tensor_tensor(out=dst, in0=dst, in1=src3, op=ALU.add), reads=[po_bank, oacc[:, :]], writes=[oacc[:, :]])
            def norm():
                drow = oacc[64:65, :]
                P.add("act", lambda e: e.activation(out=drow, in_=drow, func=AF.Ln), reads=[drow], writes=[drow])
                P.add("act", lambda e: e.activation(out=drow, in_=drow, func=AF.Exp, scale=-1.0), reads=[drow], writes=[drow])
                for tix in range(4):
                    cs_ = slice(tix * 512, (tix + 1) * 512)
                    dps = psF[0:64, 5 + cnts["q"] % 2, :]
                    cnts["q"] += 1
                    rh = oacc[64:65, cs_]
                    lt = sel_f[64:65, :]
                    P.add("pe", lambda e, dps=dps, rh=rh, lt=lt: e.matmul(dps, lhsT=lt, rhs=rh, start=True, stop=True), reads=[lt, rh], writes=[dps])
                    num = oacc[0:64, cs_]
                    dst = oT[pb0:pb0 + 64, c, cs_]
                    P.add("dve", lambda e, dst=dst, num=num, dps=dps: e.tensor_tensor(out=dst, in0=num, in1=dps, op=ALU.mult), reads=[num, dps], writes=[dst])
            return norm

        for _ in qproj_gen(0, q3b[0]):
            pass
        for c in range(KC):
            srck = kscr[c][:, 0:T]
            P.add("sp", lambda e, srck=srck: e.dma_start(out=kt_, in_=srck), reads=[kscr[c]], writes=[kt_], dma=True)
            vb = vpm[c % 2]
            if c == 0:
                gather_v(0, vb)
            if c + 1 < KC:
                gather_v(c + 1, vpm[(c + 1) % 2])
            gen = qproj_gen(c + 1, q3b[(c + 1) % 2]) if c + 1 < KC else iter(())

            def tick():
                cnts["tick"] += 1
                cnts["inhead"] += 1
                if cnts["inhead"] == 3 and pend["norm"] is not None:
                    if cnts["mid"]:
                        next(gen, None)
                    pend["norm"]()
                    pend["norm"] = None
                if cnts["tick"] % TICKN == 0:
                    next(gen, None)
            for hh in range(2):
                cnts["inhead"] = 0
                nf = attn_head(c, hh, vb, q3b[c % 2], tick)
                assert pend["norm"] is None
                pend["norm"] = nf
            for _ in gen:
                pass
        assert not cnts["mid"]
        pend["norm"]()
        arenaB.reset(B0)
        kctx = arenaB.take([128, KC, 1024])
        vctx = arenaB.take([128, 8, 1040])
        stb = [arenaB.take([128, 1024]) for _ in range(4)]
        Pgs = [arenaB.take([128, 8, 4]) for _ in range(3)]
        wob = arenaB.take([128, KC, 1024])
        zbq = [arenaB.take([128, 512]) for _ in range(6)]
        assert arenaB.off <= KS_OFF
        arenaF.reset(0)
        stf = [arenaF.take([128, 1024]) for _ in range(4)]
        arenaF.reset(F0)
        Pfs = [arenaF.take([128, 96]) for _ in range(2)]
        oas = arenaF.take([65, 16, 16])
        rds = arenaF.take([64, 256])
        lnF = ln_bufs(1, False)
        lnF = {k_: v_ + v_ for k_, v_ in lnF.items()}
        P.add("pool", lambda e: e.dma_start(out=wob, in_=w_ob_d), reads=[w_ob_d], writes=[wob], dma=True)
        P.add("dve", lambda e: e.memset(kctx[:, :, 896:1024], 0.0), writes=[kctx[:, :, 896:1024]])
        P.add("dve", lambda e: e.memset(vctx[:, 7, :], 0.0), writes=[vctx[:, 7, :]])
        P.add("dve", lambda e: e.memset(vctx[:, 0:7, :], 1.0), writes=[vctx[:, 0:7, :]])
        zst = {"zc": 0}

        def out_proj(lo, n):
            st = ln_begin([(lo, n)])
            for c in range(KC):
                Y = psF[:, 2 + c % 2, 0:n]
                for k in range(KC):
                    lt = wob[:, k, c * 128:(c + 1) * 128]
                    rh = oT[:, k, lo:lo + n]
                    P.add("pe", lambda e, Y=Y, lt=lt, rh=rh, k=k: e.matmul(Y, lhsT=lt, rhs=rh, start=(k == 0), stop=(k == KC - 1)), reads=[lt, rh], writes=[Y])
                ln_flush(st, keep=0)
                residual_add(Y, c, lo, n, mod_t, g_oc)
                zc = zst["zc"]
                ln_accum(st, c, lo, n, zbq[(zc % 3) * 2], zbq[(zc % 3) * 2 + 1])
                zst["zc"] += 1
            ln_finish(st, li)

        sc = 0
        hcnt = 0
        for s_ in range(4):
            for tile in range(7):
                for (cache, isk) in ((cache_k, True), (cache_v, False)):
                    sf = stf[sc % 4]
                    sbb = stb[sc % 4]
                    sc += 1
                    if tile < 3:
                        for t_ in range(4):
                            src = bass.AP(cache.tensor, int(cache[s_].offset) + (16 * 32 * tile + t_) * 1024, [[16 * 1024, 32], [1, 1024]])
                            dstp = sf[t_ * 32:(t_ + 1) * 32, :]
                            P.add("sp", lambda e, src=src, dstp=dstp: e.dma_start(out=dstp, in_=src), reads=[cache[s_]], writes=[dstp], dma=True)
                    else:
                        r0_ = 1536 + (tile - 3) * 128
                        src = cache[s_, r0_:r0_ + 128, :]
                        P.add("sp", lambda e, src=src, sf=sf: e.dma_start(out=sf, in_=src), reads=[src], writes=[sf], dma=True)
                    if isk:
                        P.add("act", lambda e, sf=sf, sbb=sbb: e.activation(out=sbb, in_=sf, func=AF.Identity), reads=[sf], writes=[sbb])
                        for c in range(KC):
                            o_ = psB[:, c * 128:(c + 1) * 128]
                            i_ = sbb[:, c * 128:(c + 1) * 128]
                            P.add("pe", lambda e, o_=o_, i_=i_: e.transpose(o_, i_, ident_b), reads=[i_, ident_b], writes=[o_])
                        pbv = mk(psB[:, :], [psB[:, :].ap[0], (128, KC), (1, 128)])
                        dst = kctx[:, :, tile * 128:(tile + 1) * 128]
                        P.add("dve", lambda e, dst=dst, pbv=pbv: e.tensor_copy(out=dst, in_=pbv), reads=[psB[:, :]], writes=[dst])
                    else:
                        dst = mk(vctx[:, tile, 0:1], [vctx[:, tile, 0:1].ap[0], (65, 16), (1, 64)])
                        sfv = mk(sf, [sf.ap[0], (64, 16), (1, 64)])
                        P.add("dve", lambda e, dst=dst, sfv=sfv: e.tensor_copy(out=dst, in_=sfv), reads=[sf], writes=[vctx[:, tile, :]])
            ksn = ks_all[:, :, 4 * s_:4 * s_ + 4]
            kd = kctx[:, :, 896:900]
            P.add("act", lambda e, kd=kd, ksn=ksn: e.activation(out=kd, in_=ksn, func=AF.Identity), reads=[ksn], writes=[kd])
            vsrc = vscr[T + 4 * s_:T + 4 * s_ + 4, :]
            vd = vctx[0:4, 7, :]
            P.add("sp", lambda e, vd=vd, vsrc=vsrc: e.dma_start(out=vd, in_=vsrc), reads=[vsrc], writes=[vd], dma=True)

            def head_S(h):
                c, pb0 = h // 2, (h % 2) * 64
                sps = psF[:, 4 + hcnt_of[h] % 2, 0:96]
                Pf = Pfs[hcnt_of[h] % 2]
                pg = Pgs[hcnt_of[h] % 3]
                P.add("pe", lambda e: e.matmul(sps, lhsT=ident_b, rhs=masks_b, start=True, stop=False), reads=[ident_b, masks_b], writes=[sps])
                for tile in range(8):
                    for g in range(3):
                        o_ = sps[:, tile * 12 + g * 4:tile * 12 + g * 4 + 4]
                        lt = kctx[pb0:pb0 + 64, c, tile * 128:(tile + 1) * 128]
                        rh = q3s_all[pb0:pb0 + 64, c, g, 4 * s_:4 * s_ + 4]
                        last = (tile == 7 and g == 2)
                        P.add("pe", lambda e, o_=o_, lt=lt, rh=rh, last=last: e.matmul(o_, lhsT=lt, rhs=rh, start=False, stop=last), reads=[lt, rh], writes=[o_])
                P.add("act", lambda e: e.activation(out=Pf, in_=sps, func=AF.Exp, scale=0.125), reads=[sps], writes=[Pf])
                pfv = mk(Pf, [Pf.ap[0], (12, 8), (1, 4), (4, 3)])

                def _red(e):
                    with nc.allow_low_precision(reason="3-term fp32 sum rounded once to the bf16 matmul operand"):
                        return e.tensor_reduce(out=pg, in_=pfv, axis=AX.X, op=ALU.add)
                P.add("dve", _red, reads=[Pf], writes=[pg])
                return pg

            def head_V(h, pg):
                po = psF[0:65, 6, (h % 8) * 4:(h % 8) * 4 + 4]
                for tile in range(8):
                    lt = vctx[:, tile, h * 65:(h + 1) * 65]
                    rh = pg[:, tile, :]
                    P.add("pe", lambda e, lt=lt, rh=rh, tile=tile: e.matmul(po, lhsT=lt, rhs=rh, start=(tile == 0), stop=(tile == 7)), reads=[lt, rh], writes=[po])
                if h % 8 == 7:
                    srcv = mk(psF[0:65, 6, 0:32], [psF[0:65, 6, 0:32].ap[0], (4, 8), (1, 4)])
                    dst = oas[:, (h // 8) * 8:(h // 8) * 8 + 8, 4 * s_:4 * s_ + 4]
                    P.add("dve", lambda e: e.tensor_copy(out=dst, in_=srcv), reads=[psF[0:65, 6, 0:32]], writes=[dst])

            hcnt_of = {h: hcnt + h for h in range(16)}
            hcnt += 16
            pgs_ = {0: head_S(0)}
            for h in range(16):
                if h + 1 < 16:
                    pgs_[h + 1] = head_S(h + 1)
                head_V(h, pgs_[h])
            out_proj(*all_tiles5[s_])
        dps = psF[0:64, 6, 0:256]
        oasf = mk(oas, [oas.ap[0], (1, 256)])
        P.add("pe", lambda e: e.matmul(dps, lhsT=sel_f, rhs=oasf, start=True, stop=True), reads=[sel_f, oas], writes=[dps])
        P.add("dve", lambda e: e.reciprocal(out=rds, in_=dps), reads=[dps], writes=[rds])
        for hh in range(2):
            num = mk(oas[0:64, hh:hh + 1, :], [oas[0:64, hh:hh + 1, :].ap[0], (32, 8), (1, 16)])
            rdv = mk(rds[:, hh * 16:hh * 16 + 1], [rds[:, hh * 16:hh * 16 + 1].ap[0], (32, 8), (1, 16)])
            dst = oT[hh * 64:(hh + 1) * 64, :, T:NT]
            P.add("dve", lambda e, dst=dst, num=num, rdv=rdv: e.tensor_tensor(out=dst, in0=num, in1=rdv, op=ALU.mult), reads=[oas, rds], writes=[dst])
        out_proj(*all_tiles5[4])

    arenaF_tmp16 = None
    lnF = None

    stages = [lambda: ffn(0, 0, ada_bg=True), gla_phase, lambda: ffn(0, 1), kv_phase, lambda: ffn(1, 0), attn_phase, lambda: ffn(1, 1)]
    for si, fn_ in enumerate(stages):
        if si < upto:
            fn_()

    P.add("sp", lambda e: e.dma_start(out=y_fm, in_=x), reads=[x], writes=[y_fm], dma=True)

    P.finalize_and_emit(es)
    es.close()
    return nc


def _prep_inputs(inp):
    f = np.float32
    shared = {}
    wa = np.asarray(inp["w_ada"], f)
    shared["w_ada"] = np.ascontiguousarray(wa.reshape(2, KC, 128, 8, 1152).transpose(0, 3, 2, 1, 4))
    shared["b_ada"] = np.ascontiguousarray(np.asarray(inp["b_ada"], f).reshape(2, 72, 128).transpose(2, 0, 1))
    wk = np.asarray(inp["w_ada_kv"], f)
    shared["w_adakv"] = np.ascontiguousarray(wk.reshape(KC, 128, 2, 1024).transpose(2, 1, 0, 3))
    shared["b_adakv"] = np.ascontiguousarray(np.asarray(inp["b_ada_kv"], f).reshape(16, 128).T)
    g = np.asarray(inp["ln_g"], f).reshape(6, KC, 128)
    b = np.asarray(inp["ln_b"], f).reshape(6, KC, 128)
    shared["lnp"] = np.ascontiguousarray(np.stack([g, b], 0).transpose(3, 0, 1, 2))
    ups, dns = [], []
    for l in range(2):
        for nm in ("w_ffn1", "w_ffn2"):
            wu = np.asarray(inp[nm + "_up"][l], f)
            wu = wu.reshape(KC, 128, 2, NJ, 128).transpose(3, 1, 2, 0, 4)
            ups.append(wu)
            wd = np.asarray(inp[nm + "_down"][l], f)
            wd = wd.reshape(NJ, 128, KC, 128).transpose(2, 1, 0, 3)
            dns.append(wd)
    shared["w_up"] = np.ascontiguousarray(np.stack(ups, 0))
    shared["w_dn"] = np.ascontiguousarray(np.stack(dns, 0))
    wi = np.asarray(inp["w_in_a"][0], f).reshape(KC, 128, GLA_IN).transpose(1, 0, 2)
    shared["w_in_qk"] = np.ascontiguousarray(wi[:, :, 0:1024])
    shared["w_in_v"] = np.ascontiguousarray(wi[:, :, 1024:2048])
    shared["w_in_glr"] = np.ascontiguousarray(wi[:, :, 2048:2064])
    shared["w_in_r"] = np.ascontiguousarray(wi[:, :, 2064:3088])
    shared["w_outa"] = np.ascontiguousarray(np.asarray(inp["w_out_a"][0], f).reshape(KC, 128, D).transpose(1, 0, 2))
    shared["wg2"] = np.ascontiguousarray(np.asarray(inp["w_gate2_a"][0], f))
    shared["bgate"] = np.ascontiguousarray(np.asarray(inp["b_gate_a"][0], f).reshape(4, 128).T)
    shared["gonorm"] = np.ascontiguousarray(np.broadcast_to(np.asarray(inp["g_onorm_a"][0], f)[None, :], (128, 256)))
    shared["w_kv_d"] = np.ascontiguousarray(np.asarray(inp["w_kv"], f).reshape(KC, 128, 2048).transpose(1, 0, 2))
    wq = np.asarray(inp["w_q_b"][0], f).reshape(KC, 128, 3, KC, 128)
    shared["w_q_d"] = np.ascontiguousarray(wq.transpose(3, 1, 0, 2, 4).reshape(KC, 128, KC, 384))
    shared["w_ob_d"] = np.ascontiguousarray(np.asarray(inp["w_out_b"][0], f).reshape(KC, 128, D).transpose(1, 0, 2))
    cf = np.zeros((128, 720), f)
    cf[:, 0:128] = np.triu(np.ones((128, 128), f))
    cf[:, 128:640] = 1.0
    cf[:, 128:640:128] = 0.0
    cf[:, 640:656] = 1.0
    cf[:, 640:656:4] = 0.0
    cf[64, 656:720] = 1.0
    shared["cstf_d"] = cf
    cb = np.zeros((128, 736), f)
    cb[:, 0:128] = 1.0 / 1024.0
    cb[:, 128:256] = np.eye(128, dtype=f)
    ki = np.arange(128)[:, None]
    qi = np.arange(128)[None, :]
    cb[:, 256:384] = np.where(ki <= qi, 0.0, NEG)
    cb[:, 384:512] = np.where(ki >= qi, 0.0, NEG)
    for p_ in range(128):
        cb[p_, 608 + (p_ ^ 32)] = 1.0
    pidx = np.arange(128)
    for tile in range(8):
        if tile < 3:
            rho = 16 * (32 * tile + (pidx % 32)) + pidx // 32
        elif tile < 7:
            rho = 1536 + (tile - 3) * 128 + pidx
        else:
            rho = np.where(pidx < 4, 2048 + pidx, -10 ** 6)
        for g, dil in enumerate((1, 4, 16)):
            for t in range(4):
                dist = 2048 + t - rho
                ok = (dist >= 0) & (dist % dil == 0) & (dist // dil <= 128)
                cb[:, 512 + tile * 12 + g * 4 + t] = np.where(ok, 0.0, NEG)
    shared["cstb_d"] = cb
    pos = np.concatenate([np.arange(T), np.tile(16384 + np.arange(4), 4)]).astype(f)
    inv = np.power(np.float32(10000.0), -np.arange(32, dtype=f) / np.float32(32.0)).astype(f)
    ang = (pos[None, :] * inv[:, None]).astype(f).astype(np.float64)
    cosv = np.cos(ang).astype(f)
    sinv = np.sin(ang).astype(f)
    rt = np.zeros((2, 128, NT), f)
    for p in range(128):
        rt[0, p] = cosv[p % 32]
        rt[1, p] = -sinv[p % 32] if (p % 64) < 32 else sinv[p % 32]
    shared["ropet"] = rt
    per_core = []
    xp = np.asarray(inp["x_prompt"], f)
    xs = np.asarray(inp["x_sample"], f)
    cp = np.asarray(inp["c_prompt"], f)
    cs = np.asarray(inp["c_sample"], f)
    for c in range(8):
        xa = np.concatenate([xp[c], xs[4 * c:4 * c + 4].reshape(16, D)], 0)
        xin = np.ascontiguousarray(xa.T.reshape(KC, 128, NT).transpose(1, 0, 2))
        ca = np.concatenate([cp[c:c + 1], cs[4 * c:4 * c + 4]], 0)
        cin = np.ascontiguousarray(ca.T.reshape(KC, 128, 5).transpose(1, 0, 2))
        m = dict(shared)
        m["xin"] = xin
        m["cin"] = cin
        sg = np.asarray(inp["state_gla"], f)[0, 4 * c:4 * c + 4]
        m["state_in"] = np.ascontiguousarray(sg.transpose(0, 2, 1, 3))
        m["cache_k_d"] = np.asarray(inp["cache_k"], f)[4 * c:4 * c + 4].reshape(4, 2048, 1024)
        m["cache_v_d"] = np.asarray(inp["cache_v"], f)[4 * c:4 * c + 4].reshape(4, 2048, 1024)
        per_core.append(m)
    return per_core


_NC_CACHE = {}


def kernel(**inputs):
    if "nc" not in _NC_CACHE:
        _NC_CACHE["nc"] = build_program()
    nc = _NC_CACHE["nc"]
    in_maps = _prep_inputs(inputs)
    res = run_bass_kernel_spmd(nc, in_maps, core_ids=list(range(8)))
    outs = res.results
    y = np.stack([o["y_fm"] for o in outs], 0)
    y = y.transpose(0, 3, 2, 1).reshape(8, NT, D)
    y_prompt = np.ascontiguousarray(y[:, :T])
    y_sample = np.ascontiguousarray(y[:, T:].reshape(32, 4, D))
    stp = np.stack([o["st_p"] for o in outs], 0)
    state_p = np.ascontiguousarray(stp.transpose(0, 2, 1, 3))[None]
    sts = np.stack([o["st_s"] for o in outs], 0).reshape(32, 128, 4, 256)
    state_s = np.ascontiguousarray(sts.transpose(0, 2, 1, 3))[None]
    kf = np.stack([o["k_fm"] for o in outs], 0)
    kf = kf.transpose(0, 3, 2, 1).reshape(8, NT, 16, 64)
    k_p = np.ascontiguousarray(kf[:, :T])
    k_s = np.ascontiguousarray(kf[:, T:].reshape(32, 4, 16, 64))
    vt = np.stack([o["v_tm"] for o in outs], 0).reshape(8, NT, 16, 64)
    v_p = np.ascontiguousarray(vt[:, :T])
    v_s = np.ascontiguousarray(vt[:, T:].reshape(32, 4, 16, 64))
    return y_prompt, y_sample, state_p, state_s, k_p, v_p, k_s, v_s
```

```python
import numpy as np
from contextlib import ExitStack
import concourse.bass as bass
import concourse.mybir as mybir
from concourse.bass_utils import run_bass_kernel_spmd

F32 = mybir.dt.float32
BF16 = mybir.dt.bfloat16
AF = mybir.ActivationFunctionType
ALU = mybir.AluOpType
AX = mybir.AxisListType

D = 1024
KC = 8
T = 2048
NSMP = 16
NT = T + NSMP
DFF = 2816
NJ = 22
ALPHA = (2 * 2) ** 0.25
LN_EPS = 1e-5
EPSP = LN_EPS / (ALPHA * ALPHA)
GLA_IN = 3088
NEG = -30000.0
TICKN = 3


class _Op:
    __slots__ = ("eng", "fn", "deps", "dma", "slot", "dval", "ms", "mval")


class Prog:
    ENGS = ("pe", "act", "dve", "pool", "sp")

    def __init__(self, nc):
        self.nc = nc
        self.q = {k: [] for k in self.ENGS}
        self.rows = {}
        self.psum = {}
        self.live = {}
        self.rr = {"sp": 0, "pool": 0, "act": 0}
        self.dcnt = {}
        self.NSLOT = 6

    def reg(self, ap, rowsize):
        self.rows[ap.tensor.name] = rowsize

    def box(self, ap):
        name = ap.tensor.name
        dims = ap.ap
        off = int(ap.offset)
        if name in self.rows:
            R = self.rows[name]
            p0 = off // R
            f0 = off % R
            ps, pc = dims[0]
            assert ps % R == 0, (name, dims, R)
            p1 = p0 + (pc - 1) * (ps // R) + 1
            f1 = f0 + sum((c - 1) * abs(s) for s, c in dims[1:]) + 1
            assert f1 <= R, (name, dims, off, R)
            if name in self.psum:
                bs = self.psum[name]
                return (name, 0, 128, (f0 // bs) * bs, -(-f1 // bs) * bs)
            return (name, p0, p1, f0, f1)
        f1 = off + sum((c - 1) * abs(s) for s, c in dims) + 1
        return (name, 0, 1, off, f1)

    @staticmethod
    def _ov(a, b):
        return a[1] < b[2] and b[1] < a[2] and a[3] < b[4] and b[3] < a[4]

    @staticmethod
    def _inside(a, b):
        return a[1] >= b[1] and a[2] <= b[2] and a[3] >= b[3] and a[4] <= b[4]

    def add(self, eng, fn, reads=(), writes=(), dma=False):
        op = _Op()
        op.eng, op.fn, op.dma, op.ms, op.mval = eng, fn, dma, False, 0
        idx = len(self.q[eng])
        deps = set()
        if dma:
            slot = self.rr[eng] % self.NSLOT
            self.rr[eng] += 1
            c = self.dcnt.get((eng, slot), 0) + 1
            self.dcnt[(eng, slot)] = c
            op.slot, op.dval = slot, 16 * c
            ev = ("D", eng, slot, 16 * c)
        else:
            op.slot, op.dval = None, 0
            ev = ("E", eng, idx)
        rb = [self.box(a) for a in reads]
        wb = [self.box(a) for a in writes]
        for b in rb:
            isps = b[0] in self.psum
            for rec in self.live.get(b[0], ()):
                if (rec[1] == "w" or (isps and rec[2][1] != eng)) and self._ov(b, rec[0]):
                    e = rec[2]
                    if e[0] == "E" and e[1] == eng and not dma and eng == "pe":
                        continue
                    deps.add(e)
        for b in wb:
            for rec in self.live.get(b[0], ()):
                if self._ov(b, rec[0]):
                    e = rec[2]
                    if e[0] == "E" and e[1] == eng and not dma and eng == "pe":
                        continue
                    deps.add(e)
        deps.discard(ev)
        op.deps = deps
        for b in wb:
            lst = self.live.setdefault(b[0], [])
            lst[:] = [r for r in lst if not self._inside(r[0], b)]
            lst.append((b, "w", ev))
        for b in rb:
            lst = self.live.setdefault(b[0], [])
            if not dma:
                lst[:] = [r for r in lst if not (r[1] == "r" and r[2][0] == "E" and r[2][1] == eng
                                                  and self._inside(r[0], b))]
            lst.append((b, "r", ev))
        self.q[eng].append(op)
        return op

    def finalize_and_emit(self, es):
        nc = self.nc
        for eng in self.ENGS:
            for op in self.q[eng]:
                for d in op.deps:
                    if d[0] == "E":
                        self.q[d[1]][d[2]].ms = True
        for eng in self.ENGS:
            c = 0
            for op in self.q[eng]:
                if op.ms:
                    c += 1
                    op.mval = c
        esem = {eng: es.enter_context(nc.semaphore("s_" + eng)) for eng in self.ENGS}
        dsem = {}
        for (eng, slot) in sorted(self.dcnt):
            dsem[(eng, slot)] = es.enter_context(nc.semaphore("d_%s%d" % (eng, slot)))
        engobj = {"pe": "tensor", "act": "scalar", "dve": "vector", "pool": "gpsimd", "sp": "sync"}
        block = es.enter_context(nc.Block())
        prog = self

        def make(eng):
            def body(e):
                known = {}
                for op in prog.q[eng]:
                    waits = {}
                    for d in op.deps:
                        if d[0] == "E":
                            sem, val = esem[d[1]], prog.q[d[1]][d[2]].mval
                        else:
                            sem, val = dsem[(d[1], d[2])], d[3]
                        key = id(sem)
                        if key not in waits or waits[key][1] < val:
                            waits[key] = (sem, val)
                    if op.dma and op.dval > 16:
                        sem = dsem[(eng, op.slot)]
                        key = id(sem)
                        if key not in waits or waits[key][1] < op.dval - 16:
                            waits[key] = (sem, op.dval - 16)
                    for key, (sem, val) in waits.items():
                        if known.get(key, 0) < val:
                            e.wait_ge(sem, val)
                            known[key] = val
                    ins = op.fn(e)
                    if op.dma:
                        ins.then_inc(dsem[(eng, op.slot)], 16)
                    elif op.ms:
                        ins.then_inc(esem[eng], 1)
                if eng == "sp":
                    for (qe, slot), c in sorted(prog.dcnt.items()):
                        e.wait_ge(dsem[(qe, slot)], 16 * c)
                    for oe in prog.ENGS:
                        tot = sum(1 for o in prog.q[oe] if o.ms)
                        if tot and oe != "sp":
                            e.wait_ge(esem[oe], tot)
            return body

        for eng in self.ENGS:
            getattr(block, engobj[eng])(make(eng))


def mk(ap, dims):
    return bass.AP(ap.tensor, ap.offset, [list(d) for d in dims])


def split_last(ap, a, b):
    d = list(ap.ap)
    st, c = d[-1]
    assert c == a * b
    return mk(ap, d[:-1] + [(st * b, a), (st, b)])


def bcast_last(ap, n):
    return mk(ap, list(ap.ap) + [(0, n)])


class Arena:
    def __init__(self, ap, rowsize):
        self.ap = ap
        self.R = rowsize
        self.off = 0

    def reset(self, off=0):
        self.off = off

    def take(self, shape):
        n = 1
        for s in shape[1:]:
            n *= s
        o = self.off
        self.off += n
        assert self.off <= self.R, ("arena overflow", self.off, self.R)
        v = self.ap[0:shape[0], o:o + n]
        if len(shape) == 2:
            return v
        d = list(v.ap)[:1]
        st = n
        for s in shape[1:]:
            st //= s
            d.append((st, s))
        return mk(v, d)


def build_program(debug=False, upto=99, sub=99):
    nc = bass.Bass("TRN2", target_bir_lowering=False)
    es = ExitStack()
    P = Prog(nc)

    def din(name, shape, dt=F32):
        return nc.dram_tensor(name, list(shape), dt, kind="ExternalInput").ap()

    def dout(name, shape, dt=F32):
        return nc.dram_tensor(name, list(shape), dt, kind="ExternalOutput").ap()

    xin = din("xin", [128, KC, NT])
    cin = din("cin", [128, KC, 5])
    w_ada = din("w_ada", [2, 8, 128, KC, 1152])
    b_ada = din("b_ada", [128, 2, 72])
    w_adakv = din("w_adakv", [2, 128, KC, 1024])
    b_adakv = din("b_adakv", [128, 16])
    lnp = din("lnp", [128, 2, 6, KC])
    w_up = din("w_up", [4, NJ, 128, 2, KC, 128])
    w_dn = din("w_dn", [4, KC, 128, NJ, 128])
    cstf_d = din("cstf_d", [128, 720])
    cstb_d = din("cstb_d", [128, 736])
    ropet = din("ropet", [2, 128, NT])
    w_kv_d = din("w_kv_d", [128, KC, 2048])
    w_q_d = din("w_q_d", [KC, 128, KC, 384])
    w_ob_d = din("w_ob_d", [128, KC, 1024])
    cache_k = din("cache_k_d", [4, 2048, 1024])
    cache_v = din("cache_v_d", [4, 2048, 1024])
    w_in_qk = din("w_in_qk", [128, KC, 1024])
    w_in_glr = din("w_in_glr", [128, KC, 16])
    w_in_v = din("w_in_v", [128, KC, 1024])
    w_in_r = din("w_in_r", [128, KC, 1024])
    w_outa = din("w_outa", [128, KC, 1024])
    wg2_d = din("wg2", [16, 512])
    bgate_d = din("bgate", [128, 4])
    gonorm_d = din("gonorm", [128, 256])
    state_in = din("state_in", [4, 128, 4, 256])
    y_fm = dout("y_fm", [128, KC, NT])
    st_p = dout("st_p", [128, 4, 256])
    k_fm = dout("k_fm", [128, KC, NT])
    v_tm = dout("v_tm", [NT, 1024])
    kscr = nc.dram_tensor("kscr", [KC, 128, NT], BF16, kind="Internal").ap()
    vscr = nc.dram_tensor("vscr", [NT, 1040], BF16, kind="Internal").ap()
    st_s = dout("st_s", [4, 128, 4, 256])

    def sb(name, shape, dt):
        t = es.enter_context(nc.sbuf_tensor(name, list(shape), dt))
        a = t[:]
        n = 1
        for s in shape[1:]:
            n *= s
        P.reg(a, n)
        return a

    def ps(name, shape, dt):
        t = es.enter_context(nc.psum_tensor(name, list(shape), dt))
        a = t[:]
        n = 1
        for s in shape[1:]:
            n *= s
        P.reg(a, n)
        P.psum[a.tensor.name] = 512 if dt == F32 else 1024
        return a

    x = sb("x", [128, KC, NT], F32)
    AB_R = 53300
    AF_R = 7600
    arenaB = Arena(sb("arenaB", [128, AB_R], BF16), AB_R)
    arenaF = Arena(sb("arenaF", [128, AF_R], F32), AF_R)
    mods = sb("mods", [128, 2, 72, 5], F32)
    modkv = sb("modkv", [128, 16, 5], F32)
    lnp_sb = sb("lnp_sb", [128, 2, 6, KC], F32)
    cstf = sb("cstf", [128, 720], F32)
    cstb = sb("cstb", [128, 736], BF16)
    sct = sb("sct", [128, KC, 5], BF16)
    cin_sb = sb("cin_sb", [128, KC, 5], F32)
    bada_sb = sb("bada_sb", [128, 2, 72], F32)
    badakv_sb = sb("badakv_sb", [128, 16], F32)
    psF = ps("psF", [128, 7, 512], F32)
    psB = ps("psB", [128, 1024], BF16)

    ones_b = cstb[:, 0:128]
    ident_b = cstb[:, 128:256]
    causal_f = cstf[:, 0:128]
    scanm_p = cstf[:, 128:640]
    scanm_s = cstf[:, 640:656]
    sel_f = cstf[0:65, 656:720]
    maskp_b = cstb[:, 256:512]
    pswap_b = cstb[:, 608:736]
    masks_b = cstb[:, 512:608]

    P.add("sp", lambda e: e.dma_start(out=x, in_=xin), reads=[xin], writes=[x], dma=True)
    P.add("sp", lambda e: e.dma_start(out=cin_sb, in_=cin), reads=[cin], writes=[cin_sb], dma=True)
    P.add("sp", lambda e: e.dma_start(out=lnp_sb, in_=lnp), reads=[lnp], writes=[lnp_sb], dma=True)
    P.add("sp", lambda e: e.dma_start(out=cstf, in_=cstf_d), reads=[cstf_d], writes=[cstf], dma=True)
    P.add("sp", lambda e: e.dma_start(out=bada_sb, in_=b_ada), reads=[b_ada], writes=[bada_sb], dma=True)
    P.add("sp", lambda e: e.dma_start(out=badakv_sb, in_=b_adakv), reads=[b_adakv], writes=[badakv_sb], dma=True)
    P.add("pool", lambda e: e.dma_start(out=cstb, in_=cstb_d), reads=[cstb_d], writes=[cstb], dma=True)
    P.add("act", lambda e: e.activation(out=sct, in_=cin_sb, func=AF.Silu), reads=[cin_sb], writes=[sct])

    DERIVE_ALL = ((1, None), (4, None), (7, None), (2, 0.5 / ALPHA), (5, 1.0 / ALPHA), (8, 0.5 / ALPHA))

    def derive(mt, entries=DERIVE_ALL):
        for i, coef in entries:
            v = mt[:, i * 8:(i + 1) * 8, :]
            if coef is None:
                P.add("dve", lambda e, v=v: e.tensor_scalar_add(out=v, in0=v, scalar1=1.0), reads=[v], writes=[v])
            else:
                P.add("dve", lambda e, v=v, coef=coef: e.tensor_scalar(out=v, in0=v, scalar1=1.0, scalar2=coef,
                                                                      op0=ALU.add, op1=ALU.mult), reads=[v], writes=[v])

    def ada_gen(l, bufs, noc, bank, pcs=range(8), entries=DERIVE_ALL):
        nb_ = 0
        for pc in pcs:
            for sp_ in range(9 // noc):
                wb_ = bufs[nb_ % 2]
                nb_ += 1
                src = w_ada[l, pc][:, :, sp_ * noc * 128:(sp_ + 1) * noc * 128]
                P.add("pool", lambda e, wb_=wb_, src=src: e.dma_start(out=wb_, in_=src),
                      reads=[src], writes=[wb_], dma=True)
                for ol in range(noc):
                    oc = pc * 9 + sp_ * noc + ol
                    for k in range(KC):
                        o = psF[:, bank, oc * 5:oc * 5 + 5]
                        lt = wb_[:, k, ol * 128:(ol + 1) * 128]
                        rh = sct[:, k, :]
                        P.add("pe", lambda e, o=o, lt=lt, rh=rh, k=k: e.matmul(o, lhsT=lt, rhs=rh, start=(k == 0), stop=(k == KC - 1)),
                              reads=[lt, rh], writes=[o])
                yield 1
        o0, o1 = min(pcs) * 9, (max(pcs) + 1) * 9
        pv = mk(psF[:, bank, o0 * 5:o1 * 5], [psF[:, bank, o0 * 5:o1 * 5].ap[0], (5, o1 - o0), (1, 5)])
        bb = bcast_last(bada_sb[:, l, o0:o1], 5)
        mo = mods[:, l, o0:o1, :]
        P.add("dve", lambda e, mo=mo, pv=pv, bb=bb: e.tensor_tensor(out=mo, in0=pv, in1=bb, op=ALU.add),
              reads=[pv, bada_sb[:, l, o0:o1]], writes=[mo])
        derive(mods[:, l], entries)
        yield 0

    arenaB.reset()
    wada_buf = [arenaB.take([128, KC, 1152]) for _ in range(2)]
    for _ in ada_gen(0, wada_buf, 9, 0, pcs=range(0, 3), entries=DERIVE_ALL[0:1] + DERIVE_ALL[3:4]):
        pass
    nb = 0
    for pc in range(2):
        wb_ = wada_buf[nb % 2][:, :, 0:1024]
        nb += 1
        src = w_adakv[pc]
        P.add("pool", lambda e, wb_=wb_, src=src: e.dma_start(out=wb_, in_=src), reads=[src], writes=[wb_], dma=True)
        for ol in range(8):
            oc = pc * 8 + ol
            for k in range(KC):
                o = psF[:, 2, oc * 5:oc * 5 + 5]
                lt = wb_[:, k, ol * 128:(ol + 1) * 128]
                rh = sct[:, k, :]
                P.add("pe", lambda e, o=o, lt=lt, rh=rh, k=k: e.matmul(o, lhsT=lt, rhs=rh, start=(k == 0), stop=(k == KC - 1)),
                      reads=[lt, rh], writes=[o])
    pv = mk(psF[:, 2, 0:80], [psF[:, 2, 0:80].ap[0], (5, 16), (1, 5)])
    bb = bcast_last(badakv_sb, 5)
    P.add("dve", lambda e, pv=pv, bb=bb: e.tensor_tensor(out=modkv, in0=pv, in1=bb, op=ALU.add),
          reads=[pv, badakv_sb], writes=[modkv])
    v = modkv[:, 8:16, :]
    P.add("dve", lambda e, v=v: e.tensor_scalar_add(out=v, in0=v, scalar1=1.0), reads=[v], writes=[v])

    halves = [[(0, 512), (512, 512)], [(1024, 512), (1536, 512), (2048, 16)]]

    def modulate(dst, h0, cols, mod_t, sh_oc, sc_oc, eng="act"):
        for (lo, n) in cols:
            if lo < T:
                for k in range(KC):
                    o = dst[:, k, lo - h0:lo - h0 + n]
                    i_ = x[:, k, lo:lo + n]
                    b_ = mod_t[:, sh_oc + k, 0:1]
                    s_ = mod_t[:, sc_oc + k, 0:1]
                    if eng == "act":
                        P.add("act", lambda e, o=o, i_=i_, b_=b_, s_=s_: e.activation(out=o, in_=i_, func=AF.Identity, bias=b_, scale=s_),
                              reads=[i_, b_, s_], writes=[o])
                    else:
                        P.add("dve", lambda e, o=o, i_=i_, b_=b_, s_=s_: e.tensor_scalar(out=o, in0=i_, scalar1=s_, scalar2=b_, op0=ALU.mult, op1=ALU.add),
                              reads=[i_, b_, s_], writes=[o])
            else:
                o = dst[:, :, lo - h0:lo - h0 + n]
                i_ = x[:, :, lo:lo + n]
                tmp = arenaF_tmp16
                s_ = mod_t[:, sc_oc:sc_oc + 8, 1:5]
                b_ = mod_t[:, sh_oc:sh_oc + 8, 1:5]
                P.add("dve", lambda e, i_=i_, s_=s_, tmp=tmp: e.tensor_tensor(out=split_last(tmp, 4, 4), in0=split_last(i_, 4, 4),
                                                                          in1=bcast_last(s_, 4), op=ALU.mult),
                      reads=[i_, s_], writes=[tmp])
                P.add("dve", lambda e, o=o, b_=b_, tmp=tmp: e.tensor_tensor(out=split_last(o, 4, 4), in0=split_last(tmp, 4, 4),
                                                                        in1=bcast_last(b_, 4), op=ALU.add),
                      reads=[tmp, b_], writes=[o])

    def residual_add(Y, c, lo, n, mod_t, g_oc):
        xs = x[:, c, lo:lo + n]
        if lo < T:
            g_ = mod_t[:, g_oc + c, 0:1]
            P.add("dve", lambda e, xs=xs, Y=Y, g_=g_: e.scalar_tensor_tensor(out=xs, in0=Y, scalar=g_, in1=xs, op0=ALU.mult, op1=ALU.add),
                  reads=[Y, g_, xs], writes=[xs])
        else:
            g_ = mod_t[:, g_oc + c, 1:5]
            tmp = arenaF_tmp16[:, 0, :]
            P.add("dve", lambda e, Y=Y, g_=g_, tmp=tmp: e.tensor_tensor(out=split_last(tmp, 4, 4), in0=split_last(Y, 4, 4),
                                                                    in1=bcast_last(g_, 4), op=ALU.mult),
                  reads=[Y, g_], writes=[tmp])
            P.add("dve", lambda e, xs=xs, tmp=tmp: e.tensor_tensor(out=xs, in0=xs, in1=tmp, op=ALU.add),
                  reads=[xs, tmp], writes=[xs])

    class LNState:
        pass

    def ln_begin(tiles):
        st = LNState()
        st.tiles = tiles
        st.mu = {}
        st.e2 = {}
        pi = 0
        for (lo, n) in tiles:
            if lo < T:
                st.mu[lo] = psF[:, 2 * pi, 0:n]
                st.e2[lo] = psF[:, 2 * pi + 1, 0:n]
                pi += 1
            else:
                st.mu[lo] = psF[:, 6, 0:n]
                st.e2[lo] = psF[:, 6, n:2 * n]
        st.pending = []
        return st

    def ln_accum(st, c, lo, n, zb, zq):
        xs = x[:, c, lo:lo + n]
        zb_ = zb[:, 0:n]
        zq_ = zq[:, 0:n] if lo < T else zb[:, n:2 * n]
        P.add("act", lambda e, zb_=zb_, xs=xs: e.activation(out=zb_, in_=xs, func=AF.Identity), reads=[xs], writes=[zb_])
        P.add("act", lambda e, zq_=zq_, xs=xs: e.activation(out=zq_, in_=xs, func=AF.Square), reads=[xs], writes=[zq_])
        mu, e2 = st.mu[lo], st.e2[lo]

        def later():
            if lo >= T:
                both = zb[:, 0:2 * n]
                o2 = psF[:, 6, 0:2 * n]
                P.add("pe", lambda e: e.matmul(o2, lhsT=ones_b, rhs=both, start=(c == 0), stop=(c == KC - 1)), reads=[ones_b, both], writes=[o2])
                return
            P.add("pe", lambda e: e.matmul(mu, lhsT=ones_b, rhs=zb_, start=(c == 0), stop=(c == KC - 1)), reads=[ones_b, zb_], writes=[mu])
            P.add("pe", lambda e: e.matmul(e2, lhsT=ones_b, rhs=zq_, start=(c == 0), stop=(c == KC - 1)), reads=[ones_b, zq_], writes=[e2])
        st.pending.append(later)

    def ln_flush(st, keep=0):
        while len(st.pending) > keep:
            st.pending.pop(0)()

    def ln_finish(st, li, defer=False, aff="act"):
        ln_flush(st)
        per = []
        for j, (lo, n) in enumerate(st.tiles):
            mu_ps, e2_ps = st.mu[lo], st.e2[lo]
            mu_sb = lnF["mu"][j][:, 0:n]
            m2 = lnF["va"][j][:, 0:n]
            rstd = lnF["rs"][j][:, 0:n]
            P.add("act", lambda e, mu_sb=mu_sb, mu_ps=mu_ps: e.activation(out=mu_sb, in_=mu_ps, func=AF.Identity), reads=[mu_ps], writes=[mu_sb])
            P.add("act", lambda e, m2=m2, mu_ps=mu_ps: e.activation(out=m2, in_=mu_ps, func=AF.Square), reads=[mu_ps], writes=[m2])
            P.add("dve", lambda e, m2=m2, e2_ps=e2_ps: e.tensor_tensor(out=m2, in0=e2_ps, in1=m2, op=ALU.subtract), reads=[e2_ps, m2], writes=[m2])
            P.add("act", lambda e, m2=m2, rstd=rstd: e.activation(out=rstd, in_=m2, func=AF.Sqrt, bias=EPSP, scale=1.0),
                  reads=[m2], writes=[rstd])
            P.add("dve", lambda e, rstd=rstd: e.reciprocal(out=rstd, in_=rstd), reads=[rstd], writes=[rstd])
            per.append((lo, n, mu_sb, rstd))
        tbufs = lnF["t"]

        def pass2():
            k2 = 0
            for (lo, n, mu_sb, rstd) in per:
                for c in range(KC):
                    xs = x[:, c, lo:lo + n]
                    t1 = tbufs[k2 % 2][:, 0:n]
                    k2 += 1
                    g_ = lnp_sb[:, 0, li, c:c + 1]
                    b_ = lnp_sb[:, 1, li, c:c + 1]
                    P.add("dve", lambda e, t1=t1, xs=xs, mu_sb=mu_sb: e.tensor_tensor(out=t1, in0=xs, in1=mu_sb, op=ALU.subtract), reads=[xs, mu_sb], writes=[t1])
                    P.add("dve", lambda e, t1=t1, rstd=rstd: e.tensor_tensor(out=t1, in0=t1, in1=rstd, op=ALU.mult), reads=[t1, rstd], writes=[t1])
                    if aff == "act":
                        P.add("act", lambda e, xs=xs, t1=t1, g_=g_, b_=b_: e.activation(out=xs, in_=t1, func=AF.Identity, bias=b_, scale=g_),
                              reads=[t1, g_, b_], writes=[xs])
                    else:
                        P.add("dve", lambda e, xs=xs, t1=t1, g_=g_, b_=b_: e.tensor_scalar(out=xs, in0=t1, scalar1=g_, scalar2=b_, op0=ALU.mult, op1=ALU.add),
                              reads=[t1, g_, b_], writes=[xs])
                    yield 1
        gen2 = pass2()
        if defer:
            return gen2
        for _ in gen2:
            pass
        return None

    def ln_bufs(ntile_p, with_sample):
        d = {"mu": [], "va": [], "rs": [], "t": []}
        for _ in range(ntile_p):
            for kname in ("mu", "va", "rs"):
                d[kname].append(arenaF.take([128, 512]))
        if with_sample:
            for kname in ("mu", "va", "rs"):
                d[kname].append(arenaF.take([128, 16]))
        d["t"] = [arenaF.take([128, 512]) for _ in range(2)]
        return d

    def ffn(l, which, ada_bg=False):
        fi = l * 2 + which
        li = l * 3 + (0 if which == 0 else 2)
        mod_t = mods[:, l]
        sh_oc, sc_oc, g_oc = ((0, 8, 16) if which == 0 else (48, 56, 64))
        arenaB.reset()
        hbuf = arenaB.take([128, KC, 1040])
        gbuf = arenaB.take([128, NJ, 1040])
        wup = [arenaB.take([128, 2, KC, 128]) for _ in range(3)]
        wdn = [arenaB.take([128, NJ, 128]) for _ in range(2)]
        zbq = [arenaB.take([128, 512]) for _ in range(6)]
        bg = None
        if ada_bg:
            bg = ada_gen(0, [arenaB.take([128, KC, 384]) for _ in range(2)], 3, 6, pcs=range(3, 8),
                         entries=DERIVE_ALL[1:3] + DERIVE_ALL[4:6])
        arenaF.reset()
        nonlocal arenaF_tmp16, lnF
        arenaF_tmp16 = arenaF.take([128, KC, 16])
        sa = [arenaF.take([128, 512]) for _ in range(2)]
        lnF = ln_bufs(2, True)
        cnt = 0
        wcnt = 0
        dcnt_ = 0
        pend_ln = None
        for hi, tiles in enumerate(halves):
            h0 = tiles[0][0]
            if hi == 0:
                modulate(hbuf, h0, tiles, mod_t, sh_oc, sc_oc)
            for j in range(NJ):
                wb_ = wup[wcnt % 3]
                wcnt += 1
                src = w_up[fi, j]
                P.add("pool", lambda e, wb_=wb_, src=src: e.dma_start(out=wb_, in_=src), reads=[src], writes=[wb_], dma=True)
                for (lo, n) in tiles:
                    pa = psF[:, cnt % 2, 0:n]
                    pu = psF[:, 2 + cnt % 2, 0:n]
                    sa_ = sa[cnt % 2][:, 0:n]
                    cnt += 1
                    for (o, a_or_u) in ((pa, 0), (pu, 1)):
                        for k in range(KC):
                            lt = wb_[:, a_or_u, k, :]
                            rh = hbuf[:, k, lo - h0:lo - h0 + n]
                            P.add("pe", lambda e, o=o, lt=lt, rh=rh, k=k: e.matmul(o, lhsT=lt, rhs=rh, start=(k == 0), stop=(k == KC - 1)),
                                  reads=[lt, rh], writes=[o])
                    P.add("act", lambda e, sa_=sa_, pa=pa: e.activation(out=sa_, in_=pa, func=AF.Silu), reads=[pa], writes=[sa_])
                    go = gbuf[:, j, lo - h0:lo - h0 + n]
                    P.add("dve", lambda e, go=go, sa_=sa_, pu=pu: e.tensor_tensor(out=go, in0=pu, in1=sa_, op=ALU.mult), reads=[pu, sa_], writes=[go])
                    if pend_ln is not None:
                        next(pend_ln, None)
                    if bg is not None:
                        next(bg, None)
            if hi == 0:
                modulate(hbuf, halves[1][0][0], halves[1], mod_t, sh_oc, sc_oc)
            st = ln_begin(tiles)
            zc = 0
            for c in range(KC):
                wb_ = wdn[dcnt_ % 2]
                dcnt_ += 1
                src = w_dn[fi, c]
                P.add("pool", lambda e, wb_=wb_, src=src: e.dma_start(out=wb_, in_=src), reads=[src], writes=[wb_], dma=True)
                for (lo, n) in tiles:
                    Y = psF[:, 4 + zc % 2, 0:n]
                    for kk in range(NJ):
                        lt = wb_[:, kk, :]
                        rh = gbuf[:, kk, lo - h0:lo - h0 + n]
                        P.add("pe", lambda e, Y=Y, lt=lt, rh=rh, kk=kk: e.matmul(Y, lhsT=lt, rhs=rh, start=(kk == 0), stop=(kk == NJ - 1)),
                              reads=[lt, rh], writes=[Y])
                    ln_flush(st, keep=0)
                    residual_add(Y, c, lo, n, mod_t, g_oc)
                    ln_accum(st, c, lo, n, zbq[(zc % 3) * 2], zbq[(zc % 3) * 2 + 1])
                    zc += 1
            if pend_ln is not None:
                for _ in pend_ln:
                    pass
            pend_ln = ln_finish(st, li, defer=(hi == 0))
        if bg is not None:
            for _ in bg:
                pass


    def gla_phase():
        nonlocal arenaF_tmp16, lnF
        mod_t = mods[:, 0]
        sh_oc, sc_oc, g_oc, li = 24, 32, 40, 1
        arenaB.reset()
        arenaF.reset()
        wqk = arenaB.take([128, KC, 1024])
        wglr = arenaB.take([128, KC, 16])
        wbuf = [arenaB.take([128, KC, 1024]) for _ in range(2)]
        ht = arenaB.take([128, KC, 512])
        qk = arenaB.take([128, 8, 512])
        vtm = arenaB.take([128, 4, 1024])
        ATs = [arenaB.take([128, 4, 128]) for _ in range(2)]
        ktm = [arenaB.take([128, 4, 128]) for _ in range(2)]
        Sbf = [arenaB.take([128, 4, 256]) for _ in range(2)]
        on = [arenaB.take([128, 1024]) for _ in range(2)]
        sR = arenaB.take([128, KC, 512])
        zbq = [arenaB.take([128, 512]) for _ in range(6)]
        arenaF_tmp16 = arenaF.take([128, KC, 16])
        tf = [arenaF.take([128, 512]) for _ in range(8)]
        lnF = {"mu": [tf[3], tf[3]], "va": [tf[4], tf[4]], "rs": [tf[5], tf[5]], "t": [tf[6], tf[7]]}
        S = arenaF.take([128, 4, 256])
        S1 = arenaF.take([128, 4, 256])
        eLs = arenaF.take([128, 4, 4])
        rstd4 = arenaF.take([128, 4])
        ssq = arenaF.take([128, 4])
        nbg = arenaF.take([128, 4])
        gB = arenaF.take([128, 256])
        wg2 = arenaF.take([16, 512])
        glrT = arenaF.take([16, 512])
        DKS = 128 ** -0.5

        P.add("pool", lambda e: e.dma_start(out=wqk, in_=w_in_qk), reads=[w_in_qk], writes=[wqk], dma=True)
        P.add("pool", lambda e: e.dma_start(out=wglr, in_=w_in_glr), reads=[w_in_glr], writes=[wglr], dma=True)
        P.add("sp", lambda e: e.dma_start(out=wg2, in_=wg2_d), reads=[wg2_d], writes=[wg2], dma=True)
        P.add("sp", lambda e: e.dma_start(out=nbg, in_=bgate_d), reads=[bgate_d], writes=[nbg], dma=True)
        P.add("sp", lambda e: e.dma_start(out=gB, in_=gonorm_d), reads=[gonorm_d], writes=[gB], dma=True)
        P.add("dve", lambda e: e.tensor_scalar_mul(out=nbg, in0=nbg, scalar1=-1.0), reads=[nbg], writes=[nbg])
        P.add("dve", lambda e: e.memset(S, 0.0), writes=[S])
        P.add("dve", lambda e: e.memset(Sbf[0], 0.0), writes=[Sbf[0]])
        state = {"wb": 0, "sb": 0, "ab": 0, "y": 0}

        def load_w(src):
            wb_ = wbuf[state["wb"] % 2]
            state["wb"] += 1
            P.add("pool", lambda e: e.dma_start(out=wb_, in_=src), reads=[src], writes=[wb_], dma=True)
            return wb_

        def chunk(cs, nt, vt, ck):
            i2 = state["ab"] % 2
            state["ab"] += 1
            A_, K_, on_ = ATs[i2], ktm[i2], on[i2]
            Scur = Sbf[state["sb"] % 2]
            Snext = Sbf[(state["sb"] + 1) % 2]
            state["sb"] += 1
            atp = psF[0:nt, 6, :]
            for h in range(4):
                o_ = atp[:, h * 128:h * 128 + nt]
                lt = qk[:, 4 + h, cs]
                rh = qk[:, h, cs]
                P.add("pe", lambda e, o_=o_, lt=lt, rh=rh: e.matmul(o_, lhsT=lt, rhs=rh, start=True, stop=True), reads=[lt, rh], writes=[o_])
            atv = mk(atp, [atp.ap[0], (128, 4), (1, nt)])
            av = A_[0:nt, :, 0:nt]
            cm = mk(causal_f[0:nt, 0:nt], [causal_f[0:nt, 0:nt].ap[0], (0, 4), (1, nt)])
            P.add("dve", lambda e: e.tensor_tensor(out=av, in0=atv, in1=cm, op=ALU.mult), reads=[atv, causal_f[0:nt, 0:nt]], writes=[av])
            for h in range(4):
                o_ = psB[0:nt, h * 128:(h + 1) * 128]
                i_ = qk[:, 4 + h, cs]
                P.add("pe", lambda e, o_=o_, i_=i_: e.transpose(o_, i_, ident_b), reads=[i_, ident_b], writes=[o_])
            kv_ = K_[0:nt]
            pb = mk(psB[0:nt, 0:512], [psB[0:nt, 0:512].ap[0], (128, 4), (1, 128)])
            P.add("act", lambda e: e.activation(out=kv_, in_=pb, func=AF.Identity), reads=[pb], writes=[kv_])
            ob_ = 4 if (state["ab"] % 2 == 1) else 0
            o_ps = mk(psF[0:nt, ob_, :], [psF[0:nt, ob_, :].ap[0], (256, 4), (1, 256)])
            for h in range(4):
                oh = o_ps[:, h, :]
                l1 = A_[0:nt, h, 0:nt]
                r1 = vt[0:nt, h * 256:(h + 1) * 256]
                l2 = qk[:, h, cs]
                r2 = Scur[:, h, :]
                P.add("pe", lambda e, oh=oh, l1=l1, r1=r1: e.matmul(oh, lhsT=l1, rhs=r1, start=True, stop=False), reads=[l1, r1], writes=[oh])
                P.add("pe", lambda e, oh=oh, l2=l2, r2=r2: e.matmul(oh, lhsT=l2, rhs=r2, start=False, stop=True), reads=[l2, r2], writes=[oh])
            u_ps = mk(psF[:, 2, :], [psF[:, 2, :].ap[0], (256, 4), (1, 256)])
            for h in range(4):
                uh = u_ps[:, h, :]
                l1 = K_[0:nt, h, :]
                r1 = vt[0:nt, h * 256:(h + 1) * 256]
                P.add("pe", lambda e, uh=uh, l1=l1, r1=r1: e.matmul(uh, lhsT=l1, rhs=r1, start=True, stop=True), reads=[l1, r1], writes=[uh])
            for h in range(4):
                el = eLs[:, h, ck:ck + 1]
                s_h, s1_h, uh = S[:, h, :], S1[:, h, :], u_ps[:, h, :]
                P.add("act", lambda e, s_h=s_h, s1_h=s1_h, el=el: e.activation(out=s1_h, in_=s_h, func=AF.Identity, scale=el), reads=[s_h, el], writes=[s1_h])
                P.add("dve", lambda e, s_h=s_h, s1_h=s1_h, uh=uh, el=el: e.scalar_tensor_tensor(out=s_h, in0=uh, scalar=el, in1=s1_h, op0=ALU.mult, op1=ALU.add),
                      reads=[uh, el, s1_h], writes=[s_h])
            P.add("act", lambda e: e.activation(out=Snext, in_=S, func=AF.Identity), reads=[S], writes=[Snext])
            def part_b():
                sq = mk(tf[0][0:nt, :], [tf[0][0:nt, :].ap[0], (1, 512)])
                sqv = mk(tf[0][0:nt, 0:1], [tf[0][0:nt, 0:1].ap[0], (256, 4), (1, 256)])
                sq_box = [tf[0][0:nt, :], tf[1][0:nt, :]]
                P.add("act", lambda e: e.activation(out=sqv, in_=o_ps, func=AF.Square), reads=[o_ps], writes=sq_box)
                ss = ssq[0:nt, :]
                rs = rstd4[0:nt, :]
                P.add("dve", lambda e: e.tensor_reduce(out=ss, in_=sqv, axis=AX.X, op=ALU.add), reads=sq_box, writes=[ss])
                P.add("act", lambda e: e.activation(out=rs, in_=ss, func=AF.Sqrt, bias=1e-6, scale=1.0 / 256.0), reads=[ss], writes=[rs])
                P.add("dve", lambda e: e.reciprocal(out=rs, in_=rs), reads=[rs], writes=[rs])
                for h in range(4):
                    oh = o_ps[:, h, :]
                    onh = on_[0:nt, h * 256:(h + 1) * 256]
                    r_ = rstd4[0:nt, h:h + 1]
                    g_ = gB[0:nt, :]
                    P.add("dve", lambda e, oh=oh, onh=onh, r_=r_, g_=g_: e.scalar_tensor_tensor(out=onh, in0=oh, scalar=r_, in1=g_, op0=ALU.mult, op1=ALU.mult),
                          reads=[oh, r_, g_], writes=[onh])
                for c in range(KC):
                    o_ = psB[:, c * 128:c * 128 + nt]
                    i_ = on_[0:nt, c * 128:(c + 1) * 128]
                    idn = ident_b[0:nt, 0:nt]
                    P.add("pe", lambda e, o_=o_, i_=i_, idn=idn: e.transpose(o_, i_, idn), reads=[i_, idn], writes=[o_])
                pbv = mk(psB[:, 0:1], [psB[:, 0:1].ap[0], (128, KC), (1, nt)])
                srv = sR[:, :, cs]
                P.add("dve", lambda e: e.tensor_tensor(out=srv, in0=pbv, in1=srv, op=ALU.mult), reads=[psB[:, :], srv], writes=[srv])
            return part_b

        all_tiles = [(0, 512), (512, 512), (1024, 512), (1536, 512), (2048, 16)]
        def st_laqk(ti):
            lo, n = all_tiles[ti]
            smp = lo >= T
            nch = 4
            ctok = 4 if smp else 128
            if ti == 0:
                modulate(ht, lo, [(lo, n)], mod_t, sh_oc, sc_oc)
            gp = psF[0:16, 6, 0:n]
            for k in range(KC):
                lt = wglr[:, k, :]
                rh = ht[:, k, 0:n]
                P.add("pe", lambda e, gp=gp, lt=lt, rh=rh, k=k: e.matmul(gp, lhsT=lt, rhs=rh, start=(k == 0), stop=(k == KC - 1)), reads=[lt, rh], writes=[gp])
            gl = glrT[:, 0:n]
            P.add("act", lambda e, gl=gl, gp=gp: e.activation(out=gl, in_=gp, func=AF.Identity), reads=[gp], writes=[gl])
            for h in range(4):
                ta, tb, tq, tk = [tf[(h % 2) * 4 + i][:, 0:n] for i in range(4)]
                lp = psF[:, h % 2, 0:n]
                lt = wg2[:, h * 128:(h + 1) * 128]
                P.add("pe", lambda e, lp=lp, lt=lt, gl=gl: e.matmul(lp, lhsT=lt, rhs=gl, start=True, stop=True), reads=[lt, gl], writes=[lp])
                nb_ = nbg[:, h:h + 1]
                P.add("act", lambda e, ta=ta, lp=lp, nb_=nb_: e.activation(out=ta, in_=lp, func=AF.Exp, bias=nb_, scale=-1.0), reads=[lp, nb_], writes=[ta])
                P.add("act", lambda e, ta=ta: e.activation(out=ta, in_=ta, func=AF.Ln, bias=1.0, scale=1.0), reads=[ta], writes=[ta])
                sm = (scanm_s if smp else scanm_p)[:, 0:n]
                P.add("dve", lambda e, tb=tb, ta=ta, sm=sm: e.tensor_tensor_scan(out=tb, data0=sm, data1=ta, initial=0.0, op0=ALU.mult, op1=ALU.add),
                      reads=[ta, sm], writes=[tb])
                P.add("act", lambda e, tq=tq, tb=tb: e.activation(out=tq, in_=tb, func=AF.Exp, scale=-1.0 / 16.0), reads=[tb], writes=[tq])
                P.add("act", lambda e, tk=tk, tb=tb: e.activation(out=tk, in_=tb, func=AF.Exp, scale=1.0 / 16.0), reads=[tb], writes=[tk])
                cl = 4 if smp else 128
                src_ = mk(tq[:, cl - 1:cl], [tq[:, cl - 1:cl].ap[0], (cl, 4)])
                dst_ = eLs[:, h, :]
                P.add("act", lambda e, src_=src_, dst_=dst_: e.activation(out=dst_, in_=src_, func=AF.Identity), reads=[tq], writes=[dst_])
                qp = psF[:, 2 + 2 * (h % 2), 0:n]
                kp = psF[:, 3 + 2 * (h % 2), 0:n]
                for (o_, cb) in ((qp, h * 128), (kp, 512 + h * 128)):
                    for k in range(KC):
                        lt = wqk[:, k, cb:cb + 128]
                        rh = ht[:, k, 0:n]
                        P.add("pe", lambda e, o_=o_, lt=lt, rh=rh, k=k: e.matmul(o_, lhsT=lt, rhs=rh, start=(k == 0), stop=(k == KC - 1)), reads=[lt, rh], writes=[o_])
                qo = qk[:, h, 0:n]
                ko = qk[:, 4 + h, 0:n]
                P.add("dve", lambda e, qo=qo, qp=qp, tq=tq: e.scalar_tensor_tensor(out=qo, in0=qp, scalar=DKS, in1=tq, op0=ALU.mult, op1=ALU.mult), reads=[qp, tq], writes=[qo])
                P.add("dve", lambda e, ko=ko, kp=kp, tk=tk: e.tensor_tensor(out=ko, in0=kp, in1=tk, op=ALU.mult), reads=[kp, tk], writes=[ko])

        def st_vr(ti):
            lo, n = all_tiles[ti]
            smp = lo >= T
            nch = 4
            ctok = 4 if smp else 128
            wv_ = load_w(w_in_v)
            nch = 4
            ctok = 4 if smp else 128
            for ck in range(nch):
                vp = mk(psF[0:ctok, 4, :], [psF[0:ctok, 4, :].ap[0], (1, 1024)])
                for hf in range(2):
                    o_ = psF[0:ctok, 4 + hf, :]
                    for k in range(KC):
                        lt = ht[:, k, ck * ctok:(ck + 1) * ctok]
                        rh = wv_[:, k, hf * 512:(hf + 1) * 512]
                        P.add("pe", lambda e, o_=o_, lt=lt, rh=rh, k=k: e.matmul(o_, lhsT=lt, rhs=rh, start=(k == 0), stop=(k == KC - 1)), reads=[lt, rh], writes=[o_])
                vo = vtm[0:ctok, ck, :]
                P.add("act", lambda e, vo=vo, vp=vp: e.activation(out=vo, in_=vp, func=AF.Identity), reads=[psF[0:ctok, 4:6, :]], writes=[vo])
            wr_ = load_w(w_in_r)
            for c in range(KC):
                rp = psF[:, 2 + c % 2, 0:n]
                for k in range(KC):
                    lt = wr_[:, k, c * 128:(c + 1) * 128]
                    rh = ht[:, k, 0:n]
                    P.add("pe", lambda e, rp=rp, lt=lt, rh=rh, k=k: e.matmul(rp, lhsT=lt, rhs=rh, start=(k == 0), stop=(k == KC - 1)), reads=[lt, rh], writes=[rp])
                so = sR[:, c, 0:n]
                P.add("act", lambda e, so=so, rp=rp: e.activation(out=so, in_=rp, func=AF.Silu), reads=[rp], writes=[so])
            if ti + 1 < len(all_tiles):
                nlo, nn = all_tiles[ti + 1]
                modulate(ht, nlo, [(nlo, nn)], mod_t, sh_oc, sc_oc)

        def st_rec(ti):
            lo, n = all_tiles[ti]
            smp = lo >= T
            nch = 4
            ctok = 4 if smp else 128
            pend_b = None
            for ck in range(nch):
                if smp:
                    src = state_in[ck]
                    P.add("sp", lambda e, src=src: e.dma_start(out=S, in_=src), reads=[src], writes=[S], dma=True)
                    Snx = Sbf[state["sb"] % 2]
                    P.add("act", lambda e, Snx=Snx: e.activation(out=Snx, in_=S, func=AF.Identity), reads=[S], writes=[Snx])
                nb_ = chunk(slice(ck * ctok, (ck + 1) * ctok), ctok, vtm[:, ck, :], ck)
                if smp:
                    dst = st_s[ck]
                    P.add("sp", lambda e, dst=dst: e.dma_start(out=dst, in_=S), reads=[S], writes=[dst], dma=True)
                if pend_b is not None:
                    pend_b()
                pend_b = nb_
            pend_b()
            if ti == 3:
                P.add("sp", lambda e: e.dma_start(out=st_p, in_=S), reads=[S], writes=[st_p], dma=True)

        def st_out(ti):
            lo, n = all_tiles[ti]
            smp = lo >= T
            nch = 4
            ctok = 4 if smp else 128
            wo_ = load_w(w_outa)
            st = ln_begin([(lo, n)])
            zc = 0
            for c in range(KC):
                Y = psF[:, 2 + c % 2, 0:n]
                for k in range(KC):
                    lt = wo_[:, k, c * 128:(c + 1) * 128]
                    rh = sR[:, k, 0:n]
                    P.add("pe", lambda e, Y=Y, lt=lt, rh=rh, k=k: e.matmul(Y, lhsT=lt, rhs=rh, start=(k == 0), stop=(k == KC - 1)), reads=[lt, rh], writes=[Y])
                ln_flush(st, keep=0)
                residual_add(Y, c, lo, n, mod_t, g_oc)
                ln_accum(st, c, lo, n, zbq[(zc % 3) * 2], zbq[(zc % 3) * 2 + 1])
                zc += 1
            ln_finish(st, li, aff="dve")

        nT = len(all_tiles)
        st_laqk(0)
        st_vr(0)
        for ti in range(nT):
            st_rec(ti)
            if ti + 1 < nT:
                st_laqk(ti + 1)
            st_out(ti)
            if ti + 1 < nT:
                st_vr(ti + 1)


    KS_OFF, QS_OFF = 52700, 52830
    ks_all = mk(arenaB.ap[:, KS_OFF:KS_OFF + 128], [arenaB.ap[:, KS_OFF:KS_OFF + 128].ap[0], (16, KC), (1, 16)])
    q3s_all = mk(arenaB.ap[:, QS_OFF:QS_OFF + 384], [arenaB.ap[:, QS_OFF:QS_OFF + 384].ap[0], (48, KC), (16, 3), (1, 16)])
    all_tiles5 = [(0, 512), (512, 512), (1024, 512), (1536, 512), (2048, 16)]

    def rope_tiles():
        cosT = arenaF.take([128, NT])
        sinT = arenaF.take([128, NT])
        return cosT, sinT

    def rope_a(xp, n, xb):
        P.add("act", lambda e: e.activation(out=xb, in_=xp, func=AF.Identity), reads=[xp], writes=[xb])

    def rope_b(xp, lo, n, cosT, sinT, t1, t2, outs, xb, xs_ps):
        P.add("pe", lambda e: e.matmul(xs_ps, lhsT=pswap_b, rhs=xb, start=True, stop=True), reads=[pswap_b, xb], writes=[xs_ps])
        c_ = cosT[:, lo:lo + n]
        s_ = sinT[:, lo:lo + n]
        P.add("dve", lambda e: e.tensor_tensor(out=t1, in0=xp, in1=c_, op=ALU.mult), reads=[xp, c_], writes=[t1])
        P.add("dve", lambda e: e.tensor_tensor(out=t2, in0=xs_ps, in1=s_, op=ALU.mult), reads=[xs_ps, s_], writes=[t2])
        for o_ in outs:
            if isinstance(o_, tuple):
                o_, d_ = o_
                a1 = mk(t1, [t1.ap[0], (1, d_), (d_, n // d_)])
                a2 = mk(t2, [t2.ap[0], (1, d_), (d_, n // d_)])
                P.add("dve", lambda e, o_=o_, a1=a1, a2=a2: e.tensor_tensor(out=o_, in0=a1, in1=a2, op=ALU.add), reads=[t1, t2], writes=[o_])
            else:
                P.add("dve", lambda e, o_=o_: e.tensor_tensor(out=o_, in0=t1, in1=t2, op=ALU.add), reads=[t1, t2], writes=[o_])

    def kv_phase():
        nonlocal arenaF_tmp16
        arenaB.reset()
        arenaF.reset()
        hkv = arenaB.take([128, KC, NT])
        wkv = arenaB.take([128, KC, 2048])
        kst = [arenaB.take([128, 512]) for _ in range(2)]
        vst = [arenaB.take([128, 16, 65]) for _ in range(2)]
        ada1 = ada_gen(1, [arenaB.take([128, KC, 384]) for _ in range(2)], 3, 6)
        cosT, sinT = rope_tiles()
        arenaF_tmp16 = arenaF.take([128, KC, 16])
        scr = arenaF.take([128, 3200])
        t1 = scr[:, 0:512]
        t2 = scr[:, 512:1024]
        kout = [scr[:, 1024:1536], scr[:, 1536:2048]]
        vout = [scr[:, 1024:2048], scr[:, 2048:3072]]
        P.add("sp", lambda e: e.dma_start(out=cosT, in_=ropet[0]), reads=[ropet[0]], writes=[cosT], dma=True)
        P.add("sp", lambda e: e.dma_start(out=sinT, in_=ropet[1]), reads=[ropet[1]], writes=[sinT], dma=True)
        for q_ in range(4):
            wd_, ws_ = wkv[:, :, q_ * 512:(q_ + 1) * 512], w_kv_d[:, :, q_ * 512:(q_ + 1) * 512]
            P.add("pool", lambda e, wd_=wd_, ws_=ws_: e.dma_start(out=wd_, in_=ws_), reads=[ws_], writes=[wd_], dma=True)
        for v_ in vst:
            P.add("dve", lambda e, v_=v_: e.memset(v_, 1.0), writes=[v_])
        vt_tiles = [(i * 128, 128) for i in range(16)] + [(2048, 16)]

        def v_tile(vi):
            lo, nt = vt_tiles[vi]
            vp = mk(psF[0:nt, 2, :], [psF[0:nt, 2, :].ap[0], (1, 1024)])
            for hf in range(2):
                o_ = psF[0:nt, 2 + hf, :]
                for k in range(KC):
                    lt = hkv[:, k, lo:lo + nt]
                    rh = wkv[:, k, 1024 + hf * 512:1024 + (hf + 1) * 512]
                    P.add("pe", lambda e, o_=o_, lt=lt, rh=rh, k=k: e.matmul(o_, lhsT=lt, rhs=rh, start=(k == 0), stop=(k == KC - 1)), reads=[lt, rh], writes=[o_])
            vo = vout1[0:nt, :]
            vb = vst[vi % 2][0:nt, :, 0:64]
            vpv = mk(vp, [vp.ap[0], (64, 16), (1, 64)])
            pbx = [psF[0:nt, 2:4, :]]
            P.add("act", lambda e, vo=vo, vp=vp: e.activation(out=vo, in_=vp, func=AF.Identity), reads=pbx, writes=[vo])
            P.add("dve", lambda e, vb=vb, vpv=vpv: e.tensor_copy(out=vb, in_=vpv), reads=pbx, writes=[vb])
            dst = v_tm[lo:lo + nt, :]
            P.add("sp", lambda e, dst=dst, vo=vo: e.dma_start(out=dst, in_=vo), reads=[vo], writes=[dst], dma=True)
            vfull = mk(vst[vi % 2][0:nt], [vst[vi % 2][0:nt].ap[0], (1, 1040)])
            dst2 = vscr[lo:lo + nt, :]
            P.add("sp", lambda e, dst2=dst2, vfull=vfull: e.dma_start(out=dst2, in_=vfull), reads=[vfull], writes=[dst2], dma=True)


        vout1 = scr[:, 2048:3072]
        vdone = [0]
        cnt = 0
        xbs = [arenaB.take([128, 512]) for _ in range(2)]
        prev = None

        def finish(u):
            (kp, lo, n, c, cnt_) = u
            ko = kout[cnt_ % 2][:, 0:n]
            kb = kst[cnt_ % 2][:, 0:n] if lo < T else ks_all[:, c, :]
            rope_b(kp, lo, n, cosT, sinT, t1[:, 0:n], t2[:, 0:n], [ko], xbs[cnt_ % 2][:, 0:n], psF[:, 4 + cnt_ % 2, 0:n])
            P.add("act", lambda e: e.activation(out=kb, in_=ko, func=AF.Identity), reads=[ko], writes=[kb])
            dst = k_fm[:, c, lo:lo + n]
            P.add("sp", lambda e: e.dma_start(out=dst, in_=ko), reads=[ko], writes=[dst], dma=True)
            dst2 = kscr[c, :, lo:lo + n]
            P.add("sp", lambda e: e.dma_start(out=dst2, in_=kb), reads=[kb], writes=[dst2], dma=True)

        modulate(hkv, 0, all_tiles5[0:1], modkv, 0, 8)
        for t5, (lo, n) in enumerate(all_tiles5):
            if t5 + 1 < len(all_tiles5):
                modulate(hkv, 0, all_tiles5[t5 + 1:t5 + 2], modkv, 0, 8)
            for c in range(KC):
                kp = psF[:, cnt % 2, 0:n]
                for k in range(KC):
                    lt = wkv[:, k, c * 128:(c + 1) * 128]
                    rh = hkv[:, k, lo:lo + n]
                    P.add("pe", lambda e, kp=kp, lt=lt, rh=rh, k=k: e.matmul(kp, lhsT=lt, rhs=rh, start=(k == 0), stop=(k == KC - 1)), reads=[lt, rh], writes=[kp])
                rope_a(kp, n, xbs[cnt % 2][:, 0:n])
                if prev is not None:
                    finish(prev)
                prev = (kp, lo, n, c, cnt)
                cnt += 1
                next(ada1, None)
                if cnt % 2 == 0 and vdone[0] < len(vt_tiles) and vt_tiles[vdone[0]][0] < lo + n:
                    v_tile(vdone[0])
                    vdone[0] += 1
        finish(prev)
        for _ in ada1:
            pass
        while vdone[0] < len(vt_tiles):
            v_tile(vdone[0])
            vdone[0] += 1
    def attn_phase():
        nonlocal arenaF_tmp16, lnF
        l = 1
        mod_t = mods[:, 1]
        sh_oc, sc_oc, g_oc, li = 24, 32, 40, 4
        arenaB.reset()
        arenaF.reset()
        oT = arenaB.take([128, KC, NT])
        B0 = arenaB.off
        wq_ = arenaB.take([128, KC, 384])
        ht = arenaB.take([128, KC, 512])
        q3b = [arenaB.take([128, 3, T]) for _ in range(2)]
        kt_ = arenaB.take([128, T])
        vpm = [arenaB.take([128, 3, 16, 130]) for _ in range(2)]
        Pt = [arenaB.take([128, 256]) for _ in range(4)]
        xbq = [arenaB.take([128, 512]) for _ in range(2)]
        assert arenaB.off <= KS_OFF, arenaB.off
        cosT, sinT = rope_tiles()
        P.add("sp", lambda e: e.dma_start(out=cosT, in_=ropet[0]), reads=[ropet[0]], writes=[cosT], dma=True)
        P.add("sp", lambda e: e.dma_start(out=sinT, in_=ropet[1]), reads=[ropet[1]], writes=[sinT], dma=True)
        arenaF_tmp16 = arenaF.take([128, KC, 16])
        F0 = arenaF.off
        t1 = arenaF.take([128, 512])
        t2 = arenaF.take([128, 512])
        oacc = arenaF.take([65, 2048])
        rdn = arenaF.take([64, 256])
        DILS = (1, 4, 16)

        def gather_v(c, vb):
            base = vscr[0:T, c * 130:(c + 1) * 130]
            off0 = int(base.offset)
            s0 = bass.AP(vscr.tensor, off0, [[1040, 128], [128 * 1040, 16], [1, 130]])
            d0 = vb[:, 0, :, :]
            P.add("sp", lambda e: e.dma_start(out=d0, in_=s0), reads=[vscr[0:T, :]], writes=[d0], dma=True)
            for r in range(4):
                s1 = bass.AP(vscr.tensor, off0 + r * 1040, [[4 * 1040, 128], [512 * 1040, 4], [1, 130]])
                d1 = vb[:, 1, r * 4:(r + 1) * 4, :]
                P.add("sp", lambda e, s1=s1, d1=d1: e.dma_start(out=d1, in_=s1), reads=[vscr[0:T, :]], writes=[d1], dma=True)
            s2 = bass.AP(vscr.tensor, off0, [[16 * 1040, 128], [1040, 16], [1, 130]])
            d2 = vb[:, 2, :, :]
            P.add("sp", lambda e: e.dma_start(out=d2, in_=s2), reads=[vscr[0:T, :]], writes=[d2], dma=True)

        def blocks_of(g):
            d = DILS[g]
            nbk = 16 // d
            return [(r, i, nbk) for r in range(d) for i in range(nbk)]

        def cols(ap2d, g, r, i, nblk):
            d = DILS[g]
            st = i * 128 * d + r
            return mk(ap2d[:, st:st + 1], [ap2d[:, st:st + 1].ap[0], (d, 128 * nblk)])

        cnts = {"s": 0, "p": 0, "q": 0, "tick": 0, "inhead": 0, "mid": 0}
        pend = {"norm": None}

        def qproj_gen(c, q3):
            src = w_q_d[c]
            P.add("pool", lambda e, src=src: e.dma_start(out=wq_, in_=src), reads=[src], writes=[wq_], dma=True)
            for (lo, n) in all_tiles5:
                modulate(ht, lo, [(lo, n)], mod_t, sh_oc, sc_oc, eng="dve")
                for g in range(3):
                    qp = psF[:, 5, 0:n]
                    cnts["q"] += 1
                    for k in range(KC):
                        lt = wq_[:, k, g * 128:(g + 1) * 128]
                        rh = ht[:, k, 0:n]
                        P.add("pe", lambda e, qp=qp, lt=lt, rh=rh, k=k: e.matmul(qp, lhsT=lt, rhs=rh, start=(k == 0), stop=(k == KC - 1)), reads=[lt, rh], writes=[qp])
                    if lo >= T:
                        qo = q3s_all[:, c, g, :]
                    elif g == 0:
                        qo = q3[:, g, lo:lo + n]
                    else:
                        d_ = DILS[g]
                        b_ = q3[:, g, lo // d_:lo // d_ + 1]
                        qo = (mk(b_, [b_.ap[0], (T // d_, d_), (1, n // d_)]), d_)
                    xb = xbq[cnts["q"] % 2][:, 0:n]
                    rope_a(qp, n, xb)
                    cnts["mid"] = 1
                    yield 1
                    rope_b(qp, lo, n, cosT, sinT, t1[:, 0:n], t2[:, 0:n], [qo], xb, psF[:, 6, 0:n])
                    cnts["mid"] = 0
                    yield 1

        def attn_head(c, hh, vb, q3, tick):
            pb0 = hh * 64
            for g in range(3):
                d = DILS[g]
                nbk = 16 // d
                kq = q3[pb0:pb0 + 64, g, :]
                kk = kt_[pb0:pb0 + 64, :]
                blocks = [(r, i) for r in range(d) for i in range(nbk)]
                Pof = {}

                def emit_S(b):
                    r, i = blocks[b]
                    nq = 2 if i < nbk - 1 else 1
                    sp_ = psF[:, cnts["s"] % 3, 0:128 * nq]
                    Pb = Pt[cnts["s"] % 4][:, 0:128 * nq]
                    cnts["s"] += 1
                    ma = maskp_b[:, 0:128 * nq]
                    P.add("pe", lambda e: e.matmul(sp_, lhsT=ident_b, rhs=ma, start=True, stop=False), reads=[ident_b, ma], writes=[sp_])
                    lt = cols(kk, g, r, i, 1)
                    if g == 0:
                        rh = cols(kq, g, r, i, nq)
                    else:
                        st_ = r * (T // d) + i * 128
                        rh = kq[:, st_:st_ + 128 * nq]
                    P.add("pe", lambda e: e.matmul(sp_, lhsT=lt, rhs=rh, start=False, stop=True), reads=[kk, kq], writes=[sp_])
                    P.add("act", lambda e: e.activation(out=Pb, in_=sp_, func=AF.Exp, scale=0.125), reads=[sp_], writes=[Pb])
                    Pof[b] = Pb

                def emit_V(b, po):
                    r, i = blocks[b]
                    tix = (i if g == 0 else (r * 4 + i if g == 1 else r))
                    first = (i == 0)
                    if not first:
                        l0 = vb[:, g, tix - 1, hh * 65:(hh + 1) * 65]
                        r0 = Pof[b - 1][:, 128:256]
                        P.add("pe", lambda e: e.matmul(po, lhsT=l0, rhs=r0, start=True, stop=False), reads=[l0, r0], writes=[po])
                    l1 = vb[:, g, tix, hh * 65:(hh + 1) * 65]
                    r1 = Pof[b][:, 0:128]
                    P.add("pe", lambda e: e.matmul(po, lhsT=l1, rhs=r1, start=first, stop=True), reads=[l1, r1], writes=[po])

                emit_S(0)
                emit_S(1)
                for b0 in range(0, 16, 4):
                    po_bank = psF[0:65, 3 + cnts["p"] % 2, :]
                    cnts["p"] += 1
                    for j in range(4):
                        b = b0 + j
                        if b + 2 < 16:
                            emit_S(b + 2)
                        emit_V(b, po_bank[:, j * 128:(j + 1) * 128])
                        tick()
                    rb, ib = blocks[b0]
                    if g == 0:
                        dst = oacc[:, ib * 128:(ib + 4) * 128]
                        P.add("dve", lambda e, dst=dst, pbk=po_bank: e.tensor_copy(out=dst, in_=pbk), reads=[po_bank], writes=[dst])
                    elif g == 1:
                        dst = mk(oacc[:, rb:rb + 1], [oacc[:, rb:rb + 1].ap[0], (4, 512)])
                        P.add("dve", lambda e, dst=dst, pbk=po_bank: e.tensor_tensor(out=dst, in0=dst, in1=pbk, op=ALU.add), reads=[po_bank, oacc[:, :]], writes=[oacc[:, :]])
                    else:
                        dst = mk(oacc[:, rb:rb + 1], [oacc[:, rb:rb + 1].ap[0], (1, 4), (16, 128)])
                        src3 = mk(po_bank, [po_bank.ap[0], (128, 4), (1, 128)])
                        P.add("dve", lambda e, dst=dst, src3=src3: e.tensor_tensor(out=dst, in0=dst, in1=src3, op=ALU.add), reads=[po_bank, oacc[:, :]], writes=[oacc[:, :]])
            def norm():
                drow = oacc[64:65, :]
                P.add("act", lambda e: e.activation(out=drow, in_=drow, func=AF.Ln), reads=[drow], writes=[drow])
                P.add("act", lambda e: e.activation(out=drow, in_=drow, func=AF.Exp, scale=-1.0), reads=[drow], writes=[drow])
                for tix in range(4):
                    cs_ = slice(tix * 512, (tix + 1) * 512)
                    dps = psF[0:64, 5 + cnts["q"] % 2, :]
                    cnts["q"] += 1
                    rh = oacc[64:65, cs_]
                    lt = sel_f[64:65, :]
                    P.add("pe", lambda e, dps=dps, rh=rh, lt=lt: e.matmul(dps, lhsT=lt, rhs=rh, start=True, stop=True), reads=[lt, rh], writes=[dps])
                    num = oacc[0:64, cs_]
                    dst = oT[pb0:pb0 + 64, c, cs_]
                    P.add("dve", lambda e, dst=dst, num=num, dps=dps: e.tensor_tensor(out=dst, in0=num, in1=dps, op=ALU.mult), reads=[num, dps], writes=[dst])
            return norm

        for _ in qproj_gen(0, q3b[0]):
            pass
        for c in range(KC):
            srck = kscr[c][:, 0:T]
            P.add("sp", lambda e, srck=srck: e.dma_start(out=kt_, in_=srck), reads=[kscr[c]], writes=[kt_], dma=True)
            vb = vpm[c % 2]
            if c == 0:
                gather_v(0, vb)
            if c + 1 < KC:
                gather_v(c + 1, vpm[(c + 1) % 2])
            gen = qproj_gen(c + 1, q3b[(c + 1) % 2]) if c + 1 < KC else iter(())

            def tick():
                cnts["tick"] += 1
                cnts["inhead"] += 1
                if cnts["inhead"] == 3 and pend["norm"] is not None:
                    if cnts["mid"]:
                        next(gen, None)
                    pend["norm"]()
                    pend["norm"] = None
                if cnts["tick"] % TICKN == 0:
                    next(gen, None)
            for hh in range(2):
                cnts["inhead"] = 0
                nf = attn_head(c, hh, vb, q3b[c % 2], tick)
                assert pend["norm"] is None
                pend["norm"] = nf
            for _ in gen:
                pass
        assert not cnts["mid"]
        pend["norm"]()
        arenaB.reset(B0)
        kctx = arenaB.take([128, KC, 1024])
        vctx = arenaB.take([128, 8, 1040])
        stb = [arenaB.take([128, 1024]) for _ in range(4)]
        Pgs = [arenaB.take([128, 8, 4]) for _ in range(3)]
        wob = arenaB.take([128, KC, 1024])
        zbq = [arenaB.take([128, 512]) for _ in range(6)]
        assert arenaB.off <= KS_OFF
        arenaF.reset(0)
        stf = [arenaF.take([128, 1024]) for _ in range(4)]
        arenaF.reset(F0)
        Pfs = [arenaF.take([128, 96]) for _ in range(2)]
        oas = arenaF.take([65, 16, 16])
        rds = arenaF.take([64, 256])
        lnF = ln_bufs(1, False)
        lnF = {k_: v_ + v_ for k_, v_ in lnF.items()}
        P.add("pool", lambda e: e.dma_start(out=wob, in_=w_ob_d), reads=[w_ob_d], writes=[wob], dma=True)
        P.add("dve", lambda e: e.memset(kctx[:, :, 896:1024], 0.0), writes=[kctx[:, :, 896:1024]])
        P.add("dve", lambda e: e.memset(vctx[:, 7, :], 0.0), writes=[vctx[:, 7, :]])
        P.add("dve", lambda e: e.memset(vctx[:, 0:7, :], 1.0), writes=[vctx[:, 0:7, :]])
        zst = {"zc": 0}

        def out_proj(lo, n):
            st = ln_begin([(lo, n)])
            for c in range(KC):
                Y = psF[:, 2 + c % 2, 0:n]
                for k in range(KC):
                    lt = wob[:, k, c * 128:(c + 1) * 128]
                    rh = oT[:, k, lo:lo + n]
                    P.add("pe", lambda e, Y=Y, lt=lt, rh=rh, k=k: e.matmul(Y, lhsT=lt, rhs=rh, start=(k == 0), stop=(k == KC - 1)), reads=[lt, rh], writes=[Y])
                ln_flush(st, keep=0)
                residual_add(Y, c, lo, n, mod_t, g_oc)
                zc = zst["zc"]
                ln_accum(st, c, lo, n, zbq[(zc % 3) * 2], zbq[(zc % 3) * 2 + 1])
                zst["zc"] += 1
            ln_finish(st, li)

        sc = 0
        hcnt = 0
        for s_ in range(4):
            for tile in range(7):
                for (cache, isk) in ((cache_k, True), (cache_v, False)):
                    sf = stf[sc % 4]
                    sbb = stb[sc % 4]
                    sc += 1
                    if tile < 3:
                        for t_ in range(4):
                            src = bass.AP(cache.tensor, int(cache[s_].offset) + (16 * 32 * tile + t_) * 1024, [[16 * 1024, 32], [1, 1024]])
                            dstp = sf[t_ * 32:(t_ + 1) * 32, :]
                            P.add("sp", lambda e, src=src, dstp=dstp: e.dma_start(out=dstp, in_=src), reads=[cache[s_]], writes=[dstp], dma=True)
                    else:
                        r0_ = 1536 + (tile - 3) * 128
                        src = cache[s_, r0_:r0_ + 128, :]
                        P.add("sp", lambda e, src=src, sf=sf: e.dma_start(out=sf, in_=src), reads=[src], writes=[sf], dma=True)
                    if isk:
                        P.add("act", lambda e, sf=sf, sbb=sbb: e.activation(out=sbb, in_=sf, func=AF.Identity), reads=[sf], writes=[sbb])
                        for c in range(KC):
                            o_ = psB[:, c * 128:(c + 1) * 128]
                            i_ = sbb[:, c * 128:(c + 1) * 128]
                            P.add("pe", lambda e, o_=o_, i_=i_: e.transpose(o_, i_, ident_b), reads=[i_, ident_b], writes=[o_])
                        pbv = mk(psB[:, :], [psB[:, :].ap[0], (128, KC), (1, 128)])
                        dst = kctx[:, :, tile * 128:(tile + 1) * 128]
                        P.add("dve", lambda e, dst=dst, pbv=pbv: e.tensor_copy(out=dst, in_=pbv), reads=[psB[:, :]], writes=[dst])
                    else:
                        dst = mk(vctx[:, tile, 0:1], [vctx[:, tile, 0:1].ap[0], (65, 16), (1, 64)])
                        sfv = mk(sf, [sf.ap[0], (64, 16), (1, 64)])
                        P.add("dve", lambda e, dst=dst, sfv=sfv: e.tensor_copy(out=dst, in_=sfv), reads=[sf], writes=[vctx[:, tile, :]])
            ksn = ks_all[:, :, 4 * s_:4 * s_ + 4]
            kd = kctx[:, :, 896:900]
            P.add("act", lambda e, kd=kd, ksn=ksn: e.activation(out=kd, in_=ksn, func=AF.Identity), reads=[ksn], writes=[kd])
            vsrc = vscr[T + 4 * s_:T + 4 * s_ + 4, :]
            vd = vctx[0:4, 7, :]
            P.add("sp", lambda e, vd=vd, vsrc=vsrc: e.dma_start(out=vd, in_=vsrc), reads=[vsrc], writes=[vd], dma=True)

            def head_S(h):
                c, pb0 = h // 2, (h % 2) * 64
                sps = psF[:, 4 + hcnt_of[h] % 2, 0:96]
                Pf = Pfs[hcnt_of[h] % 2]
                pg = Pgs[hcnt_of[h] % 3]
                P.add("pe", lambda e: e.matmul(sps, lhsT=ident_b, rhs=masks_b, start=True, stop=False), reads=[ident_b, masks_b], writes=[sps])
                for tile in range(8):
                    for g in range(3):
                        o_ = sps[:, tile * 12 + g * 4:tile * 12 + g * 4 + 4]
                        lt = kctx[pb0:pb0 + 64, c, tile * 128:(tile + 1) * 128]
                        rh = q3s_all[pb0:pb0 + 64, c, g, 4 * s_:4 * s_ + 4]
                        last = (tile == 7 and g == 2)
                        P.add("pe", lambda e, o_=o_, lt=lt, rh=rh, last=last: e.matmul(o_, lhsT=lt, rhs=rh, start=False, stop=last), reads=[lt, rh], writes=[o_])
                P.add("act", lambda e: e.activation(out=Pf, in_=sps, func=AF.Exp, scale=0.125), reads=[sps], writes=[Pf])
                pfv = mk(Pf, [Pf.ap[0], (12, 8), (1, 4), (4, 3)])

                def _red(e):
                    with nc.allow_low_precision(reason="3-term fp32 sum rounded once to the bf16 matmul operand"):
                        return e.tensor_reduce(out=pg, in_=pfv, axis=AX.X, op=ALU.add)
                P.add("dve", _red, reads=[Pf], writes=[pg])
                return pg

            def head_V(h, pg):
                po = psF[0:65, 6, (h % 8) * 4:(h % 8) * 4 + 4]
                for tile in range(8):
                    lt = vctx[:, tile, h * 65:(h + 1) * 65]
                    rh = pg[:, tile, :]
                    P.add("pe", lambda e, lt=lt, rh=rh, tile=tile: e.matmul(po, lhsT=lt, rhs=rh, start=(tile == 0), stop=(tile == 7)), reads=[lt, rh], writes=[po])
                if h % 8 == 7:
                    srcv = mk(psF[0:65, 6, 0:32], [psF[0:65, 6, 0:32].ap[0], (4, 8), (1, 4)])
                    dst = oas[:, (h // 8) * 8:(h // 8) * 8 + 8, 4 * s_:4 * s_ + 4]
                    P.add("dve", lambda e: e.tensor_copy(out=dst, in_=srcv), reads=[psF[0:65, 6, 0:32]], writes=[dst])

            hcnt_of = {h: hcnt + h for h in range(16)}
            hcnt += 16
            pgs_ = {0: head_S(0)}
            for h in range(16):
                if h + 1 < 16:
                    pgs_[h + 1] = head_S(h + 1)
                head_V(h, pgs_[h])
            out_proj(*all_tiles5[s_])
        dps = psF[0:64, 6, 0:256]
        oasf = mk(oas, [oas.ap[0], (1, 256)])
        P.add("pe", lambda e: e.matmul(dps, lhsT=sel_f, rhs=oasf, start=True, stop=True), reads=[sel_f, oas], writes=[dps])
        P.add("dve", lambda e: e.reciprocal(out=rds, in_=dps), reads=[dps], writes=[rds])
        for hh in range(2):
            num = mk(oas[0:64, hh:hh + 1, :], [oas[0:64, hh:hh + 1, :].ap[0], (32, 8), (1, 16)])
            rdv = mk(rds[:, hh * 16:hh * 16 + 1], [rds[:, hh * 16:hh * 16 + 1].ap[0], (32, 8), (1, 16)])
            dst = oT[hh * 64:(hh + 1) * 64, :, T:NT]
            P.add("dve", lambda e, dst=dst, num=num, rdv=rdv: e.tensor_tensor(out=dst, in0=num, in1=rdv, op=ALU.mult), reads=[oas, rds], writes=[dst])
        out_proj(*all_tiles5[4])

    arenaF_tmp16 = None
    lnF = None

    stages = [lambda: ffn(0, 0, ada_bg=True), gla_phase, lambda: ffn(0, 1), kv_phase, lambda: ffn(1, 0), attn_phase, lambda: ffn(1, 1)]
    for si, fn_ in enumerate(stages):
        if si < upto:
            fn_()

    P.add("sp", lambda e: e.dma_start(out=y_fm, in_=x), reads=[x], writes=[y_fm], dma=True)

    P.finalize_and_emit(es)
    es.close()
    return nc


def _prep_inputs(inp):
    f = np.float32
    shared = {}
    wa = np.asarray(inp["w_ada"], f)
    shared["w_ada"] = np.ascontiguousarray(wa.reshape(2, KC, 128, 8, 1152).transpose(0, 3, 2, 1, 4))
    shared["b_ada"] = np.ascontiguousarray(np.asarray(inp["b_ada"], f).reshape(2, 72, 128).transpose(2, 0, 1))
    wk = np.asarray(inp["w_ada_kv"], f)
    shared["w_adakv"] = np.ascontiguousarray(wk.reshape(KC, 128, 2, 1024).transpose(2, 1, 0, 3))
    shared["b_adakv"] = np.ascontiguousarray(np.asarray(inp["b_ada_kv"], f).reshape(16, 128).T)
    g = np.asarray(inp["ln_g"], f).reshape(6, KC, 128)
    b = np.asarray(inp["ln_b"], f).reshape(6, KC, 128)
    shared["lnp"] = np.ascontiguousarray(np.stack([g, b], 0).transpose(3, 0, 1, 2))
    ups, dns = [], []
    for l in range(2):
        for nm in ("w_ffn1", "w_ffn2"):
            wu = np.asarray(inp[nm + "_up"][l], f)
            wu = wu.reshape(KC, 128, 2, NJ, 128).transpose(3, 1, 2, 0, 4)
            ups.append(wu)
            wd = np.asarray(inp[nm + "_down"][l], f)
            wd = wd.reshape(NJ, 128, KC, 128).transpose(2, 1, 0, 3)
            dns.append(wd)
    shared["w_up"] = np.ascontiguousarray(np.stack(ups, 0))
    shared["w_dn"] = np.ascontiguousarray(np.stack(dns, 0))
    wi = np.asarray(inp["w_in_a"][0], f).reshape(KC, 128, GLA_IN).transpose(1, 0, 2)
    shared["w_in_qk"] = np.ascontiguousarray(wi[:, :, 0:1024])
    shared["w_in_v"] = np.ascontiguousarray(wi[:, :, 1024:2048])
    shared["w_in_glr"] = np.ascontiguousarray(wi[:, :, 2048:2064])
    shared["w_in_r"] = np.ascontiguousarray(wi[:, :, 2064:3088])
    shared["w_outa"] = np.ascontiguousarray(np.asarray(inp["w_out_a"][0], f).reshape(KC, 128, D).transpose(1, 0, 2))
    shared["wg2"] = np.ascontiguousarray(np.asarray(inp["w_gate2_a"][0], f))
    shared["bgate"] = np.ascontiguousarray(np.asarray(inp["b_gate_a"][0], f).reshape(4, 128).T)
    shared["gonorm"] = np.ascontiguousarray(np.broadcast_to(np.asarray(inp["g_onorm_a"][0], f)[None, :], (128, 256)))
    shared["w_kv_d"] = np.ascontiguousarray(np.asarray(inp["w_kv"], f).reshape(KC, 128, 2048).transpose(1, 0, 2))
    wq = np.asarray(inp["w_q_b"][0], f).reshape(KC, 128, 3, KC, 128)
    shared["w_q_d"] = np.ascontiguousarray(wq.transpose(3, 1, 0, 2, 4).reshape(KC, 128, KC, 384))
    shared["w_ob_d"] = np.ascontiguousarray(np.asarray(inp["w_out_b"][0], f).reshape(KC, 128, D).transpose(1, 0, 2))
    cf = np.zeros((128, 720), f)
    cf[:, 0:128] = np.triu(np.ones((128, 128), f))
    cf[:, 128:640] = 1.0
    cf[:, 128:640:128] = 0.0
    cf[:, 640:656] = 1.0
    cf[:, 640:656:4] = 0.0
    cf[64, 656:720] = 1.0
    shared["cstf_d"] = cf
    cb = np.zeros((128, 736), f)
    cb[:, 0:128] = 1.0 / 1024.0
    cb[:, 128:256] = np.eye(128, dtype=f)
    ki = np.arange(128)[:, None]
    qi = np.arange(128)[None, :]
    cb[:, 256:384] = np.where(ki <= qi, 0.0, NEG)
    cb[:, 384:512] = np.where(ki >= qi, 0.0, NEG)
    for p_ in range(128):
        cb[p_, 608 + (p_ ^ 32)] = 1.0
    pidx = np.arange(128)
    for tile in range(8):
        if tile < 3:
            rho = 16 * (32 * tile + (pidx % 32)) + pidx // 32
        elif tile < 7:
            rho = 1536 + (tile - 3) * 128 + pidx
        else:
            rho = np.where(pidx < 4, 2048 + pidx, -10 ** 6)
        for g, dil in enumerate((1, 4, 16)):
            for t in range(4):
                dist = 2048 + t - rho
                ok = (dist >= 0) & (dist % dil == 0) & (dist // dil <= 128)
                cb[:, 512 + tile * 12 + g * 4 + t] = np.where(ok, 0.0, NEG)
    shared["cstb_d"] = cb
    pos = np.concatenate([np.arange(T), np.tile(16384 + np.arange(4), 4)]).astype(f)
    inv = np.power(np.float32(10000.0), -np.arange(32, dtype=f) / np.float32(32.0)).astype(f)
    ang = (pos[None, :] * inv[:, None]).astype(f).astype(np.float64)
    cosv = np.cos(ang).astype(f)
    sinv = np.sin(ang).astype(f)
    rt = np.zeros((2, 128, NT), f)
    for p in range(128):
        rt[0, p] = cosv[p % 32]
        rt[1, p] = -sinv[p % 32] if (p % 64) < 32 else sinv[p % 32]
    shared["ropet"] = rt
    per_core = []
    xp = np.asarray(inp["x_prompt"], f)
    xs = np.asarray(inp["x_sample"], f)
    cp = np.asarray(inp["c_prompt"], f)
    cs = np.asarray(inp["c_sample"], f)
    for c in range(8):
        xa = np.concatenate([xp[c], xs[4 * c:4 * c + 4].reshape(16, D)], 0)
        xin = np.ascontiguousarray(xa.T.reshape(KC, 128, NT).transpose(1, 0, 2))
        ca = np.concatenate([cp[c:c + 1], cs[4 * c:4 * c + 4]], 0)
        cin = np.ascontiguousarray(ca.T.reshape(KC, 128, 5).transpose(1, 0, 2))
        m = dict(shared)
        m["xin"] = xin
        m["cin"] = cin
        sg = np.asarray(inp["state_gla"], f)[0, 4 * c:4 * c + 4]
        m["state_in"] = np.ascontiguousarray(sg.transpose(0, 2, 1, 3))
        m["cache_k_d"] = np.asarray(inp["cache_k"], f)[4 * c:4 * c + 4].reshape(4, 2048, 1024)
        m["cache_v_d"] = np.asarray(inp["cache_v"], f)[4 * c:4 * c + 4].reshape(4, 2048, 1024)
        per_core.append(m)
    return per_core


_NC_CACHE = {}


def kernel(**inputs):
    if "nc" not in _NC_CACHE:
        _NC_CACHE["nc"] = build_program()
    nc = _NC_CACHE["nc"]
    in_maps = _prep_inputs(inputs)
    res = run_bass_kernel_spmd(nc, in_maps, core_ids=list(range(8)))
    outs = res.results
    y = np.stack([o["y_fm"] for o in outs], 0)
    y = y.transpose(0, 3, 2, 1).reshape(8, NT, D)
    y_prompt = np.ascontiguousarray(y[:, :T])
    y_sample = np.ascontiguousarray(y[:, T:].reshape(32, 4, D))
    stp = np.stack([o["st_p"] for o in outs], 0)
    state_p = np.ascontiguousarray(stp.transpose(0, 2, 1, 3))[None]
    sts = np.stack([o["st_s"] for o in outs], 0).reshape(32, 128, 4, 256)
    state_s = np.ascontiguousarray(sts.transpose(0, 2, 1, 3))[None]
    kf = np.stack([o["k_fm"] for o in outs], 0)
    kf = kf.transpose(0, 3, 2, 1).reshape(8, NT, 16, 64)
    k_p = np.ascontiguousarray(kf[:, :T])
    k_s = np.ascontiguousarray(kf[:, T:].reshape(32, 4, 16, 64))
    vt = np.stack([o["v_tm"] for o in outs], 0).reshape(8, NT, 16, 64)
    v_p = np.ascontiguousarray(vt[:, :T])
    v_s = np.ascontiguousarray(vt[:, T:].reshape(32, 4, 16, 64))
    return y_prompt, y_sample, state_p, state_s, k_p, v_p, k_s, v_s
```
